# Optimizing a Trainium2 kernel written in Bass

```python
import math
import jax, jax.numpy as jnp
from jax import lax
import numpy as np

D_MODEL = 2048
BATCH = 4
SEQ = 4096
DEPTH = 2

D_MIX = D_MODEL
GROUP_W = D_MIX // 4
HEAD_DIM = 128
N_HEADS = GROUP_W // HEAD_DIM
RMS_EPS = 1e-6
ROPE_THETA = 10000.0
Q_BLOCK = 128
NEG = -1e30

GDN_CONV = 4
GDN_CHUNK = 64
MLA_Q_RANK = 384
MLA_KV_RANK = 256
MLA_NOPE = 128
MLA_ROPE = 64
MLA_V = HEAD_DIM
LRU_CONV = 4
LRU_C = 8.0
NSA_KV_HEADS = 1
CMP_LEN = 32
CMP_STRIDE = 16
CMP_HIDDEN = 256
SEL_LEN = 64
SEL_TOPK = 16
WINDOW = 512
FORCE = 1e9

IN_SIZES = (
    3 * GROUP_W, N_HEADS, N_HEADS, GROUP_W,
    MLA_Q_RANK, MLA_KV_RANK, MLA_ROPE, GROUP_W,
    GROUP_W, GROUP_W,
    GROUP_W, 6 * NSA_KV_HEADS * HEAD_DIM, 3 * N_HEADS, GROUP_W,
)
D_IN = sum(IN_SIZES)

kernel_name = "hybrid_parallel_heads_gdn_mla_rglru_nsa"


def rmsnorm(x, g):
    xf = x.astype(jnp.float32)
    y = xf * lax.rsqrt(jnp.mean(xf * xf, axis=-1, keepdims=True) + RMS_EPS)
    return (y * g.astype(jnp.float32)).astype(x.dtype)


def l2norm(x):
    return x * lax.rsqrt(jnp.sum(x * x, axis=-1, keepdims=True) + RMS_EPS)


def rope(x, pos):
    d = x.shape[-1]
    inv_freq = ROPE_THETA ** (-jnp.arange(0, d, 2, dtype=jnp.float32) / d)
    ang = pos.astype(jnp.float32)[..., None, None] * inv_freq
    cos, sin = jnp.cos(ang), jnp.sin(ang)
    x1, x2 = jnp.split(x.astype(jnp.float32), 2, axis=-1)
    return jnp.concatenate([x1 * cos - x2 * sin, x2 * cos + x1 * sin], axis=-1).astype(x.dtype)


def causal_dwconv(x, w):
    k_w, c = w.shape
    return lax.conv_general_dilated(
        x, w.astype(x.dtype)[:, None, :], window_strides=(1,), padding=[(k_w - 1, 0)],
        dimension_numbers=("NWC", "WIO", "NWC"), feature_group_count=c)


def _linear_combine(left, right):
    a_l, b_l = left
    a_r, b_r = right
    return a_l * a_r, a_r * b_l + b_r


def chunk_gated_delta_rule(q, k, v, g, beta):
    B, S, H, D = q.shape
    C = GDN_CHUNK
    N = S // C

    def chunks(t):
        return jnp.moveaxis(t.reshape((B, N, C, H) + t.shape[3:]), 3, 1)

    q, k, v, beta = chunks(q), chunks(k), chunks(v), chunks(beta)
    gc = jnp.cumsum(chunks(g), axis=-1)
    causal = jnp.tril(jnp.ones((C, C), bool))
    strict = jnp.tril(jnp.ones((C, C), bool), -1)
    decay = jnp.exp(jnp.where(causal, gc[..., :, None] - gc[..., None, :], -jnp.inf))
    kb = k * beta[..., None]
    low = jnp.where(strict, jnp.einsum("bhnid,bhnjd->bhnij", kb, k) * decay, 0.0)
    rhs = jnp.concatenate([v * beta[..., None], kb * jnp.exp(gc)[..., None]], axis=-1)
    uw = lax.linalg.triangular_solve(low + jnp.eye(C, dtype=low.dtype), rhs,
                                     left_side=True, lower=True, unit_diagonal=True)
    u, w = jnp.split(uw, 2, axis=-1)
    qk = jnp.einsum("bhnid,bhnjd->bhnij", q, k) * decay
    q_dec = q * jnp.exp(gc)[..., None]
    k_end = k * jnp.exp(gc[..., -1:] - gc)[..., None]
    c_dec = jnp.exp(gc[..., -1])

    def step(state, xs):
        qk_n, qd_n, u_n, w_n, ke_n, cd_n = xs
        v_new = u_n - jnp.einsum("bhcd,bhde->bhce", w_n, state)
        o_n = jnp.einsum("bhcd,bhde->bhce", qd_n, state) + jnp.einsum("bhij,bhje->bhie", qk_n, v_new)
        state = state * cd_n[..., None, None] + jnp.einsum("bhcd,bhce->bhde", ke_n, v_new)
        return state, o_n

    xs = tuple(jnp.moveaxis(t, 2, 0) for t in (qk, q_dec, u, w, k_end, c_dec))
    _, o = lax.scan(step, jnp.zeros((B, H, D, D), jnp.float32), xs)
    return jnp.moveaxis(o, 0, 2).transpose(0, 2, 3, 1, 4).reshape(B, S, H, D)


def gated_deltanet(qkv, a_logit, b_logit, gate, conv_w, a_log, dt_bias, norm_g):
    B, S, _ = qkv.shape
    out_dtype = qkv.dtype
    h = jax.nn.silu(causal_dwconv(qkv, conv_w)).astype(jnp.float32)
    q, k, v = [t.reshape(B, S, N_HEADS, HEAD_DIM) for t in jnp.split(h, 3, axis=-1)]
    q = l2norm(q) * (HEAD_DIM ** -0.5)
    k = l2norm(k)
    beta = jax.nn.sigmoid(b_logit.astype(jnp.float32))
    g = -jnp.exp(a_log.astype(jnp.float32)) * jax.nn.softplus(
        a_logit.astype(jnp.float32) + dt_bias.astype(jnp.float32))
    o = chunk_gated_delta_rule(q, k, v, g, beta)
    o = rmsnorm(o, norm_g) * jax.nn.silu(gate.astype(jnp.float32).reshape(B, S, N_HEADS, HEAD_DIM))
    return o.reshape(B, S, GROUP_W).astype(out_dtype)


def causal_attention(q, k, v, scale):
    B, S, H, Dq = q.shape
    Dv = v.shape[-1]
    nb = S // Q_BLOCK
    q_blk = jnp.moveaxis(q.reshape(B, nb, Q_BLOCK, H, Dq), 1, 0)
    key_pos = jnp.arange(S)

    def block(args):
        i, q_i = args
        t = i * Q_BLOCK + jnp.arange(Q_BLOCK)
        s = jnp.einsum("bqhd,bkhd->bhqk", q_i, k, preferred_element_type=jnp.float32) * scale
        s = jnp.where(key_pos[None, :] <= t[:, None], s, NEG)
        p = jax.nn.softmax(s, axis=-1)
        return jnp.einsum("bhqk,bkhd->bqhd", p.astype(v.dtype), v)

    o = lax.map(block, (jnp.arange(nb), q_blk))
    return jnp.moveaxis(o, 0, 1).reshape(B, S, H, Dv)


def mla(c_q, c_kv, k_pe, gate, pos, q_norm_g, w_uq, kv_norm_g, w_ukv, qk_norm_g):
    B, S, _ = c_q.shape
    q = jnp.einsum("bsr,rhd->bshd", rmsnorm(c_q, q_norm_g), w_uq)
    kv = jnp.einsum("bsr,rhd->bshd", rmsnorm(c_kv, kv_norm_g), w_ukv)
    qg, kg = qk_norm_g[0], qk_norm_g[1]
    q_nope = rmsnorm(q[..., :MLA_NOPE], qg[:MLA_NOPE])
    q_pe = rope(rmsnorm(q[..., MLA_NOPE:], qg[MLA_NOPE:]), pos)
    k_nope = rmsnorm(kv[..., :MLA_NOPE], kg[:MLA_NOPE])
    v = kv[..., MLA_NOPE:]
    k_pe = rope(rmsnorm(k_pe, kg[MLA_NOPE:])[:, :, None, :], pos)
    q = jnp.concatenate([q_nope, q_pe], axis=-1)
    k = jnp.concatenate([k_nope, jnp.broadcast_to(k_pe, (B, S, N_HEADS, MLA_ROPE))], axis=-1)
    o = causal_attention(q, k, v, (MLA_NOPE + MLA_ROPE) ** -0.5)
    o = o.astype(jnp.float32) * jax.nn.silu(gate.astype(jnp.float32).reshape(B, S, N_HEADS, MLA_V))
    return o.reshape(B, S, GROUP_W).astype(c_q.dtype)


def rglru(xb, gate, conv_w, conv_b, w_r, b_r, w_i, b_i, lam):
    B, S, W = xb.shape
    xc = (causal_dwconv(xb, conv_w) + conv_b).astype(jnp.float32)
    xh = xc.reshape(B, S, N_HEADS, W // N_HEADS)
    r = jax.nn.sigmoid(jnp.einsum("bshi,hij->bshj", xh, w_r.astype(jnp.float32)).reshape(B, S, W)
                       + b_r.astype(jnp.float32))
    i_g = jax.nn.sigmoid(jnp.einsum("bshi,hij->bshj", xh, w_i.astype(jnp.float32)).reshape(B, S, W)
                         + b_i.astype(jnp.float32))
    log_a = -LRU_C * r * jax.nn.softplus(-lam.astype(jnp.float32))
    a = jnp.exp(log_a)
    b = jnp.sqrt(-jnp.expm1(2.0 * log_a)) * (i_g * xc)
    _, h = lax.associative_scan(_linear_combine, (a, b), axis=1)
    return (h * jax.nn.silu(gate.astype(jnp.float32))).astype(xb.dtype)


def nsa(q, kv, gate_logits, gate, pos, q_norm_g, k_norm_g, cmp_pe, cmp_w1, cmp_b1, cmp_w2, cmp_b2):
    B, S, _ = q.shape
    H, D, G = N_HEADS, HEAD_DIM, NSA_KV_HEADS
    R = H // G
    scale = D ** -0.5
    q = rope(rmsnorm(q.reshape(B, S, H, D), q_norm_g), pos)
    k_c_raw, v_c_raw, k_s, v_s, k_w, v_w = [t.reshape(B, S, G, D) for t in jnp.split(kv, 6, axis=-1)]

    n_cmp = (S - CMP_LEN) // CMP_STRIDE + 1
    cmp_idx = jnp.arange(n_cmp)[:, None] * CMP_STRIDE + jnp.arange(CMP_LEN)[None, :]
    cmp_end = cmp_idx[:, -1]

    def compress(t, j):
        blk = t[:, cmp_idx] + cmp_pe[j][None, None, :, None, :]
        flat = blk.transpose(0, 1, 3, 2, 4).reshape(B, n_cmp, G, CMP_LEN * D)
        hid = jax.nn.silu(flat @ cmp_w1[j] + cmp_b1[j])
        return hid @ cmp_w2[j] + cmp_b2[j]

    k_c = rope(rmsnorm(compress(k_c_raw, 0), k_norm_g[0]), pos[:, cmp_end])
    v_c = compress(v_c_raw, 1)
    k_s = rope(rmsnorm(k_s, k_norm_g[1]), pos)
    k_w = rope(rmsnorm(k_w, k_norm_g[2]), pos)

    n_sel = S // SEL_LEN
    top_k = min(SEL_TOPK, n_sel)
    sel_start = jnp.arange(n_sel) * SEL_LEN
    blk_ids = jnp.arange(n_sel)
    overlap = ((cmp_idx[:, :1] < sel_start[None, :] + SEL_LEN)
               & (cmp_end[:, None] >= sel_start[None, :])).astype(jnp.float32)
    k_sb = k_s.reshape(B, n_sel, SEL_LEN, G, D).transpose(0, 3, 1, 2, 4)
    v_sb = v_s.reshape(B, n_sel, SEL_LEN, G, D).transpose(0, 3, 1, 2, 4)
    k_wp = jnp.pad(k_w, ((0, 0), (WINDOW, 0), (0, 0), (0, 0)))
    v_wp = jnp.pad(v_w, ((0, 0), (WINDOW, 0), (0, 0), (0, 0)))
    gates = jax.nn.sigmoid(gate_logits.astype(jnp.float32)).reshape(B, S, H, 3)

    nb = S // Q_BLOCK
    q_blk = jnp.moveaxis(q.reshape(B, nb, Q_BLOCK, G, R, D), 1, 0)
    g_blk = jnp.moveaxis(gates.reshape(B, nb, Q_BLOCK, H, 3), 1, 0)
    b_ix = jnp.arange(B)[:, None, None, None]
    g_ix = jnp.arange(G)[None, :, None, None]

    def block(args):
        i, q_i, g_i = args
        t = i * Q_BLOCK + jnp.arange(Q_BLOCK)
        s_c = jnp.einsum("bqgrd,bcgd->bgrqc", q_i, k_c, preferred_element_type=jnp.float32) * scale
        valid_c = cmp_end[None, :] <= t[:, None]
        p_c = jax.nn.softmax(jnp.where(valid_c, s_c, NEG), axis=-1)
        p_c = jnp.where(jnp.any(valid_c, axis=-1)[:, None], p_c, 0.0)
        o_c = jnp.einsum("bgrqc,bcgd->bqgrd", p_c.astype(v_c.dtype), v_c)
        imp = jnp.einsum("bgrqc,cn->bgqn", p_c, overlap)
        cur = t // SEL_LEN
        valid_s = sel_start[None, :] <= t[:, None]
        forced = (blk_ids[None, :] == 0) | (blk_ids[None, :] == cur[:, None]) | (blk_ids[None, :] == cur[:, None] - 1)
        imp = jnp.where(valid_s, imp, -1.0)
        imp = jnp.where(forced & valid_s, FORCE, imp)
        _, sel = lax.top_k(imp, top_k)
        k_g = k_sb[b_ix, g_ix, sel]
        v_g = v_sb[b_ix, g_ix, sel]
        tok = sel[..., None] * SEL_LEN + jnp.arange(SEL_LEN)
        valid_t = (tok <= t[:, None, None]) & (sel_start[sel] <= t[:, None])[..., None]
        s_s = jnp.einsum("bqgrd,bgqkld->bgrqkl", q_i, k_g, preferred_element_type=jnp.float32) * scale
        s_s = jnp.where(valid_t[:, :, None], s_s, NEG).reshape(B, G, R, Q_BLOCK, top_k * SEL_LEN)
        p_s = jax.nn.softmax(s_s, axis=-1).reshape(B, G, R, Q_BLOCK, top_k, SEL_LEN)
        o_s = jnp.einsum("bgrqkl,bgqkld->bqgrd", p_s.astype(v_g.dtype), v_g)
        k_i = lax.dynamic_slice_in_dim(k_wp, i * Q_BLOCK, WINDOW + Q_BLOCK, axis=1)
        v_i = lax.dynamic_slice_in_dim(v_wp, i * Q_BLOCK, WINDOW + Q_BLOCK, axis=1)
        kpos = i * Q_BLOCK - WINDOW + jnp.arange(WINDOW + Q_BLOCK)
        valid_w = ((kpos[None, :] <= t[:, None]) & (kpos[None, :] > t[:, None] - WINDOW)
                   & (kpos[None, :] >= 0))
        s_w = jnp.einsum("bqgrd,bkgd->bgrqk", q_i, k_i, preferred_element_type=jnp.float32) * scale
        p_w = jax.nn.softmax(jnp.where(valid_w, s_w, NEG), axis=-1)
        o_w = jnp.einsum("bgrqk,bkgd->bqgrd", p_w.astype(v_i.dtype), v_i)
        shape = (B, Q_BLOCK, H, D)
        return (g_i[..., 0:1] * o_c.reshape(shape) + g_i[..., 1:2] * o_s.reshape(shape)
                + g_i[..., 2:3] * o_w.reshape(shape))

    o = lax.map(block, (jnp.arange(nb), q_blk, g_blk))
    o = jnp.moveaxis(o, 0, 1).reshape(B, S, GROUP_W)
    return (o * jax.nn.silu(gate.astype(jnp.float32))).astype(q.dtype)


def setup_inputs(seed: int = 0) -> dict:
    key = jax.random.key(seed)
    keys = iter(jax.random.split(key, 40))
    f32 = jnp.float32
    L = DEPTH

    def nrm(shape, scale):
        return jax.random.normal(next(keys), shape, f32) * scale

    def gain(shape):
        return 1.0 + nrm(shape, 0.02)

    def unif(shape, lo, hi):
        return jax.random.uniform(next(keys), shape, f32, lo, hi)

    x = jax.random.normal(next(keys), (BATCH, SEQ, D_MODEL), f32)
    positions = jnp.broadcast_to(jnp.arange(SEQ, dtype=jnp.int32), (BATCH, SEQ))
    dt = jnp.exp(unif((L, N_HEADS), math.log(1e-3), math.log(1e-1)))
    a_lru = unif((L, GROUP_W), 0.9, 0.999) ** (1.0 / LRU_C)
    return {
        "x": x,
        "positions": positions,
        "norm_g": gain((L, D_MODEL)),
        "w_in": nrm((L, D_MODEL, D_IN), D_MODEL ** -0.5),
        "w_out": nrm((L, D_MIX, D_MODEL), D_MIX ** -0.5),
        "gdn_conv_w": nrm((L, GDN_CONV, 3 * GROUP_W), GDN_CONV ** -0.5),
        "gdn_a_log": jnp.log(unif((L, N_HEADS), 1.0, 16.0)),
        "gdn_dt_bias": dt + jnp.log(-jnp.expm1(-dt)),
        "gdn_norm_g": gain((L, HEAD_DIM)),
        "mla_q_norm_g": gain((L, MLA_Q_RANK)),
        "mla_w_uq": nrm((L, MLA_Q_RANK, N_HEADS, MLA_NOPE + MLA_ROPE), MLA_Q_RANK ** -0.5),
        "mla_kv_norm_g": gain((L, MLA_KV_RANK)),
        "mla_w_ukv": nrm((L, MLA_KV_RANK, N_HEADS, MLA_NOPE + MLA_V), MLA_KV_RANK ** -0.5),
        "mla_qk_norm_g": gain((L, 2, MLA_NOPE + MLA_ROPE)),
        "lru_conv_w": nrm((L, LRU_CONV, GROUP_W), LRU_CONV ** -0.5),
        "lru_conv_b": nrm((L, GROUP_W), 0.02),
        "lru_w_r": nrm((L, N_HEADS, HEAD_DIM, HEAD_DIM), HEAD_DIM ** -0.5),
        "lru_b_r": nrm((L, GROUP_W), 0.02),
        "lru_w_i": nrm((L, N_HEADS, HEAD_DIM, HEAD_DIM), HEAD_DIM ** -0.5),
        "lru_b_i": nrm((L, GROUP_W), 0.02),
        "lru_lambda": jnp.log(a_lru) - jnp.log1p(-a_lru),
        "nsa_q_norm_g": gain((L, HEAD_DIM)),
        "nsa_k_norm_g": gain((L, 3, HEAD_DIM)),
        "nsa_cmp_pe": nrm((L, 2, CMP_LEN, HEAD_DIM), 0.02),
        "nsa_cmp_w1": nrm((L, 2, CMP_LEN * HEAD_DIM, CMP_HIDDEN), (CMP_LEN * HEAD_DIM) ** -0.5),
        "nsa_cmp_b1": nrm((L, 2, CMP_HIDDEN), 0.02),
        "nsa_cmp_w2": nrm((L, 2, CMP_HIDDEN, HEAD_DIM), CMP_HIDDEN ** -0.5),
        "nsa_cmp_b2": nrm((L, 2, HEAD_DIM), 0.02),
    }


def reference(x, positions, norm_g, w_in, w_out, gdn_conv_w, gdn_a_log, gdn_dt_bias, gdn_norm_g,
              mla_q_norm_g, mla_w_uq, mla_kv_norm_g, mla_w_ukv, mla_qk_norm_g,
              lru_conv_w, lru_conv_b, lru_w_r, lru_b_r, lru_w_i, lru_b_i, lru_lambda,
              nsa_q_norm_g, nsa_k_norm_g, nsa_cmp_pe, nsa_cmp_w1, nsa_cmp_b1, nsa_cmp_w2, nsa_cmp_b2):
    splits = np.cumsum(IN_SIZES)[:-1].tolist()
    for l in range(DEPTH):
        h = rmsnorm(x, norm_g[l])
        proj = jnp.einsum("bsd,de->bse", h, w_in[l])
        (a_qkv, a_decay, a_beta, a_gate,
         b_cq, b_ckv, b_kpe, b_gate,
         c_x, c_gate,
         d_q, d_kv, d_gl, d_gate) = jnp.split(proj, splits, axis=-1)
        y_a = gated_deltanet(a_qkv, a_decay, a_beta, a_gate, gdn_conv_w[l], gdn_a_log[l],
                             gdn_dt_bias[l], gdn_norm_g[l])
        y_b = mla(b_cq, b_ckv, b_kpe, b_gate, positions, mla_q_norm_g[l], mla_w_uq[l],
                  mla_kv_norm_g[l], mla_w_ukv[l], mla_qk_norm_g[l])
        y_c = rglru(c_x, c_gate, lru_conv_w[l], lru_conv_b[l], lru_w_r[l], lru_b_r[l],
                    lru_w_i[l], lru_b_i[l], lru_lambda[l])
        y_d = nsa(d_q, d_kv, d_gl, d_gate, positions, nsa_q_norm_g[l], nsa_k_norm_g[l],
                  nsa_cmp_pe[l], nsa_cmp_w1[l], nsa_cmp_b1[l], nsa_cmp_w2[l], nsa_cmp_b2[l])
        y = jnp.concatenate([y_a, y_b, y_c, y_d], axis=-1)
        x = x + jnp.einsum("bse,ed->bsd", y, w_out[l])
    return x
```

```python
import contextlib
import numpy as np
import concourse.bass as bass
import concourse.mybir as mybir

F32 = mybir.dt.float32
BF16 = mybir.dt.bfloat16
I32 = mybir.dt.int32
AF = mybir.ActivationFunctionType
ALU = mybir.AluOpType
AX = mybir.AxisListType

SAME_ENG_SYNC = True


class Buf:
    __slots__ = ("name", "w", "r", "excl")

    def __init__(self, name="", excl=False):
        self.name = name
        self.w = None
        self.r = {}
        self.excl = excl


class DSem:
    def __init__(self, prog, name):
        self.prog = prog
        self.name = name
        self.sem = prog.stack.enter_context(prog.nc.semaphore(name))
        self.batch_end = []


class Prog:
    ENGS = ("pe", "act", "dve", "pool", "sp")

    def __init__(self, nc):
        self.nc = nc
        self.stack = contextlib.ExitStack()
        self.ops = {e: [] for e in self.ENGS}
        self.waited = {e: {} for e in self.ENGS}
        self.esem = {}
        for e in self.ENGS:
            self.esem[e] = self.stack.enter_context(nc.semaphore("es_" + e))
        self.sb_off = 0
        self.sb_peak = 0
        self.big = None
        self.bigps = None
        self.sb_marks = []
        self.n_alloc = 0
        self.dsems = []
        self.dsem_cache = {}

    def sb(self, shape, dtype, name=None):
        if self.big is None:
            self.SB_WORDS = 52000
            self.big = self.nc.alloc_sbuf_tensor("bigsb", [128, self.SB_WORDS], F32).ap()
        free = int(np.prod(shape[1:]))
        nbytes = free * mybir.dt.size(dtype)
        nw = (nbytes + 63) // 64 * 16
        off = self.sb_off
        self.sb_off += nw
        self.sb_peak = max(self.sb_peak, self.sb_off)
        assert self.sb_off <= self.SB_WORDS, ("SBUF overflow", self.sb_off * 4, name)
        v = self.big[0:shape[0], off:off + nw]
        if dtype != F32:
            v = v.bitcast(dtype)
        v = v[:, 0:free]
        if len(shape) == 3:
            v = v.rearrange("p (a b) -> p a b", a=shape[1])
        elif len(shape) == 4:
            v = v.rearrange("p (a b c) -> p a b c", a=shape[1], b=shape[2])
        return v

    def psum(self, bank, dtype=F32, nbanks=1):
        if self.bigps is None:
            self.bigps = self.nc.alloc_psum_tensor("bigps", [128, 4096], F32).ap()
        v = self.bigps[:, bank * 512:(bank + nbanks) * 512]
        if dtype != F32:
            v = v.bitcast(dtype)
        return v

    def mark(self):
        self.sb_marks.append(self.sb_off)

    def release(self):
        self.sb_off = self.sb_marks.pop()
        self.barrier()

    def barrier(self):
        toks = []
        for e in self.ENGS:
            for i in range(len(self.ops[e]) - 1, -1, -1):
                o = self.ops[e][i]
                if o["fn"] is not None and o["dma"] is None:
                    toks.append(("eng", e, i))
                    break
        for ds in self.dsems:
            if ds.batch_end:
                toks.append(("dma", ds, len(ds.batch_end) - 1))
        for e in self.ENGS:
            waits = []
            for t in toks:
                w = self._need(e, t, False)
                if w:
                    waits.append(w)
            self.ops[e].append(dict(fn=None, waits=waits, needed=False, dma=None))

    def dsem(self, name):
        if name in self.dsem_cache:
            return self.dsem_cache[name]
        d = DSem(self, "ds_%s_%d" % (name, len(self.dsems)))
        self.dsems.append(d)
        self.dsem_cache[name] = d
        return d

    def _need(self, eng, tok, same_ok):
        if tok is None:
            return None
        if tok[0] == "eng":
            _, e2, idx = tok
            if e2 == eng and not same_ok:
                return None
            key = ("eng", e2)
            if self.waited[eng].get(key, -1) >= idx:
                return None
            self.waited[eng][key] = idx
            self.ops[e2][idx]["needed"] = True
            return tok
        else:
            _, ds, b = tok
            key = ("dma", id(ds))
            if self.waited[eng].get(key, -1) >= b:
                return None
            self.waited[eng][key] = b
            return tok

    def _deps(self, eng, mytok, reads, writes):
        waits = []
        for b in reads:
            w = self._need(eng, b.w, SAME_ENG_SYNC)
            if w:
                waits.append(w)
            if b.excl:
                for t in b.r.values():
                    w = self._need(eng, t, False)
                    if w:
                        waits.append(w)
        for b in writes:
            w = self._need(eng, b.w, SAME_ENG_SYNC)
            if w:
                waits.append(w)
            for t in b.r.values():
                w = self._need(eng, t, SAME_ENG_SYNC)
                if w:
                    waits.append(w)
        for b in reads:
            if mytok[0] == "eng":
                b.r[("eng", mytok[1])] = mytok
            else:
                b.r[("dma", id(mytok[1]))] = mytok
        for b in writes:
            b.w = mytok
            b.r = {}
        return waits

    def op(self, eng, fn, reads=(), writes=()):
        idx = len(self.ops[eng])
        tok = ("eng", eng, idx)
        waits = self._deps(eng, tok, reads, writes)
        self.ops[eng].append(dict(fn=fn, waits=waits, needed=False, dma=None))
        return tok

    def dma(self, eng, ds, out, in_, reads=(), writes=(), batch=False, **kw):
        idx = len(self.ops[eng])
        if getattr(ds, "eng", None) is None:
            ds.eng = eng
        assert ds.eng == eng, ("DSem shared across queues", ds.name, ds.eng, eng)
        if batch and ds.batch_end:
            ds.batch_end[-1] += 16
        else:
            prev = ds.batch_end[-1] if ds.batch_end else 0
            ds.batch_end.append(prev + 16)
        b = len(ds.batch_end) - 1
        tok = ("dma", ds, b)
        waits = []
        if b > 0:
            w = self._need(eng, ("dma", ds, b - 1), False)
            if w:
                waits.append(w)
        if batch:
            self.waited[eng][("dma", id(ds))] = max(self.waited[eng].get(("dma", id(ds)), -1), b)
        waits += self._deps(eng, tok, reads, writes)
        waits = [w for w in waits if not (w[0] == "dma" and w[1] is ds and w[2] == b)]
        if batch:
            self.waited[eng][("dma", id(ds))] = b - 1
        fn = lambda e, out=out, in_=in_, kw=kw: e.dma_start(out=out, in_=in_, **kw)
        self.ops[eng].append(dict(fn=fn, waits=waits, needed=False, dma=ds))
        return tok

    def wait_all(self, eng, bufs):
        waits = []
        for b in bufs:
            w = self._need(eng, b.w, True)
            if w:
                waits.append(w)
        self.ops[eng].append(dict(fn=None, waits=waits, needed=False, dma=None))

    def emit(self):
        nc = self.nc
        EPOCH = 2000
        mil = {}
        self.mil_counts = {}
        esems = {}
        for e in self.ENGS:
            c = 0
            for i, o in enumerate(self.ops[e]):
                if o["needed"]:
                    ep, v = divmod(c, EPOCH)
                    c += 1
                    if (e, ep) not in esems:
                        esems[(e, ep)] = self.stack.enter_context(nc.semaphore("es_%s_%d" % (e, ep)))
                    mil[(e, i)] = (esems[(e, ep)], v + 1)
            self.mil_counts[e] = c

        def run(eng_name, e):
            for i, o in enumerate(self.ops[eng_name]):
                for w in o["waits"]:
                    if w[0] == "eng":
                        sm, v = mil[(w[1], w[2])]
                        e.wait_ge(sm, v)
                    else:
                        e.wait_ge(w[1].sem, w[1].batch_end[w[2]])
                if o["fn"] is None:
                    continue
                ins = o["fn"](e)
                if o["dma"] is not None:
                    ins.then_inc(o["dma"].sem, 16)
                elif o["needed"]:
                    ins.then_inc(mil[(eng_name, i)][0], 1)

        with nc.Block() as block:
            @block.sync
            def _(e):
                run("sp", e)

            @block.tensor
            def _(e):
                run("pe", e)

            @block.scalar
            def _(e):
                run("act", e)

            @block.vector
            def _(e):
                run("dve", e)

            @block.gpsimd
            def _(e):
                run("pool", e)
        self.stack.close()


import ml_dtypes
from concourse.bass_utils import run_bass_kernel_spmd

S = 4096
D = 2048
NT = S // 128
DIN = 6100
L = 2
EPS = 1e-6


class Ctx:
    pass


def dram(k, name, shape, dtype):
    kind = "Internal"
    if name in k.dbg_in:
        kind = "ExternalInput"
    if name in k.dbg_out:
        kind = "ExternalOutput"
    return k.nc.dram_tensor(name, list(shape), dtype, kind=kind).ap()


def rr_engine(k, names=("act", "dve")):
    k.rr = getattr(k, "rr", 0) + 1
    return names[k.rr % len(names)]


def copy_op(p, eng, out, in_, reads, writes):
    if eng == "act":
        p.op("act", lambda e: e.activation(out=out, in_=in_, func=AF.Copy), reads=reads, writes=writes)
    else:
        p.op(eng, lambda e: e.tensor_copy(out=out, in_=in_), reads=reads, writes=writes)


def inproj_segments():
    segs = []
    for i in range(6):
        segs.append((i * 256, 256, "FM", "A_qkvT", i * 256))
    segs.append((1536, 8, "TM", "A_gb", 0))
    for i in range(2):
        segs.append((1544 + i * 256, 256, "FM", "A_gateT", i * 256))
    segs.append((2056, 256, "FM", "B_cT", 0))
    segs.append((2312, 256, "FM", "B_cT", 256))
    segs.append((2568, 192, "FM", "B_cT", 512))
    for i in range(2):
        segs.append((2760 + i * 256, 256, "TM", "B_gate", i * 256))
    for i in range(2):
        segs.append((3272 + i * 256, 256, "FM", "C_xT", i * 256))
    for i in range(2):
        segs.append((3784 + i * 256, 256, "FM", "C_gateT", i * 256))
    for i in range(2):
        segs.append((4296 + i * 256, 256, "FM", "D_qT", i * 256))
    segs.append((4808, 256, "FM", "D_kvT", 0))
    segs.append((5064, 128, "FM", "D_kvT", 256))
    segs.append((5192, 128, "TM", "D_v", 0))
    segs.append((5320, 128, "FM", "D_kvT", 384))
    segs.append((5448, 128, "TM", "D_v", 128))
    segs.append((5576, 12, "TM", "D_gl", 0))
    for i in range(2):
        segs.append((5588 + i * 256, 256, "TM", "D_gate", i * 256))
    return segs


def make_scratch(k):
    sc = {}
    sc["A_qkvT"] = dram(k, "A_qkvT", [1536, S], BF16)
    sc["A_gb"] = dram(k, "A_gb", [S, 8], F32)
    sc["A_gateT"] = dram(k, "A_gateT", [512, S], BF16)
    sc["B_cT"] = dram(k, "B_cT", [704, S], BF16)
    sc["B_gate"] = dram(k, "B_gate", [S, 512], BF16)
    sc["C_xT"] = dram(k, "C_xT", [512, S], BF16)
    sc["C_gateT"] = dram(k, "C_gateT", [512, S], BF16)
    sc["D_qT"] = dram(k, "D_qT", [512, S], F32)
    sc["D_kvT"] = dram(k, "D_kvT", [512, S], BF16)
    sc["D_v"] = dram(k, "D_v", [S, 256], BF16)
    sc["D_gl"] = dram(k, "D_gl", [S, 12], F32)
    sc["D_gate"] = dram(k, "D_gate", [S, 512], BF16)
    sc["yT"] = dram(k, "yT", [2048, S], BF16)
    sc["xs"] = dram(k, "xs", [S, D], F32)
    sc["rope"] = dram(k, "rope", [4, 128, S], BF16)
    k.sc = sc
    k.scb = {n: Buf(n) for n in sc}


def stage_inproj(k, l, x_src, bx_src):
    p = k.p
    inp = k.inp
    p.mark()
    TG = 2048
    gcol = p.sb([128, 16], F32, "gcol")
    b_g = Buf("gcol")
    p.dma("pool", k.ds_miscp, gcol, inp["norm_g"][l].rearrange("(kc q) -> q kc", q=128), writes=[b_g],
          allow_slow_non_contiguous=True)
    hT = p.sb([128, 16, TG], BF16, "hT")
    b_hT = Buf("hT")
    xt = [p.sb([128, D], F32, "xt%d" % i) for i in range(2)]
    b_xt = [Buf("xt%d" % i) for i in range(2)]
    ds_x = [p.dsem("x%d" % i) for i in range(2)]
    hb = [p.sb([128, D], BF16, "hb%d" % i) for i in range(2)]
    b_hb = [Buf("hb%d" % i) for i in range(2)]
    st = [p.sb([128, 1], F32, "ss%d" % i) for i in range(6)]
    b_st = [Buf("st%d" % i) for i in range(2)]
    junk = p.sb([128, D], BF16, "junk")
    b_junk = Buf("junk")
    wst = [p.sb([128, 16, 256], F32, "wst%d" % i) for i in range(2)]
    b_wst = [Buf("wst%d" % i) for i in range(2)]
    ds_w = [p.dsem("w%d" % i) for i in range(2)]
    wb = [p.sb([128, 16, 256], BF16, "wb%d" % i) for i in range(2)]
    b_wb = [Buf("wb%d" % i) for i in range(2)]
    ost = [p.sb([128, TG], F32, "ost%d" % i) for i in range(2)]
    b_ost = [Buf("ost%d" % i) for i in range(2)]
    ds_o = [p.dsem("o%d" % i) for i in range(2)]
    b_ps = k.b_ps
    segs = inproj_segments()
    w_l = inp["w_in"][l].rearrange("(kc q) c -> q kc c", q=128)
    ocount = 0
    pcount = 0
    for tg in range(S // TG):
        for ti in range(TG // 128):
            t0 = tg * TG + ti * 128
            i2 = ti % 2
            p.dma("sp", ds_x[i2], xt[i2], x_src[t0:t0 + 128, :], reads=[bx_src], writes=[b_xt[i2]])
            ss, sq, rs = st[i2 * 3], st[i2 * 3 + 1], st[i2 * 3 + 2]
            p.op("act", lambda e, i2=i2, ss=ss: e.activation(out=junk, in_=xt[i2], func=AF.Square, accum_out=ss),
                 reads=[b_xt[i2]], writes=[b_junk, b_st[i2]])
            p.op("act", lambda e, ss=ss, sq=sq: e.activation(out=sq, in_=ss, func=AF.Sqrt, scale=1.0 / D, bias=k.eps_col),
                 reads=[b_st[i2]], writes=[b_st[i2]])
            p.op("dve", lambda e, sq=sq, rs=rs: e.reciprocal(out=rs, in_=sq), reads=[b_st[i2]], writes=[b_st[i2]])
            p.op("dve", lambda e, i2=i2, rs=rs: e.tensor_scalar(out=hb[i2], in0=xt[i2], scalar1=rs, scalar2=None, op0=ALU.mult),
                 reads=[b_xt[i2], b_st[i2]], writes=[b_hb[i2]])
            pst = p.psum(6, BF16, 2)
            for kc in range(16):
                p.op("pe", lambda e, kc=kc, i2=i2, pst=pst: e.transpose(out=pst[:, kc * 128:(kc + 1) * 128], in_=hb[i2][:, kc * 128:(kc + 1) * 128], identity=k.ident),
                     reads=[b_hb[i2], k.b_const], writes=[b_ps[6]])
            p.op("act", lambda e, ti=ti, pst=pst: e.activation(out=hT[:, :, ti * 128:(ti + 1) * 128], in_=pst.rearrange("q (a b) -> q a b", a=16), func=AF.Copy),
                 reads=[b_ps[6]], writes=[b_hT])
        for si, (c0, n, mode, dest, doff) in enumerate(segs):
            wi = si % 2
            p.dma("sp", ds_w[wi], wst[wi][:, :, 0:n], w_l[:, :, c0:c0 + n], writes=[b_wst[wi]])
            for kc in range(16):
                eng = "pool" if kc % 2 == 0 else "dve"
                if eng == "pool":
                    p.op("pool", lambda e, wi=wi, kc=kc, n=n: e.tensor_scalar(out=wb[wi][:, kc, 0:n], in0=wst[wi][:, kc, 0:n], scalar1=gcol[:, kc:kc + 1], scalar2=1.0, op0=ALU.mult, op1=ALU.mult),
                         reads=[b_wst[wi], b_g], writes=[b_wb[wi]])
                else:
                    p.op("dve", lambda e, wi=wi, kc=kc, n=n: e.tensor_scalar(out=wb[wi][:, kc, 0:n], in0=wst[wi][:, kc, 0:n], scalar1=gcol[:, kc:kc + 1], scalar2=None, op0=ALU.mult),
                         reads=[b_wst[wi], b_g], writes=[b_wb[wi]])
            dst = k.sc[dest]
            bdst = k.scb[dest]
            ddt = dst.dtype
            if mode == "FM":
                for cb in range(0, n, 128):
                    nb = min(128, n - cb)
                    oi = ocount % 2
                    ocount += 1
                    ov = ost[oi] if ddt == F32 else ost[oi].bitcast(BF16)[:, 0:TG]
                    for tq in range(TG // 512):
                        bank = pcount % 4
                        pcount += 1
                        ps = p.psum(bank)
                        for kc in range(16):
                            p.op("pe", lambda e, ps=ps, wi=wi, kc=kc, cb=cb, nb=nb, tq=tq: e.matmul(ps[0:nb, :], lhsT=wb[wi][:, kc, cb:cb + nb], rhs=hT[:, kc, tq * 512:(tq + 1) * 512], start=(kc == 0), stop=(kc == 15)),
                                 reads=[b_wb[wi], b_hT], writes=[b_ps[bank]])
                        copy_op(p, rr_engine(k), ov[0:nb, tq * 512:(tq + 1) * 512], ps[0:nb, :], [b_ps[bank]], [b_ost[oi]])
                    p.dma("pool", ds_o[oi], dst[doff + cb:doff + cb + nb, tg * TG:(tg + 1) * TG], ov[0:nb, :], reads=[b_ost[oi]], writes=[bdst])
            else:
                oi = ocount % 2
                ocount += 1
                ovf = ost[oi] if ddt == F32 else ost[oi].bitcast(BF16)
                ov = ovf[:, 0:16 * n].rearrange("q (a b) -> q a b", a=16)
                for ti in range(TG // 128):
                    bank = pcount % 4
                    pcount += 1
                    ps = p.psum(bank)
                    for kc in range(16):
                        p.op("pe", lambda e, ps=ps, wi=wi, kc=kc, n=n, ti=ti: e.matmul(ps[:, 0:n], lhsT=hT[:, kc, ti * 128:(ti + 1) * 128], rhs=wb[wi][:, kc, 0:n], start=(kc == 0), stop=(kc == 15)),
                             reads=[b_wb[wi], b_hT], writes=[b_ps[bank]])
                    copy_op(p, rr_engine(k), ov[:, ti, :], ps[:, 0:n], [b_ps[bank]], [b_ost[oi]])
                p.dma("pool", ds_o[oi], dst[tg * TG:(tg + 1) * TG, doff:doff + n].rearrange("(a q) c -> q a c", q=128), ov, reads=[b_ost[oi]], writes=[bdst],
                      allow_slow_non_contiguous=(n < 64))
    p.release()


def stage_outproj(k, l, x_src, bx_src, x_dst, bx_dst):
    p = k.p
    inp = k.inp
    p.mark()
    wo = p.sb([128, 16, D], BF16, "wo")
    b_wo = Buf("wo")
    wst = [p.sb([128, 4, 512], F32, "wost%d" % i) for i in range(2)]
    b_wst = [Buf("wost%d" % i) for i in range(2)]
    ds_w = [p.dsem("wo%d" % i) for i in range(2)]
    w_l = inp["w_out"][l].rearrange("(kc q) c -> q kc c", q=128)
    cnt = 0
    for kc4 in range(4):
        for cq in range(4):
            wi = cnt % 2
            cnt += 1
            p.dma("sp", ds_w[wi], wst[wi], w_l[:, kc4 * 4:(kc4 + 1) * 4, cq * 512:(cq + 1) * 512], writes=[b_wst[wi]])
            copy_op(p, rr_engine(k, ("act", "dve", "pool")), wo[:, kc4 * 4:(kc4 + 1) * 4, cq * 512:(cq + 1) * 512], wst[wi], [b_wst[wi]], [b_wo])
    yt = [p.sb([128, 16, 512], BF16, "yt%d" % i) for i in range(2)]
    b_yt = [Buf("yt%d" % i) for i in range(2)]
    ds_y = [p.dsem("y%d" % i) for i in range(2)]
    xt = [p.sb([128, D], F32, "oxt%d" % i) for i in range(2)]
    b_xt = [Buf("oxt%d" % i) for i in range(2)]
    ds_x = [p.dsem("ox%d" % i) for i in range(2)]
    xo = [p.sb([128, D], F32, "oxo%d" % i) for i in range(2)]
    b_xo = [Buf("oxo%d" % i) for i in range(2)]
    ds_xo = [p.dsem("oxo%d" % i) for i in range(2)]
    yT = k.sc["yT"].rearrange("(ec q) t -> q ec t", q=128)
    b_ps = k.b_ps
    pcount = 0
    for tq in range(S // 512):
        yi = tq % 2
        p.dma("sp", ds_y[yi], yt[yi], yT[:, :, tq * 512:(tq + 1) * 512], reads=[k.scb["yT"]], writes=[b_yt[yi]])
        for tj in range(4):
            ti = tq * 4 + tj
            xi = ti % 2
            p.dma("sp", ds_x[xi], xt[xi], x_src[ti * 128:(ti + 1) * 128, :], reads=[bx_src], writes=[b_xt[xi]])
            for dq in range(4):
                bank = pcount % 4
                pcount += 1
                ps = p.psum(bank)
                for ec in range(16):
                    p.op("pe", lambda e, ps=ps, yi=yi, ec=ec, tj=tj, dq=dq: e.matmul(ps, lhsT=yt[yi][:, ec, tj * 128:(tj + 1) * 128], rhs=wo[:, ec, dq * 512:(dq + 1) * 512], start=(ec == 0), stop=(ec == 15)),
                         reads=[b_yt[yi], b_wo], writes=[b_ps[bank]])
                p.op("dve", lambda e, ps=ps, xi=xi, dq=dq: e.tensor_tensor(out=xo[xi][:, dq * 512:(dq + 1) * 512], in0=ps, in1=xt[xi][:, dq * 512:(dq + 1) * 512], op=ALU.add),
                     reads=[b_ps[bank], b_xt[xi]], writes=[b_xo[xi]])
            p.dma("pool", ds_xo[xi], x_dst[ti * 128:(ti + 1) * 128, :], xo[xi], reads=[b_xo[xi]], writes=[bx_dst])
    p.release()


INPUT_SPECS = [
    ("x", [S, D], F32), ("positions", [S], I32), ("norm_g", [L, D], F32), ("w_in", [L, D, DIN], F32),
    ("w_out", [L, D, D], F32), ("gdn_conv_w", [L, 4, 1536], F32), ("gdn_a_log", [L, 4], F32),
    ("gdn_dt_bias", [L, 4], F32), ("gdn_norm_g", [L, 128], F32), ("mla_q_norm_g", [L, 384], F32),
    ("mla_w_uq", [L, 384, 4, 192], F32), ("mla_kv_norm_g", [L, 256], F32), ("mla_w_ukv", [L, 256, 4, 256], F32),
    ("mla_qk_norm_g", [L, 2, 192], F32), ("lru_conv_w", [L, 4, 512], F32), ("lru_conv_b", [L, 512], F32),
    ("lru_w_r", [L, 4, 128, 128], F32), ("lru_b_r", [L, 512], F32), ("lru_w_i", [L, 4, 128, 128], F32),
    ("lru_b_i", [L, 512], F32), ("lru_lambda", [L, 512], F32), ("nsa_q_norm_g", [L, 128], F32),
    ("nsa_k_norm_g", [L, 3, 128], F32), ("nsa_cmp_pe", [L, 2, 32, 128], F32), ("nsa_cmp_w1", [L, 2, 4096, 256], F32),
    ("nsa_cmp_b1", [L, 2, 256], F32), ("nsa_cmp_w2", [L, 2, 256, 128], F32), ("nsa_cmp_b2", [L, 2, 128], F32),
]


def host_consts():
    c = {}
    c["c_ident"] = np.eye(128, dtype=np.float32).astype(ml_dtypes.bfloat16)
    c["c_identf"] = np.eye(128, dtype=np.float32)
    pidx = np.arange(128)
    invf = np.zeros((128, 2), np.float32)
    invf[:, 0] = 10000.0 ** (-(2.0 * (pidx % 32)) / 64.0)
    invf[:, 1] = 10000.0 ** (-(2.0 * (pidx % 64)) / 128.0)
    c["c_invf"] = invf
    for P_, nm in ((64, "c_perm64"), (128, "c_perm128")):
        h = P_ // 2
        m = np.zeros((P_, P_), np.float32)
        for i in range(P_):
            if i < h:
                m[i + h, i] = -1.0
            else:
                m[i - h, i] = 1.0
        c[nm] = m.astype(ml_dtypes.bfloat16)
    jj, ii = np.meshgrid(np.arange(128), np.arange(128), indexing="ij")
    c["c_tri"] = (jj <= ii).astype(np.float32).astype(ml_dtypes.bfloat16)
    c["c_anti"] = (jj > ii).astype(np.float32).astype(ml_dtypes.bfloat16)
    c["c_trif"] = (jj <= ii).astype(np.float32)
    cc = np.arange(256)
    tt = np.arange(S)
    cm = ((16 * cc[:, None] + 31) <= tt[None, :]) & (cc[:, None] < 255)
    c["c_cmask"] = cm.reshape(2, 128, S).astype(np.float32).astype(ml_dtypes.bfloat16)
    nn = np.arange(64)
    ovl = ((16 * cc[:, None]) < (64 * nn[None, :] + 64)) & ((16 * cc[:, None] + 31) >= 64 * nn[None, :]) & (cc[:, None] < 255)
    vx = np.zeros((2, 128, 65), np.float32)
    vx[:, :, 0:64] = ovl.reshape(2, 128, 64)
    vx[:, :, 64] = 1.0
    vx[1, 127, :] = 0.0
    c["c_ovl"] = vx.astype(ml_dtypes.bfloat16)
    valid = (64 * nn[None, :] <= tt[:, None])
    cur = tt // 64
    forced = (nn[None, :] == 0) | (nn[None, :] == cur[:, None]) | (nn[None, :] == cur[:, None] - 1)
    c["c_valid"] = valid.astype(np.float32).reshape(NT, 128, 64).transpose(1, 0, 2).copy()
    c["c_seladd"] = ((valid.astype(np.float32) - 1.0) + (forced & valid).astype(np.float32) * 1e9).reshape(NT, 128, 64).transpose(1, 0, 2).copy()
    ex = np.zeros((64, NT, 128), np.float32)
    for kt in range(NT):
        ex[2 * kt, kt, 0:64] = 1.0
        ex[2 * kt + 1, kt, 64:128] = 1.0
    c["c_expand"] = ex.astype(ml_dtypes.bfloat16)
    return c


CONST_SPECS = [("c_ident", [128, 128], BF16), ("c_identf", [128, 128], F32), ("c_invf", [128, 2], F32),
               ("c_perm64", [64, 64], BF16), ("c_perm128", [128, 128], BF16), ("c_tri", [128, 128], BF16),
               ("c_anti", [128, 128], BF16), ("c_trif", [128, 128], F32), ("c_cmask", [2, 128, S], BF16),
               ("c_ovl", [2, 128, 65], BF16), ("c_valid", [128, NT, 64], F32), ("c_seladd", [128, NT, 64], F32),
               ("c_expand", [64, NT, 128], BF16)]


class LazyInputs(dict):
    def __init__(self, nc):
        super().__init__()
        self.nc = nc
        self.specs = {n: (sh, d) for n, sh, d in INPUT_SPECS + CONST_SPECS}

    def __missing__(self, name):
        sh, d = self.specs[name]
        ap = self.nc.dram_tensor(name, list(sh), d, kind="ExternalInput").ap()
        self[name] = ap
        return ap


def build(dbg_in=(), dbg_out=(), stages=("inproj", "A", "B", "C", "D", "outproj"), layers=(0, 1), **dbg_attrs):
    nc = bass.Bass("TRN2", target_bir_lowering=False)
    k = Ctx()
    k.nc = nc
    k.dbg_in = set(dbg_in)
    k.dbg_out = set(dbg_out)
    for kk_, vv_ in dbg_attrs.items():
        setattr(k, kk_, vv_)
    k.inp = LazyInputs(nc)
    k.out = nc.dram_tensor("out", [S, D], F32, kind="ExternalOutput").ap()
    b_out = Buf("out")
    p = k.p = Prog(nc)
    k.b_ps = [Buf("ps%d" % i, excl=True) for i in range(8)]
    k.b_const = Buf("const")
    k.ds_misc = p.dsem("misc")
    k.ds_miscp = p.dsem("miscp")
    make_scratch(k)
    k.ident = p.sb([128, 128], BF16, "ident")
    k.identf = p.sb([128, 128], F32, "identf")
    k.eps_col = p.sb([128, 1], F32, "eps")
    p.dma("sp", k.ds_misc, k.ident, k.inp["c_ident"], writes=[k.b_const])
    p.dma("sp", k.ds_misc, k.identf, k.inp["c_identf"], writes=[k.b_const])
    p.op("pool", lambda e: e.memset(k.eps_col, EPS), writes=[k.b_const])
    k.one_col = p.sb([128, 1], F32, "one")
    p.op("pool", lambda e: e.memset(k.one_col, 1.0), writes=[k.b_const])
    k.onesb = p.sb([128, 128], BF16, "onesb")
    k.zerosb = p.sb([128, 512], BF16, "zerosb")
    p.op("pool", lambda e: e.memset(k.onesb, 1.0), writes=[k.b_const])
    p.op("pool", lambda e: e.memset(k.zerosb, 0.0), writes=[k.b_const])
    k.tri = p.sb([128, 128], BF16, "tri")
    k.anti = p.sb([128, 128], BF16, "anti")
    p.dma("sp", k.ds_misc, k.tri, k.inp["c_tri"], writes=[k.b_const])
    p.dma("sp", k.ds_misc, k.anti, k.inp["c_anti"], writes=[k.b_const])
    if "B" in stages or "D" in stages or "rope" in stages:
        stage_rope(k)
    b_x = Buf("x_in")
    x_cur, bx_cur = k.inp["x"], b_x
    for l in layers:
        if "inproj" in stages:
            stage_inproj(k, l, x_cur, bx_cur)
        if "A" in stages:
            stage_gdn(k, l)
        if "B" in stages:
            stage_mla(k, l)
        if "C" in stages:
            stage_lru(k, l)
        if "D" in stages:
            stage_nsa(k, l)
        if "outproj" in stages:
            if l == layers[-1]:
                x_dst, bx_dst = k.out, b_out
            else:
                x_dst, bx_dst = k.sc["xs"], k.scb["xs"]
            stage_outproj(k, l, x_cur, bx_cur, x_dst, bx_dst)
            x_cur, bx_cur = x_dst, bx_dst
    outs = [b_out] + [k.scb[n] for n in k.dbg_out if n in k.scb]
    for eng in ("sp", "pool", "act"):
        p.wait_all(eng, outs)
    p.emit()
    return nc, k


def make_in_maps(inputs, ncores=8, used=None):
    consts = host_consts()
    maps = []
    for c in range(ncores):
        b = c % 4
        m = {}
        for name, shape, dt_ in INPUT_SPECS:
            if used is not None and name not in used:
                continue
            a = np.asarray(inputs[name])
            if name in ("x", "positions"):
                a = a[b]
            m[name] = np.ascontiguousarray(a)
        for cn, cv in consts.items():
            if used is None or cn in used:
                m[cn] = cv
        maps.append(m)
    return maps


_CACHE = {}


def kernel(**inputs):
    if "nc" not in _CACHE:
        _CACHE["nc"], _CACHE["k"] = build()
    nc = _CACHE["nc"]
    maps = make_in_maps(inputs, used=set(_CACHE["k"].inp.keys()))
    res = run_bass_kernel_spmd(nc, maps, core_ids=list(range(8)))
    out = np.stack([np.asarray(res.results[b]["out"]) for b in range(4)], axis=0)
    return out.astype(np.float32)


def load_cols(k, dst, src_ap, b_dst, eng="pool"):
    k.p.dma(eng, k.ds_miscp if eng == "pool" else k.ds_misc, dst, src_ap, writes=[b_dst], allow_slow_non_contiguous=True)


def stage_lru(k, l):
    p = k.p
    inp = k.inp
    p.mark()
    b_par = Buf("lru_par")
    cw = p.sb([128, 4, 4], F32, "lcw")
    for g in range(4):
        load_cols(k, cw[:, g, :], inp["lru_conv_w"][l][:, g * 128:(g + 1) * 128].rearrange("j q -> q j"), b_par)
    cols = {}
    for nm in ("lru_conv_b", "lru_b_r", "lru_b_i", "lru_lambda"):
        cols[nm] = p.sb([128, 4], F32, nm)
        load_cols(k, cols[nm], inp[nm][l].rearrange("(g q) -> q g", q=128), b_par)
    sp = p.sb([128, 4], F32, "lsp")
    ccol = p.sb([128, 4], F32, "lcc")
    p.op("act", lambda e: e.activation(out=sp, in_=cols["lru_lambda"], func=AF.Exp, scale=-1.0), reads=[b_par], writes=[b_par])
    p.op("act", lambda e: e.activation(out=sp, in_=sp, func=AF.Ln, bias=k.one_col, scale=1.0), reads=[b_par], writes=[b_par])
    p.op("dve", lambda e: e.tensor_scalar(out=ccol, in0=sp, scalar1=-8.0, scalar2=None, op0=ALU.mult), reads=[b_par], writes=[b_par])
    wf = p.sb([128, 2, 128], F32, "lwf")
    b_wf = Buf("lwf")
    wbf = p.sb([128, 2, 128], BF16, "lwb")
    b_wbf = Buf("lwb")
    ds_w = p.dsem("lruw")
    xin = p.sb([128, 3 + S], BF16, "lxin")
    b_xin = Buf("lxin")
    ds_x = p.dsem("lrux")
    gin = p.sb([128, S], BF16, "lgin")
    b_gin = Buf("lgin")
    ds_g = p.dsem("lrug")
    xc = p.sb([128, S], F32, "lxc")
    b_xc = Buf("lxc")
    xcb = p.sb([128, S], BF16, "lxcb")
    b_xcb = Buf("lxcb")
    ra = p.sb([128, S], F32, "lra")
    b_ra = Buf("lra")
    ig = p.sb([128, S], F32, "lig")
    b_ig = Buf("lig")
    bb = p.sb([128, S], F32, "lbb")
    b_bb = Buf("lbb")
    hh = p.sb([128, S], F32, "lhh")
    b_hh = Buf("lhh")
    yb = p.sb([128, S], BF16, "lyb")
    b_yb = Buf("lyb")
    ds_y = p.dsem("lruy")
    b_ps = k.b_ps
    p.op("pool", lambda e: e.memset(xin[:, 0:3], 0.0), writes=[b_xin])
    pc = 0
    for g in range(4):
        p.dma("sp", ds_x, xin[:, 3:3 + S], k.sc["C_xT"][g * 128:(g + 1) * 128, :], reads=[k.scb["C_xT"]], writes=[b_xin])
        p.dma("sp", ds_g, gin, k.sc["C_gateT"][g * 128:(g + 1) * 128, :], reads=[k.scb["C_gateT"]], writes=[b_gin])
        p.dma("sp", ds_w, wf[:, 0, :], inp["lru_w_r"][l, g], writes=[b_wf])
        p.dma("sp", ds_w, wf[:, 1, :], inp["lru_w_i"][l, g], writes=[b_wf], batch=True)
        p.op("dve", lambda e: e.tensor_copy(out=wbf, in_=wf), reads=[b_wf], writes=[b_wbf])
        cut = 99
        if cut >= 2:
          p.op("dve", lambda e, g=g: e.tensor_scalar(out=xc, in0=xin[:, 3:3 + S], scalar1=cw[:, g, 3:4], scalar2=cols["lru_conv_b"][:, g:g + 1], op0=ALU.mult, op1=ALU.add),
             reads=[b_xin, b_par], writes=[b_xc])
        for j in range(3 if cut >= 2 else 0):
            p.op("dve", lambda e, g=g, j=j: e.scalar_tensor_tensor(out=xc, in0=xin[:, j:j + S], scalar=cw[:, g, j:j + 1], in1=xc, op0=ALU.mult, op1=ALU.add),
                 reads=[b_xin, b_par, b_xc], writes=[b_xc])
        p.op("act", lambda e: e.activation(out=xcb, in_=xc, func=AF.Copy), reads=[b_xc], writes=[b_xcb])
        for tq in range(S // 512 if cut >= 3 else 0):
            for which, dstt, bd, bias in ((0, ra, b_ra, "lru_b_r"), (1, ig, b_ig, "lru_b_i")):
                bank = pc % 4
                pc += 1
                ps = p.psum(bank)
                p.op("pe", lambda e, ps=ps, which=which, tq=tq: e.matmul(ps, lhsT=wbf[:, which, :], rhs=xcb[:, tq * 512:(tq + 1) * 512], start=True, stop=True),
                     reads=[b_wbf, b_xcb], writes=[b_ps[bank]])
                p.op("act", lambda e, ps=ps, dstt=dstt, tq=tq, bias=bias, g=g: e.activation(out=dstt[:, tq * 512:(tq + 1) * 512], in_=ps, func=AF.Sigmoid, bias=cols[bias][:, g:g + 1]),
                     reads=[b_ps[bank], b_par], writes=[bd])
        if cut < 4:
            p.dma("pool", ds_y, k.sc["yT"][1024 + g * 128:1024 + (g + 1) * 128, :], yb, reads=[b_yb], writes=[k.scb["yT"]])
            continue
        p.op("act", lambda e, g=g: e.activation(out=ra, in_=ra, func=AF.Exp, scale=ccol[:, g:g + 1]), reads=[b_ra, b_par], writes=[b_ra])
        p.op("dve", lambda e: e.tensor_tensor(out=bb, in0=ra, in1=ra, op=ALU.mult), reads=[b_ra], writes=[b_bb])
        p.op("act", lambda e: e.activation(out=bb, in_=bb, func=AF.Sqrt, scale=-1.0, bias=k.one_col), reads=[b_bb], writes=[b_bb])
        p.op("pool", lambda e: e.tensor_tensor(out=ig, in0=ig, in1=xc, op=ALU.mult), reads=[b_ig, b_xc], writes=[b_ig])
        p.op("dve", lambda e: e.tensor_tensor(out=bb, in0=bb, in1=ig, op=ALU.mult), reads=[b_bb, b_ig], writes=[b_bb])
        if cut >= 5:
            p.op("dve", lambda e: e.tensor_tensor_scan(out=hh, data0=ra, data1=bb, initial=0.0, op0=ALU.mult, op1=ALU.add), reads=[b_ra, b_bb], writes=[b_hh])
        p.op("act", lambda e: e.activation(out=ig, in_=gin, func=AF.Silu), reads=[b_gin], writes=[b_ig])
        p.op("dve", lambda e: e.tensor_tensor(out=yb, in0=hh, in1=ig, op=ALU.mult), reads=[b_hh, b_ig], writes=[b_yb])
        p.dma("pool", ds_y, k.sc["yT"][1024 + g * 128:1024 + (g + 1) * 128, :], yb, reads=[b_yb], writes=[k.scb["yT"]])
    p.release()


TWO_PI = 6.283185307179586


def stage_rope(k):
    p = k.p
    inp = k.inp
    p.mark()
    posi = p.sb([128, S], I32, "posi")
    b_pos = Buf("posi")
    p.dma("sp", k.ds_misc, posi, inp["positions"].partition_broadcast(128), writes=[b_pos])
    posf = p.sb([128, S], F32, "posf")
    b_posf = Buf("posf")
    p.op("dve", lambda e: e.tensor_copy(out=posf, in_=posi), reads=[b_pos], writes=[b_posf])
    invf = p.sb([128, 2], F32, "invf")
    b_invf = Buf("invf")
    p.dma("sp", k.ds_misc, invf, inp["c_invf"], writes=[b_invf])
    ang = p.sb([128, S], F32, "ang")
    qi = p.sb([128, S], I32, "qi")
    qf = p.sb([128, S], F32, "qf")
    r = p.sb([128, S], F32, "r")
    ob = p.sb([128, S], BF16, "ropeo")
    b_t = Buf("ropetmp")
    b_ob = Buf("ropeo")
    ds = p.dsem("ropeo")
    negpi = p.sb([128, 1], F32, "negpi")
    p.op("pool", lambda e: e.memset(negpi, -3.141592653589793), writes=[b_t])
    for tab in range(2):
        for cs in range(2):
            shift = 1.5707963267948966 if cs == 0 else 0.0
            p.op("dve", lambda e, tab=tab, shift=shift: e.tensor_scalar(out=ang, in0=posf, scalar1=invf[:, tab:tab + 1], scalar2=shift, op0=ALU.mult, op1=ALU.add),
                 reads=[b_posf, b_invf], writes=[b_t])
            p.op("dve", lambda e: e.tensor_scalar(out=qf, in0=ang, scalar1=1.0 / TWO_PI, scalar2=None, op0=ALU.mult), reads=[b_t], writes=[b_t])
            p.op("dve", lambda e: e.tensor_copy(out=qi, in_=qf), reads=[b_t], writes=[b_t])
            p.op("dve", lambda e: e.tensor_copy(out=qf, in_=qi), reads=[b_t], writes=[b_t])
            p.op("dve", lambda e: e.scalar_tensor_tensor(out=r, in0=qf, scalar=-TWO_PI, in1=ang, op0=ALU.mult, op1=ALU.add), reads=[b_t], writes=[b_t])
            p.op("dve", lambda e: e.tensor_scalar(out=qf, in0=r, scalar1=0.0, scalar2=TWO_PI, op0=ALU.is_lt, op1=ALU.mult), reads=[b_t], writes=[b_t])
            p.op("dve", lambda e: e.tensor_tensor(out=r, in0=r, in1=qf, op=ALU.add), reads=[b_t], writes=[b_t])
            p.op("dve", lambda e: e.tensor_scalar(out=qf, in0=r, scalar1=TWO_PI, scalar2=-TWO_PI, op0=ALU.is_ge, op1=ALU.mult), reads=[b_t], writes=[b_t])
            p.op("dve", lambda e: e.tensor_tensor(out=r, in0=r, in1=qf, op=ALU.add), reads=[b_t], writes=[b_t])
            p.op("dve", lambda e: e.tensor_scalar(out=r, in0=r, scalar1=0.0, scalar2=TWO_PI, op0=ALU.max, op1=ALU.min), reads=[b_t], writes=[b_t])
            p.op("act", lambda e: e.activation(out=qf, in_=r, func=AF.Sin, bias=negpi, scale=1.0), reads=[b_t], writes=[b_t])
            p.op("dve", lambda e: e.tensor_scalar(out=ob, in0=qf, scalar1=-1.0, scalar2=None, op0=ALU.mult), reads=[b_t], writes=[b_ob])
            p.dma("sp", ds, k.sc["rope"][tab * 2 + cs], ob, reads=[b_ob], writes=[k.scb["rope"]])
    p.release()


def fm_rmsnorm(k, srcs, P, nfeat, gcol, out, reads, writes, bank, tmp):
    p = k.p
    N = srcs[0].shape[-1]
    ps = p.psum(bank)
    b_ps = k.b_ps[bank]
    for ci, src in enumerate(srcs):
        p.op("act", lambda e, src=src, ci=ci: e.activation(out=tmp["sq"][0:P, ci, 0:N], in_=src, func=AF.Square), reads=reads, writes=[tmp["b_sq"]])
    for ci in range(len(srcs)):
        p.op("pe", lambda e, ci=ci: e.matmul(ps[0:P, 0:N], lhsT=k.onesb[0:P, 0:P], rhs=tmp["sq"][0:P, ci, 0:N], start=(ci == 0), stop=(ci == len(srcs) - 1)),
             reads=[tmp["b_sq"], k.b_const], writes=[b_ps])
    p.op("act", lambda e: e.activation(out=tmp["rstd"][0:P, 0:N], in_=ps[0:P, 0:N], func=AF.Sqrt, scale=1.0 / nfeat, bias=k.eps_col[0:P]), reads=[b_ps], writes=[tmp["b_rstd"]])
    p.op("dve", lambda e: e.reciprocal(out=tmp["rstd"][0:P, 0:N], in_=tmp["rstd"][0:P, 0:N]), reads=[tmp["b_rstd"]], writes=[tmp["b_rstd"]])
    for ci, src in enumerate(srcs):
        if gcol is not None:
            p.op("dve", lambda e, src=src, ci=ci: e.scalar_tensor_tensor(out=out[ci], in0=src, scalar=gcol[ci], in1=tmp["rstd"][0:P, 0:N], op0=ALU.mult, op1=ALU.mult),
                 reads=list(reads) + [tmp["b_rstd"]], writes=writes)
        else:
            p.op("dve", lambda e, src=src, ci=ci: e.tensor_tensor(out=out[ci], in0=src, in1=tmp["rstd"][0:P, 0:N], op=ALU.mult),
                 reads=list(reads) + [tmp["b_rstd"]], writes=writes)


def fm_rope(k, x, P, perm, cos, sin, out, reads, writes, bank, tmp):
    p = k.p
    N = x.shape[-1]
    ps = p.psum(bank)
    b_ps = k.b_ps[bank]
    p.op("pe", lambda e: e.matmul(ps[0:P, 0:N], lhsT=perm, rhs=x, start=True, stop=True), reads=list(reads) + [k.b_const], writes=[b_ps])
    p.op("dve", lambda e: e.tensor_tensor(out=tmp["r1"][0:P, 0:N], in0=ps[0:P, 0:N], in1=sin, op=ALU.mult), reads=[b_ps, tmp["b_tab"]], writes=[tmp["b_r1"]])
    p.op("dve", lambda e: e.tensor_tensor(out=tmp["r2"][0:P, 0:N], in0=x, in1=cos, op=ALU.mult), reads=list(reads) + [tmp["b_tab"]], writes=[tmp["b_r2"]])
    p.op("dve", lambda e: e.tensor_tensor(out=out, in0=tmp["r1"][0:P, 0:N], in1=tmp["r2"][0:P, 0:N], op=ALU.add), reads=[tmp["b_r1"], tmp["b_r2"]], writes=writes)


def attention(k, q_chunks, k_chunks, V, b_q, b_k, b_v, scale, mask_kind, out_cb, extra_mask=None, nkt=NT):
    p = k.p
    b_ps = k.b_ps
    if getattr(k, "attn_cache", None) is None:
        k.attn_cache = ([p.sb([128, 512], BF16, "Pt%d" % i) for i in range(3)], [Buf("Pt%d" % i) for i in range(3)])
    Pt, b_Pt = k.attn_cache
    cnt = 0
    for qg in range(S // 512):
        qts = [4 * qg + j for j in range(4)]
        kts = [kt for kt in range(nkt) if any(mask_kind(kt, qt) for qt in qts)]
        Obanks = [p.psum(2), p.psum(3)]
        for bi in range(2):
            p.op("pe", lambda e, bi=bi: e.matmul(Obanks[bi][:, 0:258], lhsT=k.zerosb[:, 0:128], rhs=k.zerosb[:, 0:258], start=True, stop=False, skip_group_check=True),
                 reads=[k.b_const], writes=[b_ps[2 + bi]])
        last_kt = {qt: max(kt for kt in kts if mask_kind(kt, qt)) for qt in qts if any(mask_kind(kt, qt) for kt in kts)}
        for kt in kts:
            kinds = [mask_kind(kt, qt) for qt in qts]
            js = [j for j in range(4) if kinds[j]]
            j0, j1 = js[0], js[-1] + 1
            c0, c1 = j0 * 128, j1 * 128
            sb_i = cnt % 2
            pi = cnt % 3
            cnt += 1
            ps = p.psum(sb_i)
            for ci in range(len(q_chunks)):
                p.op("pe", lambda e, ps=ps, ci=ci, kt=kt, c0=c0, c1=c1, qg=qg: e.matmul(ps[:, c0:c1], lhsT=k_chunks[ci][:, kt * 128:(kt + 1) * 128], rhs=q_chunks[ci][:, qg * 512 + c0:qg * 512 + c1], start=(ci == 0), stop=(ci == len(q_chunks) - 1)),
                     reads=[b_q] + (list(b_k) if isinstance(b_k, (list, tuple)) else [b_k]), writes=[b_ps[sb_i]])
            p.op("act", lambda e, ps=ps, pi=pi, c0=c0, c1=c1: e.activation(out=Pt[pi][:, c0:c1], in_=ps[:, c0:c1], func=AF.Exp, scale=scale),
                 reads=[b_ps[sb_i]], writes=[b_Pt[pi]])
            for j in js:
                if kinds[j] in ("diag", "anti"):
                    m = k.tri if kinds[j] == "diag" else k.anti
                    p.op("dve", lambda e, pi=pi, j=j, m=m: e.tensor_tensor(out=Pt[pi][:, j * 128:(j + 1) * 128], in0=Pt[pi][:, j * 128:(j + 1) * 128], in1=m, op=ALU.mult),
                         reads=[b_Pt[pi], k.b_const], writes=[b_Pt[pi]])
            if extra_mask is not None:
                ml, mr, mreads = extra_mask(kt, qg)
                mb = 4 + (cnt % 2)
                psm = p.psum(mb)
                p.op("pe", lambda e, psm=psm, ml=ml, mr=mr, c0=c0, c1=c1: e.matmul(psm[:, c0:c1], lhsT=ml, rhs=mr[:, c0:c1], start=True, stop=True),
                     reads=mreads, writes=[b_ps[mb]])
                p.op("dve", lambda e, psm=psm, pi=pi, c0=c0, c1=c1: e.tensor_tensor(out=Pt[pi][:, c0:c1], in0=Pt[pi][:, c0:c1], in1=psm[:, c0:c1], op=ALU.mult),
                     reads=[b_ps[mb], b_Pt[pi]], writes=[b_Pt[pi]])
            for j in js:
                qt = qts[j]
                O = Obanks[j // 2][:, (j % 2) * 129:(j % 2) * 129 + 129]
                p.op("pe", lambda e, O=O, pi=pi, j=j, kt=kt, qt=qt, lk=last_kt: e.matmul(O, lhsT=Pt[pi][:, j * 128:(j + 1) * 128], rhs=V[:, kt, :], start=False, stop=(kt == lk[qt]), skip_group_check=True),
                     reads=[b_Pt[pi], b_v], writes=[b_ps[2 + j // 2]])
        for j in range(4):
            O = Obanks[j // 2][:, (j % 2) * 129:(j % 2) * 129 + 129]
            out_cb(qts[j], O, b_ps[2 + j // 2])


def stage_mla(k, l):
    p = k.p
    inp = k.inp
    p.mark()
    b_ps = k.b_ps
    b_par = Buf("mla_par")
    qg_c = p.sb([128, 3], F32, "mqg")
    kvg_c = p.sb([128, 2], F32, "mkvg")
    load_cols(k, qg_c, inp["mla_q_norm_g"][l].rearrange("(c q) -> q c", q=128), b_par)
    load_cols(k, kvg_c, inp["mla_kv_norm_g"][l].rearrange("(c q) -> q c", q=128), b_par)
    qkg = p.sb([128, 4], F32, "mqkg")
    load_cols(k, qkg[:, 0:1], inp["mla_qk_norm_g"][l, 0, 0:128].rearrange("(q o) -> q o", o=1), b_par)
    load_cols(k, qkg[0:64, 1:2], inp["mla_qk_norm_g"][l, 0, 128:192].rearrange("(q o) -> q o", o=1), b_par)
    load_cols(k, qkg[:, 2:3], inp["mla_qk_norm_g"][l, 1, 0:128].rearrange("(q o) -> q o", o=1), b_par)
    load_cols(k, qkg[0:64, 3:4], inp["mla_qk_norm_g"][l, 1, 128:192].rearrange("(q o) -> q o", o=1), b_par)
    perm = p.sb([64, 64], BF16, "mperm")
    p.dma("sp", k.ds_misc, perm, inp["c_perm64"], writes=[b_par])
    tmp = dict(sq=p.sb([128, 3, 512], BF16, "msq"), b_sq=Buf("msq"), rstd=p.sb([128, 512], F32, "mrstd"), b_rstd=Buf("mrstd"),
               r1=p.sb([128, 512], F32, "mr1"), b_r1=Buf("mr1"), r2=p.sb([128, 512], F32, "mr2"), b_r2=Buf("mr2"), b_tab=Buf("mtab"))
    cosT = p.sb([64, S], BF16, "mcos")
    sinT = p.sb([64, S], BF16, "msin")
    p.dma("sp", k.ds_misc, cosT, k.sc["rope"][0, 0:64, :], reads=[k.scb["rope"]], writes=[tmp["b_tab"]])
    p.dma("sp", k.ds_misc, sinT, k.sc["rope"][1, 0:64, :], reads=[k.scb["rope"]], writes=[tmp["b_tab"]])
    wqf = p.sb([128, 3, 768], F32, "mwqf")
    wq = p.sb([128, 3, 768], BF16, "mwq")
    wkvf = p.sb([128, 2, 1024], F32, "mwkvf")
    wkv = p.sb([128, 2, 1024], BF16, "mwkv")
    b_wf = Buf("mwf")
    b_w = Buf("mw")
    ds_w = p.dsem("mlaw")
    p.dma("sp", ds_w, wqf, inp["mla_w_uq"][l].rearrange("(c q) h d -> q c (h d)", q=128), writes=[b_wf])
    p.dma("sp", ds_w, wkvf, inp["mla_w_ukv"][l].rearrange("(c q) h d -> q c (h d)", q=128), writes=[b_wf], batch=True)
    for c in range(3):
        p.op("dve", lambda e, c=c: e.tensor_scalar(out=wq[:, c, :], in0=wqf[:, c, :], scalar1=qg_c[:, c:c + 1], scalar2=None, op0=ALU.mult), reads=[b_wf, b_par], writes=[b_w])
    for c in range(2):
        p.op("dve", lambda e, c=c: e.tensor_scalar(out=wkv[:, c, :], in0=wkvf[:, c, :], scalar1=kvg_c[:, c:c + 1], scalar2=None, op0=ALU.mult), reads=[b_wf, b_par], writes=[b_w])
    cT = p.sb([128, 6, S], BF16, "mcT")
    b_cT = Buf("mcT")
    ds_c = p.dsem("mlac")
    for c in range(6):
        rows = 128 if c < 5 else 64
        p.dma("sp", ds_c, cT[0:rows, c, :], k.sc["B_cT"][c * 128:c * 128 + rows, :], reads=[k.scb["B_cT"]], writes=[b_cT], batch=(c > 0))
    kpeT = p.sb([64, S], BF16, "mkpe")
    b_kpe = Buf("mkpe")
    xtmp = p.sb([128, 512], BF16, "mxtmp")
    b_xtmp = Buf("mxtmp")
    for blk in range(8):
        sl = slice(blk * 512, (blk + 1) * 512)
        fm_rmsnorm(k, [cT[:, c, sl] for c in range(3)], 128, 384, None, [cT[:, c, sl] for c in range(3)], [b_cT], [b_cT], 6, tmp)
        fm_rmsnorm(k, [cT[:, c, sl] for c in (3, 4)], 128, 256, None, [cT[:, c, sl] for c in (3, 4)], [b_cT], [b_cT], 7, tmp)
        fm_rmsnorm(k, [cT[0:64, 5, sl]], 64, 64, [qkg[0:64, 3:4]], [xtmp[0:64, :]], [b_cT, b_par], [b_xtmp], 6, tmp)
        fm_rope(k, xtmp[0:64, :], 64, perm, cosT[:, sl], sinT[:, sl], kpeT[:, sl], [b_xtmp, b_par], [b_kpe], 7, tmp)
    sg = p.sb([128, NT, 512], BF16, "msg")
    b_sg = Buf("msg")
    ds_g = p.dsem("mlag")
    p.dma("sp", ds_g, sg, k.sc["B_gate"].rearrange("(a q) c -> q a c", q=128), reads=[k.scb["B_gate"]], writes=[b_sg])
    for a in range(0, NT, 8):
        p.op("act", lambda e, a=a: e.activation(out=sg[:, a:a + 8, :], in_=sg[:, a:a + 8, :], func=AF.Silu), reads=[b_sg], writes=[b_sg])
    qT = p.sb([128, S], BF16, "mqT")
    qpeT = p.sb([64, S], BF16, "mqpeT")
    kT = p.sb([128, S], BF16, "mkT")
    V = p.sb([128, NT, 129], BF16, "mV")
    b_q, b_k, b_v = Buf("mq"), Buf("mk"), Buf("mv")
    ystage = p.sb([128, S], BF16, "mys")
    b_ys = Buf("mys")
    ds_y = p.dsem("mlay")
    otm = [p.sb([128, 128], BF16, "motm%d" % i) for i in range(2)]
    b_otm = [Buf("motm%d" % i) for i in range(2)]
    rs = [p.sb([128, 1], F32, "mrs%d" % i) for i in range(2)]
    b_rs = [Buf("mrs%d" % i) for i in range(2)]
    p.op("pool", lambda e: e.memset(V[:, :, 128:129], 1.0), writes=[b_v])
    k.attn_cache = None
    pc = 0
    for h in range(4):
        for blk in range(8):
            sl = slice(blk * 512, (blk + 1) * 512)
            bank = 4 + (pc % 2)
            pc += 1
            ps = p.psum(bank)
            for c in range(3):
                p.op("pe", lambda e, ps=ps, c=c, h=h, sl=sl: e.matmul(ps, lhsT=wq[:, c, h * 192:h * 192 + 128], rhs=cT[:, c, sl], start=(c == 0), stop=(c == 2)), reads=[b_w, b_cT], writes=[b_ps[bank]])
            fm_rmsnorm(k, [ps], 128, 128, [qkg[:, 0:1]], [qT[:, sl]], [b_ps[bank], b_par], [b_q], 6, tmp)
            bank = 4 + (pc % 2)
            pc += 1
            ps = p.psum(bank)
            for c in range(3):
                p.op("pe", lambda e, ps=ps, c=c, h=h, sl=sl: e.matmul(ps[0:64, :], lhsT=wq[:, c, h * 192 + 128:h * 192 + 192], rhs=cT[:, c, sl], start=(c == 0), stop=(c == 2)), reads=[b_w, b_cT], writes=[b_ps[bank]])
            fm_rmsnorm(k, [ps[0:64, :]], 64, 64, [qkg[0:64, 1:2]], [xtmp[0:64, :]], [b_ps[bank], b_par], [b_xtmp], 7, tmp)
            fm_rope(k, xtmp[0:64, :], 64, perm, cosT[:, sl], sinT[:, sl], qpeT[:, sl], [b_xtmp, b_par], [b_q], 6, tmp)
            bank = 4 + (pc % 2)
            pc += 1
            ps = p.psum(bank)
            for c in range(2):
                p.op("pe", lambda e, ps=ps, c=c, h=h, sl=sl: e.matmul(ps, lhsT=wkv[:, c, h * 256:h * 256 + 128], rhs=cT[:, 3 + c, sl], start=(c == 0), stop=(c == 1)), reads=[b_w, b_cT], writes=[b_ps[bank]])
            fm_rmsnorm(k, [ps], 128, 128, [qkg[:, 2:3]], [kT[:, sl]], [b_ps[bank], b_par], [b_k], 7, tmp)
        for kt in range(NT):
            bank = 4 + (pc % 2)
            pc += 1
            ps = p.psum(bank)
            for c in range(2):
                p.op("pe", lambda e, ps=ps, c=c, h=h, kt=kt: e.matmul(ps[:, 0:128], lhsT=cT[:, 3 + c, kt * 128:(kt + 1) * 128], rhs=wkv[:, c, h * 256 + 128:h * 256 + 256], start=(c == 0), stop=(c == 1)), reads=[b_w, b_cT], writes=[b_ps[bank]])
            copy_op(p, rr_engine(k), V[:, kt, 0:128], ps[:, 0:128], [b_ps[bank]], [b_v])

        def out_cb(qt, O, bO, h=h):
            i2 = qt % 2
            p.op("dve", lambda e: e.reciprocal(out=rs[i2], in_=O[:, 128:129]), reads=[bO], writes=[b_rs[i2]])
            p.op("dve", lambda e: e.scalar_tensor_tensor(out=otm[i2], in0=O[:, 0:128], scalar=rs[i2], in1=sg[:, qt, h * 128:(h + 1) * 128], op0=ALU.mult, op1=ALU.mult),
                 reads=[bO, b_rs[i2], b_sg], writes=[b_otm[i2]])
            pst = p.psum(6, BF16)
            p.op("pe", lambda e: e.transpose(out=pst[:, 0:128], in_=otm[i2], identity=k.ident), reads=[b_otm[i2], k.b_const], writes=[b_ps[6]])
            p.op("act", lambda e: e.activation(out=ystage[:, qt * 128:(qt + 1) * 128], in_=pst[:, 0:128], func=AF.Copy), reads=[b_ps[6]], writes=[b_ys])

        attention(k, [qT, qpeT], [kT, kpeT], V, b_q, [b_k, b_kpe], b_v, 192.0 ** -0.5,
                  lambda kt, qt: ("diag" if kt == qt else ("full" if kt < qt else None)), out_cb)
        p.dma("pool", ds_y, k.sc["yT"][512 + h * 128:512 + (h + 1) * 128, :], ystage, reads=[b_ys], writes=[k.scb["yT"]])
    p.release()


def stage_gdn(k, l):
    p = k.p
    inp = k.inp
    p.mark()
    b_ps = k.b_ps
    b_par = Buf("gpar")
    cw = p.sb([128, 12, 4], F32, "gcw")
    for g in range(12):
        load_cols(k, cw[:, g, :], inp["gdn_conv_w"][l][:, g * 128:(g + 1) * 128].rearrange("j q -> q j"), b_par)
    ng = p.sb([128, 1], F32, "gng")
    load_cols(k, ng, inp["gdn_norm_g"][l].rearrange("(q o) -> q o", o=1), b_par)
    alog = p.sb([128, 4], F32, "galog")
    dtb = p.sb([128, 4], F32, "gdtb")
    p.dma("sp", k.ds_misc, alog, inp["gdn_a_log"][l].partition_broadcast(128), writes=[b_par])
    p.dma("sp", k.ds_misc, dtb, inp["gdn_dt_bias"][l].partition_broadcast(128), writes=[b_par])
    nea = p.sb([128, 4], F32, "gnea")
    p.op("act", lambda e: e.activation(out=nea, in_=alog, func=AF.Exp), reads=[b_par], writes=[b_par])
    p.op("dve", lambda e: e.tensor_scalar(out=nea, in0=nea, scalar1=-1.0, scalar2=None, op0=ALU.mult), reads=[b_par], writes=[b_par])
    qsc = p.sb([128, 1], F32, "gqsc")
    p.op("pool", lambda e: e.memset(qsc, 128.0 ** -0.5), writes=[b_par])
    trif = p.sb([128, 128], F32, "gtrif")
    onesf = p.sb([128, 128], F32, "gonesf")
    p.dma("sp", k.ds_misc, trif, inp["c_trif"], writes=[b_par])
    p.op("pool", lambda e: e.memset(onesf, 1.0), writes=[b_par])
    gb = p.sb([128, NT, 8], F32, "ggb")
    b_gb = Buf("ggb")
    p.dma("sp", k.ds_misc, gb, k.sc["A_gb"].rearrange("(a q) c -> q a c", q=128), reads=[k.scb["A_gb"]], writes=[b_gb])
    beta = p.sb([128, NT, 4], F32, "gbeta")
    gg = p.sb([128, NT, 4], F32, "ggg")
    b_gt = Buf("ggates")
    p.op("act", lambda e: e.activation(out=beta, in_=gb[:, :, 4:8], func=AF.Sigmoid), reads=[b_gb], writes=[b_gt])
    for h in range(4):
        p.op("act", lambda e, h=h: e.activation(out=gg[:, :, h], in_=gb[:, :, h], func=AF.Exp, bias=dtb[:, h:h + 1]), reads=[b_gb, b_par], writes=[b_gt])
    p.op("act", lambda e: e.activation(out=gg, in_=gg, func=AF.Ln, bias=k.one_col), reads=[b_gt], writes=[b_gt])
    for h in range(4):
        p.op("dve", lambda e, h=h: e.tensor_scalar(out=gg[:, :, h], in0=gg[:, :, h], scalar1=nea[:, h:h + 1], scalar2=None, op0=ALU.mult), reads=[b_gt, b_par], writes=[b_gt])
    gc = p.sb([128, NT, 4], F32, "ggc")
    gcl = p.sb([128, NT, 4], F32, "ggcl")
    eg = p.sb([128, NT, 4], F32, "geg")
    egl = p.sb([128, NT, 4], F32, "gegl")
    cdec = p.sb([128, NT, 4], F32, "gcdec")
    bge = p.sb([128, NT, 4], F32, "gbge")
    ngc = p.sb([128, NT, 4], F32, "gngc")
    ps = p.psum(4)
    for t in range(NT):
        p.op("pe", lambda e, t=t: e.matmul(ps[:, t * 4:(t + 1) * 4], lhsT=trif, rhs=gg[:, t, :], start=True, stop=True), reads=[b_gt, b_par], writes=[b_ps[4]])
    p.op("dve", lambda e: e.tensor_copy(out=gc, in_=ps[:, 0:NT * 4].rearrange("q (a b) -> q a b", a=NT)), reads=[b_ps[4]], writes=[b_gt])
    ps2 = p.psum(5)
    for t in range(NT):
        p.op("pe", lambda e, t=t: e.matmul(ps2[:, t * 4:(t + 1) * 4], lhsT=onesf, rhs=gg[:, t, :], start=True, stop=True), reads=[b_gt, b_par], writes=[b_ps[5]])
    p.op("dve", lambda e: e.tensor_copy(out=gcl, in_=ps2[:, 0:NT * 4].rearrange("q (a b) -> q a b", a=NT)), reads=[b_ps[5]], writes=[b_gt])
    p.op("act", lambda e: e.activation(out=eg, in_=gc, func=AF.Exp), reads=[b_gt], writes=[b_gt])
    p.op("act", lambda e: e.activation(out=cdec, in_=gcl, func=AF.Exp), reads=[b_gt], writes=[b_gt])
    p.op("dve", lambda e: e.tensor_tensor(out=egl, in0=gcl, in1=gc, op=ALU.subtract), reads=[b_gt], writes=[b_gt])
    p.op("act", lambda e: e.activation(out=egl, in_=egl, func=AF.Exp), reads=[b_gt], writes=[b_gt])
    p.op("dve", lambda e: e.tensor_tensor(out=bge, in0=beta, in1=eg, op=ALU.mult), reads=[b_gt], writes=[b_gt])
    p.op("dve", lambda e: e.tensor_scalar(out=ngc, in0=gc, scalar1=-1.0, scalar2=None, op0=ALU.mult), reads=[b_gt], writes=[b_gt])
    xin = p.sb([128, 3 + S], BF16, "gxin")
    b_xin = Buf("gxin")
    ds_x = p.dsem("gdnx")
    p.op("pool", lambda e: e.memset(xin[:, 0:3], 0.0), writes=[b_xin])
    xc = p.sb([128, S], F32, "gxc")
    b_xc = Buf("gxc")
    qT = p.sb([128, S], F32, "gqT")
    kT = p.sb([128, S], F32, "gkT")
    vT = p.sb([128, S], F32, "gvT")
    b_q, b_k, b_v = Buf("gq"), Buf("gk"), Buf("gv")
    tmp = dict(sq=p.sb([128, 1, 512], BF16, "gsq"), b_sq=Buf("gsq"), rstd=p.sb([128, 512], F32, "grstd"), b_rstd=Buf("grstd"))
    gateT = p.sb([128, S], BF16, "ggate")
    b_gate = Buf("ggate")
    ds_g = p.dsem("gdng")
    ystage = p.sb([128, S], BF16, "gys")
    b_ys = Buf("gys")
    ds_y = p.dsem("gdny")
    Gb = p.sb([128, 128], F32, "gGb")
    b_Gb = Buf("gGb")
    t1 = p.sb([128, 128], F32, "gt1")
    ta = p.sb([128, 128], F32, "gta")
    tb = p.sb([128, 128], F32, "gtb")
    E1 = p.sb([128, 128], F32, "gE1")
    E2 = p.sb([128, 128], F32, "gE2")
    egrow = p.sb([128, 128], F32, "gegrow")
    b_d = Buf("gdecay")
    Lm = [p.sb([128, 128], F32, "gL%d" % i) for i in range(2)]
    LmT = [p.sb([128, 128], F32, "gLT%d" % i) for i in range(2)]
    b_L = [Buf("gL%d" % i) for i in range(2)]
    b_LT = [Buf("gLT%d" % i) for i in range(2)]
    TT = [p.sb([128, 128], F32, "gTT%d" % i) for i in range(2)]
    b_TT = [Buf("gTT%d" % i) for i in range(2)]
    qkT = p.sb([128, 128], F32, "gqkT")
    b_qk = Buf("gqkT")
    vk = p.sb([128, 256], F32, "gvk")
    b_vk = Buf("gvk")
    kend = p.sb([128, 128], F32, "gkend")
    b_kend = Buf("gkend")
    uwf = p.sb([128, 256], F32, "guwf")
    uwb = p.sb([128, 256], BF16, "guwb")
    b_uw = Buf("guw")
    AT = p.sb([128, 128], F32, "gAT")
    Bn = p.sb([128, 128], F32, "gBn")
    b_AB = Buf("gAB")
    Sf = [p.sb([128, 128], F32, "gSf%d" % i) for i in range(2)]
    Sb = [p.sb([128, 128], BF16, "gSb%d" % i) for i in range(2)]
    b_S = [Buf("gS%d" % i) for i in range(2)]
    QpT = p.sb([128, 128], F32, "gQpT")
    b_Qp = Buf("gQp")
    oT = p.sb([128, 512], F32, "goT")
    b_oT = Buf("goT")
    sgT = p.sb([128, 512], BF16, "gsgT")
    b_sg = Buf("gsgT")
    cut = getattr(k, "gdn_cut", 99)
    for h in range(getattr(k, "gdn_heads", 4)):
        if cut < 2:
            break
        for which, dst, bd in ((0, qT, b_q), (1, kT, b_k), (2, vT, b_v)):
            grp = which * 4 + h
            p.dma("sp", ds_x, xin[:, 3:3 + S], k.sc["A_qkvT"][grp * 128:(grp + 1) * 128, :], reads=[k.scb["A_qkvT"]], writes=[b_xin])
            p.op("dve", lambda e, grp=grp: e.tensor_scalar(out=xc, in0=xin[:, 3:3 + S], scalar1=cw[:, grp, 3:4], scalar2=None, op0=ALU.mult), reads=[b_xin, b_par], writes=[b_xc])
            for j in range(3):
                p.op("dve", lambda e, grp=grp, j=j: e.scalar_tensor_tensor(out=xc, in0=xin[:, j:j + S], scalar=cw[:, grp, j:j + 1], in1=xc, op0=ALU.mult, op1=ALU.add),
                     reads=[b_xin, b_par, b_xc], writes=[b_xc])
            if which == 2:
                p.op("act", lambda e: e.activation(out=vT, in_=xc, func=AF.Silu), reads=[b_xc], writes=[b_v])
            else:
                p.op("act", lambda e: e.activation(out=xc, in_=xc, func=AF.Silu), reads=[b_xc], writes=[b_xc])
                for blk in range(8):
                    sl = slice(blk * 512, (blk + 1) * 512)
                    fm_rmsnorm(k, [xc[:, sl]], 128, 1.0, [qsc] if which == 0 else None, [dst[:, sl]], [b_xc, b_par], [bd], 6 + blk % 2, tmp)
        p.dma("sp", ds_g, gateT, k.sc["A_gateT"][h * 128:(h + 1) * 128, :], reads=[k.scb["A_gateT"]], writes=[b_gate])
        p.op("pool", lambda e: e.memset(Sf[0], 0.0), writes=[b_S[0]])
        p.op("pool", lambda e: e.memset(Sb[0], 0.0), writes=[b_S[0]])
        for t in range(getattr(k, "gdn_tiles", NT)):
            if cut < 3:
                break
            ts = slice(t * 128, (t + 1) * 128)
            si, so = t % 2, (t + 1) % 2
            col = lambda a, t=t, h=h: a[:, t, h:h + 1]
            p.op("dve", lambda e, t=t, h=h: e.tensor_copy(out=Gb, in_=gg[:, t, h:h + 1].to_broadcast([128, 128])), reads=[b_gt], writes=[b_Gb])
            pr = p.psum(0)
            p.op("pe", lambda e, pr=pr: e.matmul(pr[:, 0:128], lhsT=Gb, rhs=trif, start=True, stop=True), reads=[b_Gb, b_par], writes=[b_ps[0]])
            p.op("dve", lambda e, pr=pr, t=t, h=h: e.tensor_scalar(out=t1, in0=pr[:, 0:128], scalar1=gc[:, t, h:h + 1], scalar2=None, op0=ALU.subtract), reads=[b_ps[0], b_gt], writes=[b_d])
            p.op("act", lambda e, pr=pr: e.activation(out=egrow, in_=pr[:, 0:128], func=AF.Exp), reads=[b_ps[0]], writes=[b_d])
            p.op("dve", lambda e: e.tensor_scalar(out=ta, in0=t1, scalar1=0.0, scalar2=None, op0=ALU.min), reads=[b_d], writes=[b_d])
            p.op("dve", lambda e: e.tensor_scalar(out=tb, in0=t1, scalar1=0.0, scalar2=-1.0, op0=ALU.max, op1=ALU.mult), reads=[b_d], writes=[b_d])
            p.op("act", lambda e: e.activation(out=ta, in_=ta, func=AF.Exp), reads=[b_d], writes=[b_d])
            p.op("act", lambda e: e.activation(out=tb, in_=tb, func=AF.Exp), reads=[b_d], writes=[b_d])
            p.op("dve", lambda e: e.tensor_tensor(out=E1, in0=ta, in1=k.tri, op=ALU.mult), reads=[b_d, k.b_const], writes=[b_d])
            p.op("dve", lambda e: e.tensor_tensor(out=E2, in0=tb, in1=k.anti, op=ALU.mult), reads=[b_d, k.b_const], writes=[b_d])
            if cut < 4:
                continue
            pk = p.psum(1)
            p.op("pe", lambda e, pk=pk, ts=ts: e.matmul(pk[:, 0:128], lhsT=kT[:, ts], rhs=kT[:, ts], start=True, stop=True), reads=[b_k], writes=[b_ps[1]])
            p.op("pe", lambda e, pk=pk, ts=ts: e.matmul(pk[:, 128:256], lhsT=kT[:, ts], rhs=qT[:, ts], start=True, stop=True), reads=[b_k, b_q], writes=[b_ps[1]])
            p.op("dve", lambda e, pk=pk, t=t, h=h: e.scalar_tensor_tensor(out=Lm[0], in0=pk[:, 0:128], scalar=beta[:, t, h:h + 1], in1=E2, op0=ALU.mult, op1=ALU.mult),
                 reads=[b_ps[1], b_gt, b_d], writes=[b_L[0]])
            p.op("dve", lambda e, pk=pk: e.tensor_tensor(out=qkT, in0=pk[:, 128:256], in1=E1, op=ALU.mult), reads=[b_ps[1], b_d], writes=[b_qk])
            ptr = p.psum(2)
            p.op("pe", lambda e, ptr=ptr, ts=ts: e.transpose(out=ptr[:, 0:128], in_=vT[:, ts], identity=k.identf), reads=[b_v, k.b_const], writes=[b_ps[2]])
            p.op("pe", lambda e, ptr=ptr, ts=ts: e.transpose(out=ptr[:, 128:256], in_=kT[:, ts], identity=k.identf), reads=[b_k, k.b_const], writes=[b_ps[2]])
            p.op("pe", lambda e, ptr=ptr: e.transpose(out=ptr[:, 256:384], in_=Lm[0], identity=k.identf), reads=[b_L[0], k.b_const], writes=[b_ps[2]])
            p.op("dve", lambda e, ptr=ptr, t=t, h=h: e.tensor_scalar(out=vk[:, 0:128], in0=ptr[:, 0:128], scalar1=beta[:, t, h:h + 1], scalar2=None, op0=ALU.mult), reads=[b_ps[2], b_gt], writes=[b_vk])
            p.op("dve", lambda e, ptr=ptr, t=t, h=h: e.tensor_scalar(out=vk[:, 128:256], in0=ptr[:, 128:256], scalar1=bge[:, t, h:h + 1], scalar2=None, op0=ALU.mult), reads=[b_ps[2], b_gt], writes=[b_vk])
            p.op("dve", lambda e, ptr=ptr, t=t, h=h: e.tensor_scalar(out=kend, in0=ptr[:, 128:256], scalar1=egl[:, t, h:h + 1], scalar2=None, op0=ALU.mult), reads=[b_ps[2], b_gt], writes=[b_kend])
            p.op("act", lambda e, ptr=ptr: e.activation(out=LmT[0], in_=ptr[:, 256:384], func=AF.Copy), reads=[b_ps[2]], writes=[b_LT[0]])
            if cut < 5:
                continue
            p.op("dve", lambda e: e.tensor_tensor(out=TT[0], in0=k.identf, in1=LmT[0], op=ALU.subtract), reads=[b_LT[0], k.b_const], writes=[b_TT[0]])
            ci, ti = 0, 0
            for lev in range(6):
                ni = 1 - ci
                pd = p.psum(3)
                p.op("pe", lambda e, pd=pd, ci=ci: e.matmul(pd[:, 0:128], lhsT=LmT[ci], rhs=Lm[ci], start=True, stop=True), reads=[b_L[ci], b_LT[ci]], writes=[b_ps[3]])
                p.op("pe", lambda e, pd=pd, ci=ci: e.matmul(pd[:, 128:256], lhsT=Lm[ci], rhs=LmT[ci], start=True, stop=True), reads=[b_L[ci], b_LT[ci]], writes=[b_ps[3]])
                p.op("act", lambda e, pd=pd, ni=ni: e.activation(out=Lm[ni], in_=pd[:, 0:128], func=AF.Copy), reads=[b_ps[3]], writes=[b_L[ni]])
                p.op("dve", lambda e, pd=pd, ni=ni: e.tensor_copy(out=LmT[ni], in_=pd[:, 128:256]), reads=[b_ps[3]], writes=[b_LT[ni]])
                pt2 = p.psum(4)
                p.op("pe", lambda e, pt2=pt2, ni=ni, ti=ti: e.matmul(pt2[:, 0:128], lhsT=Lm[ni], rhs=TT[ti], start=True, stop=True), reads=[b_L[ni], b_TT[ti]], writes=[b_ps[4]])
                p.op("dve", lambda e, pt2=pt2, ti=ti: e.tensor_tensor(out=TT[1 - ti], in0=pt2[:, 0:128], in1=TT[ti], op=ALU.add), reads=[b_ps[4], b_TT[ti]], writes=[b_TT[1 - ti]])
                ci, ti = ni, 1 - ti
            if cut < 6:
                continue
            pu = p.psum(5)
            p.op("pe", lambda e, pu=pu, ti=ti: e.matmul(pu[:, 0:256], lhsT=TT[ti], rhs=vk, start=True, stop=True), reads=[b_TT[ti], b_vk], writes=[b_ps[5]])
            p.op("act", lambda e, pu=pu: e.activation(out=uwf, in_=pu[:, 0:256], func=AF.Copy), reads=[b_ps[5]], writes=[b_uw])
            p.op("dve", lambda e, pu=pu: e.tensor_copy(out=uwb, in_=pu[:, 0:256]), reads=[b_ps[5]], writes=[b_uw])
            pa = p.psum(6)
            p.op("pe", lambda e, pa=pa: e.matmul(pa[:, 0:128], lhsT=uwf[:, 128:256], rhs=kend, start=True, stop=True), reads=[b_uw, b_kend], writes=[b_ps[6]])
            p.op("pe", lambda e, pa=pa: e.matmul(pa[:, 128:256], lhsT=kend, rhs=uwf[:, 0:128], start=True, stop=True), reads=[b_uw, b_kend], writes=[b_ps[6]])
            p.op("dve", lambda e, pa=pa, t=t, h=h: e.scalar_tensor_tensor(out=AT, in0=k.identf, scalar=cdec[:, t, h:h + 1], in1=pa[:, 0:128], op0=ALU.mult, op1=ALU.subtract),
                 reads=[b_ps[6], b_gt, k.b_const], writes=[b_AB])
            p.op("act", lambda e, pa=pa: e.activation(out=Bn, in_=pa[:, 128:256], func=AF.Copy), reads=[b_ps[6]], writes=[b_AB])
            if cut < 7:
                continue
            pq = p.psum(7)
            p.op("pe", lambda e, pq=pq: e.matmul(pq[:, 0:128], lhsT=uwf[:, 128:256], rhs=qkT, start=True, stop=True), reads=[b_uw, b_qk], writes=[b_ps[7]])
            p.op("dve", lambda e, ts=ts: e.tensor_tensor(out=t1, in0=qT[:, ts], in1=egrow, op=ALU.mult), reads=[b_q, b_d], writes=[b_d])
            p.op("dve", lambda e, pq=pq: e.tensor_tensor(out=QpT, in0=t1, in1=pq[:, 0:128], op=ALU.subtract), reads=[b_ps[7], b_d], writes=[b_Qp])
            po = p.psum(7)
            p.op("pe", lambda e, po=po, si=si: e.matmul(po[:, 256:384], lhsT=Sf[si], rhs=QpT, start=True, stop=False), reads=[b_S[si], b_Qp], writes=[b_ps[7]])
            p.op("pe", lambda e, po=po: e.matmul(po[:, 256:384], lhsT=uwf[:, 0:128], rhs=qkT, start=False, stop=True), reads=[b_uw, b_qk], writes=[b_ps[7]])
            p.op("act", lambda e, po=po, t=t: e.activation(out=oT[:, (t % 4) * 128:(t % 4 + 1) * 128], in_=po[:, 256:384], func=AF.Copy), reads=[b_ps[7]], writes=[b_oT])
            pS = p.psum(6)
            p.op("pe", lambda e, pS=pS, si=si: e.matmul(pS[:, 256:384], lhsT=AT, rhs=Sf[si], start=True, stop=True), reads=[b_AB, b_S[si]], writes=[b_ps[6]])
            p.op("dve", lambda e, pS=pS, so=so: e.tensor_tensor(out=Sf[so], in0=pS[:, 256:384], in1=Bn, op=ALU.add), reads=[b_ps[6], b_AB], writes=[b_S[so]])
            p.op("act", lambda e, so=so: e.activation(out=Sb[so], in_=Sf[so], func=AF.Copy), reads=[b_S[so]], writes=[b_S[so]])
            if cut < 8:
                continue
            if t % 4 == 3:
                blk = t // 4
                sl = slice(blk * 512, (blk + 1) * 512)
                p.op("act", lambda e, sl=sl: e.activation(out=sgT, in_=gateT[:, sl], func=AF.Silu), reads=[b_gate], writes=[b_sg])
                fm_rmsnorm(k, [oT], 128, 128.0, [ng], [oT], [b_oT, b_par], [b_oT], 0, tmp)
                p.op("dve", lambda e, sl=sl: e.tensor_tensor(out=ystage[:, sl], in0=oT, in1=sgT, op=ALU.mult), reads=[b_oT, b_sg], writes=[b_ys])
        p.dma("pool", ds_y, k.sc["yT"][h * 128:(h + 1) * 128, :], ystage, reads=[b_ys], writes=[k.scb["yT"]])
    p.release()


def stage_nsa(k, l):
    p = k.p
    inp = k.inp
    p.mark()
    b_ps = k.b_ps
    b_par = Buf("npar")
    sc128 = 128.0 ** -0.5
    qT = p.sb([128, 4, S], BF16, "nqT")
    ksT = p.sb([128, S], BF16, "nksT")
    kwT = p.sb([128, S], BF16, "nkwT")
    Vs = p.sb([128, NT, 129], BF16, "nVs")
    Vw = p.sb([128, NT, 129], BF16, "nVw")
    oc = p.sb([128, NT, 4, 128], BF16, "noc")
    imp = p.sb([128, NT, 64], F32, "nimp")
    gts = p.sb([128, NT, 12], F32, "ngts")
    selT = p.sb([64, S], BF16, "nselT")
    kcT = p.sb([128, 256], BF16, "nkcT")
    Vc = p.sb([128, 2, 193], BF16, "nVc")
    b_q, b_ks, b_kw, b_vs, b_vw, b_oc, b_imp, b_gts, b_selT, b_kc, b_vc = [Buf(n) for n in ("nq", "nks", "nkw", "nvs", "nvw", "noc", "nimp", "ngts", "nselT", "nkc", "nvc")]
    qng = p.sb([128, 1], F32, "nqng")
    kng = p.sb([128, 3], F32, "nkng")
    load_cols(k, qng, inp["nsa_q_norm_g"][l].rearrange("(q o) -> q o", o=1), b_par)
    load_cols(k, kng, inp["nsa_k_norm_g"][l].rearrange("j q -> q j"), b_par)
    perm = p.sb([128, 128], BF16, "nperm")
    p.dma("sp", k.ds_misc, perm, inp["c_perm128"], writes=[b_par])
    ds_a = p.dsem("nsaa")
    ds_b = p.dsem("nsab")
    p.dma("sp", ds_a, gts, k.sc["D_gl"].rearrange("(a q) c -> q a c", q=128), reads=[k.scb["D_gl"]], writes=[b_gts])
    p.op("act", lambda e: e.activation(out=gts, in_=gts, func=AF.Sigmoid), reads=[b_gts], writes=[b_gts])
    p.dma("sp", ds_a, Vs[:, :, 0:128], k.sc["D_v"][:, 0:128].rearrange("(a q) c -> q a c", q=128), reads=[k.scb["D_v"]], writes=[b_vs])
    p.dma("sp", ds_a, Vw[:, :, 0:128], k.sc["D_v"][:, 128:256].rearrange("(a q) c -> q a c", q=128), reads=[k.scb["D_v"]], writes=[b_vw])
    p.op("pool", lambda e: e.memset(Vs[:, :, 128:129], 1.0), writes=[b_vs])
    p.op("pool", lambda e: e.memset(Vw[:, :, 128:129], 1.0), writes=[b_vw])
    p.op("pool", lambda e: e.memset(imp, 0.0), writes=[b_imp])
    p.op("pool", lambda e: e.memset(kcT, 0.0), writes=[b_kc])
    p.op("pool", lambda e: e.memset(Vc, 0.0), writes=[b_vc])
    p.mark()
    tmp = dict(sq=p.sb([128, 1, 512], BF16, "nsq"), b_sq=Buf("nsq"), rstd=p.sb([128, 512], F32, "nrstd"), b_rstd=Buf("nrstd"),
               r1=p.sb([128, 512], F32, "nr1"), b_r1=Buf("nr1"), r2=p.sb([128, 512], F32, "nr2"), b_r2=Buf("nr2"), b_tab=Buf("ntab"))
    cosT = p.sb([128, S], BF16, "ncos")
    sinT = p.sb([128, S], BF16, "nsin")
    p.dma("sp", ds_a, cosT, k.sc["rope"][2], reads=[k.scb["rope"]], writes=[tmp["b_tab"]])
    p.dma("sp", ds_a, sinT, k.sc["rope"][3], reads=[k.scb["rope"]], writes=[tmp["b_tab"]])
    qraw = p.sb([128, S], F32, "nqraw")
    b_qraw = Buf("nqraw")
    kraw = p.sb([128, S], BF16, "nkraw")
    b_kraw = Buf("nkraw")
    xtmp = p.sb([128, 512], BF16, "nxtmp")
    b_xtmp = Buf("nxtmp")
    for h in range(4):
        p.dma("sp", ds_b, qraw, k.sc["D_qT"][h * 128:(h + 1) * 128, :], reads=[k.scb["D_qT"]], writes=[b_qraw])
        for blk in range(8):
            sl = slice(blk * 512, (blk + 1) * 512)
            fm_rmsnorm(k, [qraw[:, sl]], 128, 128.0, [qng], [xtmp], [b_qraw, b_par], [b_xtmp], 6, tmp)
            fm_rope(k, xtmp, 128, perm, cosT[:, sl], sinT[:, sl], qT[:, h, sl], [b_xtmp, b_par], [b_q], 7, tmp)
    for which, dst, bd in ((1, ksT, b_ks), (2, kwT, b_kw)):
        r0 = 256 if which == 1 else 384
        p.dma("sp", ds_b, kraw, k.sc["D_kvT"][r0:r0 + 128, :], reads=[k.scb["D_kvT"]], writes=[b_kraw])
        for blk in range(8):
            sl = slice(blk * 512, (blk + 1) * 512)
            fm_rmsnorm(k, [kraw[:, sl]], 128, 128.0, [kng[:, which:which + 1]], [xtmp], [b_kraw, b_par], [b_xtmp], 6, tmp)
            fm_rope(k, xtmp, 128, perm, cosT[:, sl], sinT[:, sl], dst[:, sl], [b_xtmp, b_par], [bd], 7, tmp)
    w1f = p.sb([128, 8, 256], F32, "nw1f")
    w1b = p.sb([128, 32, 256], BF16, "nw1b")
    b_w1f, b_w1 = Buf("nw1f"), Buf("nw1")
    w2f = p.sb([128, 2, 128], F32, "nw2f")
    w2b = p.sb([128, 2, 128], BF16, "nw2b")
    b_w2 = Buf("nw2")
    peTf = p.sb([128, 32], F32, "npeTf")
    peT = p.sb([128, 32], BF16, "npeT")
    b_pe = Buf("npe")
    b1c = p.sb([128, 2], F32, "nb1c")
    b2c = p.sb([128, 1], F32, "nb2c")
    b2r = p.sb([1, 128], F32, "nb2r")
    b2rb = p.sb([1, 128], BF16, "nb2rb")
    biasc = p.sb([128, 2], F32, "nbiasc")
    b_b = Buf("nbias")
    hidT = p.sb([128, 2, 256], BF16, "nhidT")
    b_hid = Buf("nhid")
    kcf = p.sb([128, 256], F32, "nkcf")
    b_kcf = Buf("nkcf")
    xr = p.sb([128, S], BF16, "nxr")
    b_xr = Buf("nxr")
    for j in range(2):
        p.dma("sp", ds_b, xr, k.sc["D_kvT"][j * 128:(j + 1) * 128, :], reads=[k.scb["D_kvT"]], writes=[b_xr])
        for lq in range(4):
            p.dma("sp", ds_a, w1f, inp["nsa_cmp_w1"][l, j, lq * 1024:(lq + 1) * 1024, :].rearrange("(a q) n -> q a n", q=128), writes=[b_w1f])
            copy_op(p, rr_engine(k, ("act", "dve")), w1b[:, lq * 8:(lq + 1) * 8, :], w1f, [b_w1f], [b_w1])
        p.dma("sp", ds_a, w2f, inp["nsa_cmp_w2"][l, j].rearrange("(c q) d -> q c d", q=128), writes=[b_w2])
        p.op("dve", lambda e: e.tensor_copy(out=w2b, in_=w2f), reads=[b_w2], writes=[b_w2])
        load_cols(k, peTf, inp["nsa_cmp_pe"][l, j].rearrange("a d -> d a"), b_pe)
        p.op("dve", lambda e: e.tensor_copy(out=peT, in_=peTf), reads=[b_pe], writes=[b_pe])
        load_cols(k, b1c, inp["nsa_cmp_b1"][l, j].rearrange("(c q) -> q c", q=128), b_b)
        load_cols(k, b2c, inp["nsa_cmp_b2"][l, j].rearrange("(q o) -> q o", o=1), b_b)
        p.dma("sp", ds_a, b2r, inp["nsa_cmp_b2"][l, j].rearrange("(o q) -> o q", o=1), writes=[b_b])
        p.op("dve", lambda e: e.tensor_copy(out=b2rb, in_=b2r), reads=[b_b], writes=[b_b])
        for nc_ in range(2):
            pc_ = p.psum(4)
            for a in range(32):
                p.op("pe", lambda e, pc_=pc_, a=a, nc_=nc_: e.matmul(pc_[:, 0:1], lhsT=w1b[:, a, nc_ * 128:(nc_ + 1) * 128], rhs=peT[:, a:a + 1], start=(a == 0), stop=(a == 31)),
                     reads=[b_w1, b_pe], writes=[b_ps[4]])
            p.op("dve", lambda e, pc_=pc_, nc_=nc_: e.tensor_tensor(out=biasc[:, nc_:nc_ + 1], in0=pc_[:, 0:1], in1=b1c[:, nc_:nc_ + 1], op=ALU.add), reads=[b_ps[4], b_b], writes=[b_b])
            ph = p.psum(5)
            for a in range(32):
                p.op("pe", lambda e, ph=ph, a=a, nc_=nc_: e.matmul(ph[:, 0:255], lhsT=w1b[:, a, nc_ * 128:(nc_ + 1) * 128], rhs=xr[:, a:a + 16 * 254 + 1:16], start=(a == 0), stop=(a == 31)),
                     reads=[b_w1, b_xr], writes=[b_ps[5]])
            p.op("act", lambda e, ph=ph, nc_=nc_: e.activation(out=hidT[:, nc_, 0:255], in_=ph[:, 0:255], func=AF.Silu, bias=biasc[:, nc_:nc_ + 1]), reads=[b_ps[5], b_b], writes=[b_hid])
        if j == 0:
            pk_ = p.psum(4)
            for nc_ in range(2):
                p.op("pe", lambda e, pk_=pk_, nc_=nc_: e.matmul(pk_[:, 0:255], lhsT=w2b[:, nc_, :], rhs=hidT[:, nc_, 0:255], start=(nc_ == 0), stop=(nc_ == 1)), reads=[b_w2, b_hid], writes=[b_ps[4]])
            p.op("act", lambda e, pk_=pk_: e.activation(out=kcf[:, 0:255], in_=pk_[:, 0:255], func=AF.Identity, bias=b2c), reads=[b_ps[4], b_b], writes=[b_kcf])
            fm_rmsnorm(k, [kcf[:, 0:255]], 128, 128.0, [kng[:, 0:1]], [xtmp[:, 0:255]], [b_kcf, b_par], [b_xtmp], 6, tmp)
            fm_rope(k, xtmp[:, 0:255], 128, perm, cosT[:, 31:31 + 16 * 254 + 1:16], sinT[:, 31:31 + 16 * 254 + 1:16], kcT[:, 0:255], [b_xtmp, b_par], [b_kc], 7, tmp)
        else:
            for cc in range(2):
                M = 128 if cc == 0 else 127
                pv_ = p.psum(4)
                for nc_ in range(2):
                    p.op("pe", lambda e, pv_=pv_, nc_=nc_, cc=cc, M=M: e.matmul(pv_[0:M, 0:128], lhsT=hidT[:, nc_, cc * 128:cc * 128 + M], rhs=w2b[:, nc_, :], start=(nc_ == 0), stop=False), reads=[b_w2, b_hid], writes=[b_ps[4]])
                p.op("pe", lambda e, pv_=pv_, M=M: e.matmul(pv_[0:M, 0:128], lhsT=k.onesb[0:1, 0:M], rhs=b2rb, start=False, stop=True), reads=[b_b, k.b_const], writes=[b_ps[4]])
                p.op("act", lambda e, pv_=pv_, cc=cc, M=M: e.activation(out=Vc[0:M, cc, 0:128], in_=pv_[0:M, 0:128], func=AF.Copy), reads=[b_ps[4]], writes=[b_vc])
    p.dma("sp", ds_a, Vc[:, :, 128:193], inp["c_ovl"].rearrange("c q n -> q c n"), writes=[b_vc])
    p.release()
    p.mark()
    cmask = p.sb([128, 2, S], BF16, "ncmask")
    b_cm = Buf("ncm")
    p.dma("sp", ds_a, cmask, inp["c_cmask"].rearrange("c q t -> q c t"), writes=[b_cm])
    Et = [p.sb([128, 512], BF16, "nEt%d" % i) for i in range(2)]
    b_Et = [Buf("nEt%d" % i) for i in range(2)]
    rsc = p.sb([128, 4], F32, "nrsc")
    b_rsc = Buf("nrsc")
    itmp = p.sb([128, 64], F32, "nitmp")
    b_it = Buf("nitmp")
    for qg in range(8):
        ccs = [0] if qg < 4 else [0, 1]
        for h in range(4):
            for cc in ccs:
                M = 128 if cc == 0 else 127
                ps_ = p.psum(cc)
                p.op("pe", lambda e, ps_=ps_, cc=cc, M=M, h=h, qg=qg: e.matmul(ps_[0:M, :], lhsT=kcT[:, cc * 128:cc * 128 + M], rhs=qT[:, h, qg * 512:(qg + 1) * 512], start=True, stop=True), reads=[b_kc, b_q], writes=[b_ps[cc]])
                p.op("act", lambda e, ps_=ps_, cc=cc, M=M: e.activation(out=Et[cc][0:M, :], in_=ps_[0:M, :], func=AF.Exp, scale=sc128), reads=[b_ps[cc]], writes=[b_Et[cc]])
                p.op("dve", lambda e, cc=cc, M=M, qg=qg: e.tensor_tensor(out=Et[cc][0:M, :], in0=Et[cc][0:M, :], in1=cmask[0:M, cc, qg * 512:(qg + 1) * 512], op=ALU.mult), reads=[b_Et[cc], b_cm], writes=[b_Et[cc]])
            for jt in range(4):
                qt = qg * 4 + jt
                bank = 2 + jt % 2
                R = p.psum(bank)[:, (jt // 2) * 193:(jt // 2) * 193 + 193]
                for ci, cc in enumerate(ccs):
                    M = 128 if cc == 0 else 127
                    p.op("pe", lambda e, R=R, cc=cc, M=M, jt=jt, ci=ci, ccs=ccs: e.matmul(R, lhsT=Et[cc][0:M, jt * 128:(jt + 1) * 128], rhs=Vc[0:M, cc, :], start=(ci == 0), stop=(ci == len(ccs) - 1)),
                         reads=[b_Et[cc], b_vc], writes=[b_ps[bank]])
                p.op("dve", lambda e, R=R, jt=jt: e.tensor_scalar(out=rsc[:, jt:jt + 1], in0=R[:, 192:193], scalar1=1e-30, scalar2=None, op0=ALU.max), reads=[b_ps[bank]], writes=[b_rsc])
                p.op("dve", lambda e, jt=jt: e.reciprocal(out=rsc[:, jt:jt + 1], in_=rsc[:, jt:jt + 1]), reads=[b_rsc], writes=[b_rsc])
                p.op("dve", lambda e, R=R, jt=jt, qt=qt: e.scalar_tensor_tensor(out=imp[:, qt, :], in0=R[:, 128:192], scalar=rsc[:, jt:jt + 1], in1=imp[:, qt, :], op0=ALU.mult, op1=ALU.add),
                     reads=[b_ps[bank], b_rsc, b_imp], writes=[b_imp])
                p.op("dve", lambda e, jt=jt, qt=qt, h=h: e.tensor_scalar(out=rsc[:, jt:jt + 1], in0=rsc[:, jt:jt + 1], scalar1=gts[:, qt, h * 3:h * 3 + 1], scalar2=None, op0=ALU.mult), reads=[b_rsc, b_gts], writes=[b_rsc])
                p.op("act", lambda e, R=R, jt=jt, qt=qt, h=h: e.activation(out=oc[:, qt, h, :], in_=R[:, 0:128], func=AF.Copy, scale=rsc[:, jt:jt + 1]), reads=[b_ps[bank], b_rsc], writes=[b_oc])
    p.release()
    p.mark()
    valid = p.sb([128, NT, 64], F32, "nvalid")
    seladd = p.sb([128, NT, 64], F32, "nseladd")
    b_vc2 = Buf("nvalid")
    p.dma("sp", ds_a, valid, inp["c_valid"], writes=[b_vc2])
    p.dma("sp", ds_a, seladd, inp["c_seladd"], writes=[b_vc2])
    m8 = p.sb([128, 8], F32, "nm8")
    wk = p.sb([128, 64], F32, "nwk")
    selm = p.sb([128, 64], BF16, "nselm")
    b_tk = Buf("ntk")
    b_selm = Buf("nselm")
    p.op("dve", lambda e: e.tensor_tensor(out=imp, in0=imp, in1=valid, op=ALU.mult), reads=[b_imp, b_vc2], writes=[b_imp])
    p.op("dve", lambda e: e.tensor_tensor(out=imp, in0=imp, in1=seladd, op=ALU.add), reads=[b_imp, b_vc2], writes=[b_imp])
    for qt in range(NT):
        p.op("dve", lambda e, qt=qt: e.max(out=m8, in_=imp[:, qt, :]), reads=[b_imp], writes=[b_tk])
        p.op("dve", lambda e, qt=qt: e.match_replace(out=wk, in_to_replace=m8, in_values=imp[:, qt, :], imm_value=-2.0), reads=[b_imp, b_tk], writes=[b_tk])
        p.op("dve", lambda e: e.max(out=m8, in_=wk), reads=[b_tk], writes=[b_tk])
        p.op("dve", lambda e, qt=qt: e.tensor_scalar(out=wk, in0=imp[:, qt, :], scalar1=m8[:, 7:8], scalar2=None, op0=ALU.is_ge), reads=[b_imp, b_tk], writes=[b_tk])
        p.op("dve", lambda e, qt=qt: e.tensor_tensor(out=selm, in0=wk, in1=valid[:, qt, :], op=ALU.mult), reads=[b_tk, b_vc2], writes=[b_selm])
        pst = p.psum(6, BF16)
        p.op("pe", lambda e, pst=pst: e.transpose(out=pst[0:64, 0:128], in_=selm, identity=k.ident), reads=[b_selm, k.b_const], writes=[b_ps[6]])
        p.op("act", lambda e, pst=pst, qt=qt: e.activation(out=selT[:, qt * 128:(qt + 1) * 128], in_=pst[0:64, 0:128], func=AF.Copy), reads=[b_ps[6]], writes=[b_selT])
    p.release()
    p.mark()
    expand = p.sb([64, NT, 128], BF16, "nexp")
    b_exp = Buf("nexp")
    p.dma("sp", ds_a, expand, inp["c_expand"], writes=[b_exp])
    sg = p.sb([128, NT, 512], BF16, "nsg")
    b_sg = Buf("nsg")
    p.dma("sp", ds_b, sg, k.sc["D_gate"].rearrange("(a q) c -> q a c", q=128), reads=[k.scb["D_gate"]], writes=[b_sg])
    for a in range(0, NT, 8):
        p.op("act", lambda e, a=a: e.activation(out=sg[:, a:a + 8, :], in_=sg[:, a:a + 8, :], func=AF.Silu), reads=[b_sg], writes=[b_sg])
    acc = p.sb([128, NT, 128], F32, "nacc")
    b_acc = Buf("nacc")
    ystage = p.sb([128, S], BF16, "nys")
    b_ys = Buf("nys")
    ds_y = p.dsem("nsay")
    otm = [p.sb([128, 128], BF16, "notm%d" % i) for i in range(2)]
    b_otm = [Buf("notm%d" % i) for i in range(2)]
    rs = [p.sb([128, 1], F32, "nrs%d" % i) for i in range(2)]
    b_rs = [Buf("nrs%d" % i) for i in range(2)]
    k.attn_cache = None
    for h in range(4):
        def cb_sel(qt, O, bO, h=h):
            i2 = qt % 2
            p.op("dve", lambda e: e.reciprocal(out=rs[i2], in_=O[:, 128:129]), reads=[bO], writes=[b_rs[i2]])
            p.op("dve", lambda e: e.tensor_scalar(out=rs[i2], in0=rs[i2], scalar1=gts[:, qt, h * 3 + 1:h * 3 + 2], scalar2=None, op0=ALU.mult), reads=[b_rs[i2], b_gts], writes=[b_rs[i2]])
            p.op("dve", lambda e: e.scalar_tensor_tensor(out=acc[:, qt, :], in0=O[:, 0:128], scalar=rs[i2], in1=oc[:, qt, h, :], op0=ALU.mult, op1=ALU.add),
                 reads=[bO, b_rs[i2], b_oc], writes=[b_acc])

        def cb_win(qt, O, bO, h=h):
            i2 = qt % 2
            p.op("dve", lambda e: e.reciprocal(out=rs[i2], in_=O[:, 128:129]), reads=[bO], writes=[b_rs[i2]])
            p.op("dve", lambda e: e.tensor_scalar(out=rs[i2], in0=rs[i2], scalar1=gts[:, qt, h * 3 + 2:h * 3 + 3], scalar2=None, op0=ALU.mult), reads=[b_rs[i2], b_gts], writes=[b_rs[i2]])
            p.op("dve", lambda e: e.scalar_tensor_tensor(out=acc[:, qt, :], in0=O[:, 0:128], scalar=rs[i2], in1=acc[:, qt, :], op0=ALU.mult, op1=ALU.add),
                 reads=[bO, b_rs[i2], b_acc], writes=[b_acc])
            p.op("dve", lambda e: e.tensor_tensor(out=otm[i2], in0=acc[:, qt, :], in1=sg[:, qt, h * 128:(h + 1) * 128], op=ALU.mult), reads=[b_acc, b_sg], writes=[b_otm[i2]])
            pst = p.psum(6, BF16)
            p.op("pe", lambda e: e.transpose(out=pst[:, 0:128], in_=otm[i2], identity=k.ident), reads=[b_otm[i2], k.b_const], writes=[b_ps[6]])
            p.op("act", lambda e: e.activation(out=ystage[:, qt * 128:(qt + 1) * 128], in_=pst[:, 0:128], func=AF.Copy), reads=[b_ps[6]], writes=[b_ys])

        attention(k, [qT[:, h, :]], [ksT], Vs, b_q, b_ks, b_vs, sc128,
                  lambda kt, qt: ("diag" if kt == qt else ("full" if kt < qt else None)), cb_sel,
                  extra_mask=lambda kt, qg: (expand[:, kt, :], selT[:, qg * 512:(qg + 1) * 512], [b_exp, b_selT]))
        attention(k, [qT[:, h, :]], [kwT], Vw, b_q, b_kw, b_vw, sc128,
                  lambda kt, qt: ("diag" if kt == qt else ("full" if 0 < qt - kt < 4 else ("anti" if qt - kt == 4 else None))), cb_win)
        p.dma("pool", ds_y, k.sc["yT"][1536 + h * 128:1536 + (h + 1) * 128, :], ystage, reads=[b_ys], writes=[k.scb["yT"]])
    p.release()
    p.release()
```

```python
import contextlib
import numpy as np
import concourse.bass as bass
import concourse.mybir as mybir

F32 = mybir.dt.float32
BF16 = mybir.dt.bfloat16
I32 = mybir.dt.int32
AF = mybir.ActivationFunctionType
ALU = mybir.AluOpType
AX = mybir.AxisListType

SAME_ENG_SYNC = True


class Buf:
    __slots__ = ("name", "w", "r", "excl")

    def __init__(self, name="", excl=False):
        self.name = name
        self.w = None
        self.r = {}
        self.excl = excl


class DSem:
    def __init__(self, prog, name):
        self.prog = prog
        self.name = name
        self.sem = prog.stack.enter_context(prog.nc.semaphore(name))
        self.batch_end = []


class Prog:
    ENGS = ("pe", "act", "dve", "pool", "sp")

    def __init__(self, nc):
        self.nc = nc
        self.stack = contextlib.ExitStack()
        self.ops = {e: [] for e in self.ENGS}
        self.waited = {e: {} for e in self.ENGS}
        self.esem = {}
        for e in self.ENGS:
            self.esem[e] = self.stack.enter_context(nc.semaphore("es_" + e))
        self.sb_off = 0
        self.sb_peak = 0
        self.big = None
        self.bigps = None
        self.sb_marks = []
        self.n_alloc = 0
        self.dsems = []
        self.dsem_cache = {}

    def sb(self, shape, dtype, name=None):
        if self.big is None:
            self.SB_WORDS = 52000
            self.big = self.nc.alloc_sbuf_tensor("bigsb", [128, self.SB_WORDS], F32).ap()
        free = int(np.prod(shape[1:]))
        nbytes = free * mybir.dt.size(dtype)
        nw = (nbytes + 63) // 64 * 16
        off = self.sb_off
        self.sb_off += nw
        self.sb_peak = max(self.sb_peak, self.sb_off)
        assert self.sb_off <= self.SB_WORDS, ("SBUF overflow", self.sb_off * 4, name)
        v = self.big[0:shape[0], off:off + nw]
        if dtype != F32:
            v = v.bitcast(dtype)
        v = v[:, 0:free]
        if len(shape) == 3:
            v = v.rearrange("p (a b) -> p a b", a=shape[1])
        elif len(shape) == 4:
            v = v.rearrange("p (a b c) -> p a b c", a=shape[1], b=shape[2])
        return v

    def psum(self, bank, dtype=F32, nbanks=1):
        if self.bigps is None:
            self.bigps = self.nc.alloc_psum_tensor("bigps", [128, 4096], F32).ap()
        v = self.bigps[:, bank * 512:(bank + nbanks) * 512]
        if dtype != F32:
            v = v.bitcast(dtype)
        return v

    def mark(self):
        self.sb_marks.append(self.sb_off)

    def release(self):
        self.sb_off = self.sb_marks.pop()
        self.barrier()

    def barrier(self):
        toks = []
        for e in self.ENGS:
            for i in range(len(self.ops[e]) - 1, -1, -1):
                o = self.ops[e][i]
                if o["fn"] is not None and o["dma"] is None:
                    toks.append(("eng", e, i))
                    break
        for ds in self.dsems:
            if ds.batch_end:
                toks.append(("dma", ds, len(ds.batch_end) - 1))
        for e in self.ENGS:
            waits = []
            for t in toks:
                w = self._need(e, t, False)
                if w:
                    waits.append(w)
            self.ops[e].append(dict(fn=None, waits=waits, needed=False, dma=None))

    def dsem(self, name):
        if name in self.dsem_cache:
            return self.dsem_cache[name]
        d = DSem(self, "ds_%s_%d" % (name, len(self.dsems)))
        self.dsems.append(d)
        self.dsem_cache[name] = d
        return d

    def _need(self, eng, tok, same_ok):
        if tok is None:
            return None
        if tok[0] == "eng":
            _, e2, idx = tok
            if e2 == eng and not same_ok:
                return None
            key = ("eng", e2)
            if self.waited[eng].get(key, -1) >= idx:
                return None
            self.waited[eng][key] = idx
            self.ops[e2][idx]["needed"] = True
            return tok
        else:
            _, ds, b = tok
            key = ("dma", id(ds))
            if self.waited[eng].get(key, -1) >= b:
                return None
            self.waited[eng][key] = b
            return tok

    def _deps(self, eng, mytok, reads, writes):
        waits = []
        for b in reads:
            w = self._need(eng, b.w, SAME_ENG_SYNC)
            if w:
                waits.append(w)
            if b.excl:
                for t in b.r.values():
                    w = self._need(eng, t, False)
                    if w:
                        waits.append(w)
        for b in writes:
            w = self._need(eng, b.w, SAME_ENG_SYNC)
            if w:
                waits.append(w)
            for t in b.r.values():
                w = self._need(eng, t, SAME_ENG_SYNC)
                if w:
                    waits.append(w)
        for b in reads:
            if mytok[0] == "eng":
                b.r[("eng", mytok[1])] = mytok
            else:
                b.r[("dma", id(mytok[1]))] = mytok
        for b in writes:
            b.w = mytok
            b.r = {}
        return waits

    def op(self, eng, fn, reads=(), writes=()):
        idx = len(self.ops[eng])
        tok = ("eng", eng, idx)
        waits = self._deps(eng, tok, reads, writes)
        self.ops[eng].append(dict(fn=fn, waits=waits, needed=False, dma=None))
        return tok

    def dma(self, eng, ds, out, in_, reads=(), writes=(), batch=False, **kw):
        idx = len(self.ops[eng])
        if getattr(ds, "eng", None) is None:
            ds.eng = eng
        assert ds.eng == eng, ("DSem shared across queues", ds.name, ds.eng, eng)
        if batch and ds.batch_end:
            ds.batch_end[-1] += 16
        else:
            prev = ds.batch_end[-1] if ds.batch_end else 0
            ds.batch_end.append(prev + 16)
        b = len(ds.batch_end) - 1
        tok = ("dma", ds, b)
        waits = []
        if b > 0:
            w = self._need(eng, ("dma", ds, b - 1), False)
            if w:
                waits.append(w)
        if batch:
            self.waited[eng][("dma", id(ds))] = max(self.waited[eng].get(("dma", id(ds)), -1), b)
        waits += self._deps(eng, tok, reads, writes)
        waits = [w for w in waits if not (w[0] == "dma" and w[1] is ds and w[2] == b)]
        if batch:
            self.waited[eng][("dma", id(ds))] = b - 1
        fn = lambda e, out=out, in_=in_, kw=kw: e.dma_start(out=out, in_=in_, **kw)
        self.ops[eng].append(dict(fn=fn, waits=waits, needed=False, dma=ds))
        return tok

    def wait_all(self, eng, bufs):
        waits = []
        for b in bufs:
            w = self._need(eng, b.w, True)
            if w:
                waits.append(w)
        self.ops[eng].append(dict(fn=None, waits=waits, needed=False, dma=None))

    def emit(self):
        nc = self.nc
        EPOCH = 2000
        mil = {}
        self.mil_counts = {}
        esems = {}
        for e in self.ENGS:
            c = 0
            for i, o in enumerate(self.ops[e]):
                if o["needed"]:
                    ep, v = divmod(c, EPOCH)
                    c += 1
                    if (e, ep) not in esems:
                        esems[(e, ep)] = self.stack.enter_context(nc.semaphore("es_%s_%d" % (e, ep)))
                    mil[(e, i)] = (esems[(e, ep)], v + 1)
            self.mil_counts[e] = c

        def run(eng_name, e):
            for i, o in enumerate(self.ops[eng_name]):
                for w in o["waits"]:
                    if w[0] == "eng":
                        sm, v = mil[(w[1], w[2])]
                        e.wait_ge(sm, v)
                    else:
                        e.wait_ge(w[1].sem, w[1].batch_end[w[2]])
                if o["fn"] is None:
                    continue
                ins = o["fn"](e)
                if o["dma"] is not None:
                    ins.then_inc(o["dma"].sem, 16)
                elif o["needed"]:
                    ins.then_inc(mil[(eng_name, i)][0], 1)

        with nc.Block() as block:
            @block.sync
            def _(e):
                run("sp", e)

            @block.tensor
            def _(e):
                run("pe", e)

            @block.scalar
            def _(e):
                run("act", e)

            @block.vector
            def _(e):
                run("dve", e)

            @block.gpsimd
            def _(e):
                run("pool", e)
        self.stack.close()


import ml_dtypes
from concourse.bass_utils import run_bass_kernel_spmd

S = 4096
D = 2048
NT = S // 128
DIN = 6100
L = 2
EPS = 1e-6


class Ctx:
    pass


def dram(k, name, shape, dtype):
    kind = "Internal"
    if name in k.dbg_in:
        kind = "ExternalInput"
    if name in k.dbg_out:
        kind = "ExternalOutput"
    return k.nc.dram_tensor(name, list(shape), dtype, kind=kind).ap()


def rr_engine(k, names=("act", "dve")):
    k.rr = getattr(k, "rr", 0) + 1
    return names[k.rr % len(names)]


def copy_op(p, eng, out, in_, reads, writes):
    if eng == "act":
        p.op("act", lambda e: e.activation(out=out, in_=in_, func=AF.Copy), reads=reads, writes=writes)
    else:
        p.op(eng, lambda e: e.tensor_copy(out=out, in_=in_), reads=reads, writes=writes)


def inproj_segments():
    segs = []
    for i in range(6):
        segs.append((i * 256, 256, "FM", "A_qkvT", i * 256))
    segs.append((1536, 8, "TM", "A_gb", 0))
    for i in range(2):
        segs.append((1544 + i * 256, 256, "FM", "A_gateT", i * 256))
    segs.append((2056, 256, "FM", "B_cT", 0))
    segs.append((2312, 256, "FM", "B_cT", 256))
    segs.append((2568, 192, "FM", "B_cT", 512))
    for i in range(2):
        segs.append((2760 + i * 256, 256, "TM", "B_gate", i * 256))
    for i in range(2):
        segs.append((3272 + i * 256, 256, "FM", "C_xT", i * 256))
    for i in range(2):
        segs.append((3784 + i * 256, 256, "FM", "C_gateT", i * 256))
    for i in range(2):
        segs.append((4296 + i * 256, 256, "FM", "D_qT", i * 256))
    segs.append((4808, 256, "FM", "D_kvT", 0))
    segs.append((5064, 128, "FM", "D_kvT", 256))
    segs.append((5192, 128, "TM", "D_v", 0))
    segs.append((5320, 128, "FM", "D_kvT", 384))
    segs.append((5448, 128, "TM", "D_v", 128))
    segs.append((5576, 12, "TM", "D_gl", 0))
    for i in range(2):
        segs.append((5588 + i * 256, 256, "TM", "D_gate", i * 256))
    return segs


def make_scratch(k):
    sc = {}
    sc["A_qkvT"] = dram(k, "A_qkvT", [1536, S], BF16)
    sc["A_gb"] = dram(k, "A_gb", [S, 8], F32)
    sc["A_gateT"] = dram(k, "A_gateT", [512, S], BF16)
    sc["B_cT"] = dram(k, "B_cT", [704, S], BF16)
    sc["B_gate"] = dram(k, "B_gate", [S, 512], BF16)
    sc["C_xT"] = dram(k, "C_xT", [512, S], BF16)
    sc["C_gateT"] = dram(k, "C_gateT", [512, S], BF16)
    sc["D_qT"] = dram(k, "D_qT", [512, S], F32)
    sc["D_kvT"] = dram(k, "D_kvT", [512, S], BF16)
    sc["D_v"] = dram(k, "D_v", [S, 256], BF16)
    sc["D_gl"] = dram(k, "D_gl", [S, 12], F32)
    sc["D_gate"] = dram(k, "D_gate", [S, 512], BF16)
    sc["yT"] = dram(k, "yT", [2048, S], BF16)
    sc["xs"] = dram(k, "xs", [S, D], F32)
    sc["rope"] = dram(k, "rope", [4, 128, S], BF16)
    k.sc = sc
    k.scb = {n: Buf(n) for n in sc}


def stage_inproj(k, l, x_src, bx_src):
    p = k.p
    inp = k.inp
    p.mark()
    TG = 2048
    gcol = p.sb([128, 16], F32, "gcol")
    b_g = Buf("gcol")
    p.dma("pool", k.ds_miscp, gcol, inp["norm_g"][l].rearrange("(kc q) -> q kc", q=128), writes=[b_g],
          allow_slow_non_contiguous=True)
    hT = p.sb([128, 16, TG], BF16, "hT")
    b_hT = Buf("hT")
    xt = [p.sb([128, D], F32, "xt%d" % i) for i in range(2)]
    b_xt = [Buf("xt%d" % i) for i in range(2)]
    ds_x = [p.dsem("x%d" % i) for i in range(2)]
    hb = [p.sb([128, D], BF16, "hb%d" % i) for i in range(2)]
    b_hb = [Buf("hb%d" % i) for i in range(2)]
    st = [p.sb([128, 1], F32, "ss%d" % i) for i in range(6)]
    b_st = [Buf("st%d" % i) for i in range(2)]
    junk = p.sb([128, D], BF16, "junk")
    b_junk = Buf("junk")
    wst = [p.sb([128, 16, 256], F32, "wst%d" % i) for i in range(2)]
    b_wst = [Buf("wst%d" % i) for i in range(2)]
    ds_w = [p.dsem("w%d" % i) for i in range(2)]
    wb = [p.sb([128, 16, 256], BF16, "wb%d" % i) for i in range(2)]
    b_wb = [Buf("wb%d" % i) for i in range(2)]
    ost = [p.sb([128, TG], F32, "ost%d" % i) for i in range(2)]
    b_ost = [Buf("ost%d" % i) for i in range(2)]
    ds_o = [p.dsem("o%d" % i) for i in range(2)]
    b_ps = k.b_ps
    segs = inproj_segments()
    w_l = inp["w_in"][l].rearrange("(kc q) c -> q kc c", q=128)
    ocount = 0
    pcount = 0
    for tg in range(S // TG):
        for ti in range(TG // 128):
            t0 = tg * TG + ti * 128
            i2 = ti % 2
            p.dma("sp", ds_x[i2], xt[i2], x_src[t0:t0 + 128, :], reads=[bx_src], writes=[b_xt[i2]])
            ss, sq, rs = st[i2 * 3], st[i2 * 3 + 1], st[i2 * 3 + 2]
            p.op("act", lambda e, i2=i2, ss=ss: e.activation(out=junk, in_=xt[i2], func=AF.Square, accum_out=ss),
                 reads=[b_xt[i2]], writes=[b_junk, b_st[i2]])
            p.op("act", lambda e, ss=ss, sq=sq: e.activation(out=sq, in_=ss, func=AF.Sqrt, scale=1.0 / D, bias=k.eps_col),
                 reads=[b_st[i2]], writes=[b_st[i2]])
            p.op("dve", lambda e, sq=sq, rs=rs: e.reciprocal(out=rs, in_=sq), reads=[b_st[i2]], writes=[b_st[i2]])
            p.op("dve", lambda e, i2=i2, rs=rs: e.tensor_scalar(out=hb[i2], in0=xt[i2], scalar1=rs, scalar2=None, op0=ALU.mult),
                 reads=[b_xt[i2], b_st[i2]], writes=[b_hb[i2]])
            pst = p.psum(6, BF16, 2)
            for kc in range(16):
                p.op("pe", lambda e, kc=kc, i2=i2, pst=pst: e.transpose(out=pst[:, kc * 128:(kc + 1) * 128], in_=hb[i2][:, kc * 128:(kc + 1) * 128], identity=k.ident),
                     reads=[b_hb[i2], k.b_const], writes=[b_ps[6]])
            p.op("act", lambda e, ti=ti, pst=pst: e.activation(out=hT[:, :, ti * 128:(ti + 1) * 128], in_=pst.rearrange("q (a b) -> q a b", a=16), func=AF.Copy),
                 reads=[b_ps[6]], writes=[b_hT])
        for si, (c0, n, mode, dest, doff) in enumerate(segs):
            wi = si % 2
            p.dma("sp", ds_w[wi], wst[wi][:, :, 0:n], w_l[:, :, c0:c0 + n], writes=[b_wst[wi]])
            for kc in range(16):
                if kc % 2 == 0:
                    p.op("act", lambda e, wi=wi, kc=kc, n=n: e.activation(out=wb[wi][:, kc, 0:n], in_=wst[wi][:, kc, 0:n], func=AF.Copy, scale=gcol[:, kc:kc + 1]),
                         reads=[b_wst[wi], b_g], writes=[b_wb[wi]])
                else:
                    p.op("dve", lambda e, wi=wi, kc=kc, n=n: e.tensor_scalar(out=wb[wi][:, kc, 0:n], in0=wst[wi][:, kc, 0:n], scalar1=gcol[:, kc:kc + 1], scalar2=None, op0=ALU.mult),
                         reads=[b_wst[wi], b_g], writes=[b_wb[wi]])
            dst = k.sc[dest]
            bdst = k.scb[dest]
            ddt = dst.dtype
            if mode == "FM":
                for cb in range(0, n, 128):
                    nb = min(128, n - cb)
                    oi = ocount % 2
                    ocount += 1
                    ov = ost[oi] if ddt == F32 else ost[oi].bitcast(BF16)[:, 0:TG]
                    for tq in range(TG // 512):
                        bank = pcount % 4
                        pcount += 1
                        ps = p.psum(bank)
                        for kc in range(16):
                            p.op("pe", lambda e, ps=ps, wi=wi, kc=kc, cb=cb, nb=nb, tq=tq: e.matmul(ps[0:nb, :], lhsT=wb[wi][:, kc, cb:cb + nb], rhs=hT[:, kc, tq * 512:(tq + 1) * 512], start=(kc == 0), stop=(kc == 15)),
                                 reads=[b_wb[wi], b_hT], writes=[b_ps[bank]])
                        copy_op(p, rr_engine(k), ov[0:nb, tq * 512:(tq + 1) * 512], ps[0:nb, :], [b_ps[bank]], [b_ost[oi]])
                    p.dma("pool", ds_o[oi], dst[doff + cb:doff + cb + nb, tg * TG:(tg + 1) * TG], ov[0:nb, :], reads=[b_ost[oi]], writes=[bdst])
            else:
                oi = ocount % 2
                ocount += 1
                ovf = ost[oi] if ddt == F32 else ost[oi].bitcast(BF16)
                ov = ovf[:, 0:16 * n].rearrange("q (a b) -> q a b", a=16)
                for ti in range(TG // 128):
                    bank = pcount % 4
                    pcount += 1
                    ps = p.psum(bank)
                    for kc in range(16):
                        p.op("pe", lambda e, ps=ps, wi=wi, kc=kc, n=n, ti=ti: e.matmul(ps[:, 0:n], lhsT=hT[:, kc, ti * 128:(ti + 1) * 128], rhs=wb[wi][:, kc, 0:n], start=(kc == 0), stop=(kc == 15)),
                             reads=[b_wb[wi], b_hT], writes=[b_ps[bank]])
                    copy_op(p, rr_engine(k), ov[:, ti, :], ps[:, 0:n], [b_ps[bank]], [b_ost[oi]])
                p.dma("pool", ds_o[oi], dst[tg * TG:(tg + 1) * TG, doff:doff + n].rearrange("(a q) c -> q a c", q=128), ov, reads=[b_ost[oi]], writes=[bdst],
                      allow_slow_non_contiguous=(n < 64))
    p.release()


def stage_outproj(k, l, x_src, bx_src, x_dst, bx_dst):
    p = k.p
    inp = k.inp
    p.mark()
    wo = p.sb([128, 16, D], BF16, "wo")
    b_wo = Buf("wo")
    wst = [p.sb([128, 4, 512], F32, "wost%d" % i) for i in range(2)]
    b_wst = [Buf("wost%d" % i) for i in range(2)]
    ds_w = [p.dsem("wo%d" % i) for i in range(2)]
    w_l = inp["w_out"][l].rearrange("(kc q) c -> q kc c", q=128)
    cnt = 0
    for kc4 in range(4):
        for cq in range(4):
            wi = cnt % 2
            cnt += 1
            p.dma("sp", ds_w[wi], wst[wi], w_l[:, kc4 * 4:(kc4 + 1) * 4, cq * 512:(cq + 1) * 512], writes=[b_wst[wi]])
            copy_op(p, rr_engine(k, ("act", "dve")), wo[:, kc4 * 4:(kc4 + 1) * 4, cq * 512:(cq + 1) * 512], wst[wi], [b_wst[wi]], [b_wo])
    yt = [p.sb([128, 16, 512], BF16, "yt%d" % i) for i in range(2)]
    b_yt = [Buf("yt%d" % i) for i in range(2)]
    ds_y = [p.dsem("y%d" % i) for i in range(2)]
    xt = [p.sb([128, D], F32, "oxt%d" % i) for i in range(2)]
    b_xt = [Buf("oxt%d" % i) for i in range(2)]
    ds_x = [p.dsem("ox%d" % i) for i in range(2)]
    xo = [p.sb([128, D], F32, "oxo%d" % i) for i in range(2)]
    b_xo = [Buf("oxo%d" % i) for i in range(2)]
    ds_xo = [p.dsem("oxo%d" % i) for i in range(2)]
    yT = k.sc["yT"].rearrange("(ec q) t -> q ec t", q=128)
    b_ps = k.b_ps
    pcount = 0
    for tq in range(S // 512):
        yi = tq % 2
        p.dma("sp", ds_y[yi], yt[yi], yT[:, :, tq * 512:(tq + 1) * 512], reads=[k.scb["yT"]], writes=[b_yt[yi]])
        for tj in range(4):
            ti = tq * 4 + tj
            xi = ti % 2
            p.dma("sp", ds_x[xi], xt[xi], x_src[ti * 128:(ti + 1) * 128, :], reads=[bx_src], writes=[b_xt[xi]])
            for dq in range(4):
                bank = pcount % 4
                pcount += 1
                ps = p.psum(bank)
                for ec in range(16):
                    p.op("pe", lambda e, ps=ps, yi=yi, ec=ec, tj=tj, dq=dq: e.matmul(ps, lhsT=yt[yi][:, ec, tj * 128:(tj + 1) * 128], rhs=wo[:, ec, dq * 512:(dq + 1) * 512], start=(ec == 0), stop=(ec == 15)),
                         reads=[b_yt[yi], b_wo], writes=[b_ps[bank]])
                p.op("dve", lambda e, ps=ps, xi=xi, dq=dq: e.tensor_tensor(out=xo[xi][:, dq * 512:(dq + 1) * 512], in0=ps, in1=xt[xi][:, dq * 512:(dq + 1) * 512], op=ALU.add),
                     reads=[b_ps[bank], b_xt[xi]], writes=[b_xo[xi]])
            p.dma("pool", ds_xo[xi], x_dst[ti * 128:(ti + 1) * 128, :], xo[xi], reads=[b_xo[xi]], writes=[bx_dst])
    p.release()


INPUT_SPECS = [
    ("x", [S, D], F32), ("positions", [S], I32), ("norm_g", [L, D], F32), ("w_in", [L, D, DIN], F32),
    ("w_out", [L, D, D], F32), ("gdn_conv_w", [L, 4, 1536], F32), ("gdn_a_log", [L, 4], F32),
    ("gdn_dt_bias", [L, 4], F32), ("gdn_norm_g", [L, 128], F32), ("mla_q_norm_g", [L, 384], F32),
    ("mla_w_uq", [L, 384, 4, 192], F32), ("mla_kv_norm_g", [L, 256], F32), ("mla_w_ukv", [L, 256, 4, 256], F32),
    ("mla_qk_norm_g", [L, 2, 192], F32), ("lru_conv_w", [L, 4, 512], F32), ("lru_conv_b", [L, 512], F32),
    ("lru_w_r", [L, 4, 128, 128], F32), ("lru_b_r", [L, 512], F32), ("lru_w_i", [L, 4, 128, 128], F32),
    ("lru_b_i", [L, 512], F32), ("lru_lambda", [L, 512], F32), ("nsa_q_norm_g", [L, 128], F32),
    ("nsa_k_norm_g", [L, 3, 128], F32), ("nsa_cmp_pe", [L, 2, 32, 128], F32), ("nsa_cmp_w1", [L, 2, 4096, 256], F32),
    ("nsa_cmp_b1", [L, 2, 256], F32), ("nsa_cmp_w2", [L, 2, 256, 128], F32), ("nsa_cmp_b2", [L, 2, 128], F32),
]


def host_consts():
    c = {}
    c["c_ident"] = np.eye(128, dtype=np.float32).astype(ml_dtypes.bfloat16)
    c["c_identf"] = np.eye(128, dtype=np.float32)
    pidx = np.arange(128)
    invf = np.zeros((128, 2), np.float32)
    invf[:, 0] = 10000.0 ** (-(2.0 * (pidx % 32)) / 64.0)
    invf[:, 1] = 10000.0 ** (-(2.0 * (pidx % 64)) / 128.0)
    c["c_invf"] = invf
    for P_, nm in ((64, "c_perm64"), (128, "c_perm128")):
        h = P_ // 2
        m = np.zeros((P_, P_), np.float32)
        for i in range(P_):
            if i < h:
                m[i + h, i] = -1.0
            else:
                m[i - h, i] = 1.0
        c[nm] = m.astype(ml_dtypes.bfloat16)
    jj, ii = np.meshgrid(np.arange(128), np.arange(128), indexing="ij")
    c["c_tri"] = (jj <= ii).astype(np.float32).astype(ml_dtypes.bfloat16)
    c["c_anti"] = (jj > ii).astype(np.float32).astype(ml_dtypes.bfloat16)
    c["c_trif"] = (jj <= ii).astype(np.float32)
    cc = np.arange(256)
    tt = np.arange(S)
    cm = ((16 * cc[:, None] + 31) <= tt[None, :]) & (cc[:, None] < 255)
    c["c_cmask"] = cm.reshape(2, 128, S).astype(np.float32).astype(ml_dtypes.bfloat16)
    nn = np.arange(64)
    ovl = ((16 * cc[:, None]) < (64 * nn[None, :] + 64)) & ((16 * cc[:, None] + 31) >= 64 * nn[None, :]) & (cc[:, None] < 255)
    vx = np.zeros((2, 128, 65), np.float32)
    vx[:, :, 0:64] = ovl.reshape(2, 128, 64)
    vx[:, :, 64] = 1.0
    vx[1, 127, :] = 0.0
    c["c_ovl"] = vx.astype(ml_dtypes.bfloat16)
    valid = (64 * nn[None, :] <= tt[:, None])
    cur = tt // 64
    forced = (nn[None, :] == 0) | (nn[None, :] == cur[:, None]) | (nn[None, :] == cur[:, None] - 1)
    c["c_valid"] = valid.astype(np.float32).reshape(NT, 128, 64).transpose(1, 0, 2).copy()
    c["c_seladd"] = ((valid.astype(np.float32) - 1.0) + (forced & valid).astype(np.float32) * 1e9).reshape(NT, 128, 64).transpose(1, 0, 2).copy()
    ex = np.zeros((64, NT, 128), np.float32)
    for kt in range(NT):
        ex[2 * kt, kt, 0:64] = 1.0
        ex[2 * kt + 1, kt, 64:128] = 1.0
    c["c_expand"] = ex.astype(ml_dtypes.bfloat16)
    return c


CONST_SPECS = [("c_ident", [128, 128], BF16), ("c_identf", [128, 128], F32), ("c_invf", [128, 2], F32),
               ("c_perm64", [64, 64], BF16), ("c_perm128", [128, 128], BF16), ("c_tri", [128, 128], BF16),
               ("c_anti", [128, 128], BF16), ("c_trif", [128, 128], F32), ("c_cmask", [2, 128, S], BF16),
               ("c_ovl", [2, 128, 65], BF16), ("c_valid", [128, NT, 64], F32), ("c_seladd", [128, NT, 64], F32),
               ("c_expand", [64, NT, 128], BF16)]


class LazyInputs(dict):
    def __init__(self, nc):
        super().__init__()
        self.nc = nc
        self.specs = {n: (sh, d) for n, sh, d in INPUT_SPECS + CONST_SPECS}

    def __missing__(self, name):
        sh, d = self.specs[name]
        ap = self.nc.dram_tensor(name, list(sh), d, kind="ExternalInput").ap()
        self[name] = ap
        return ap


def build(dbg_in=(), dbg_out=(), stages=("inproj", "A", "B", "C", "D", "outproj"), layers=(0, 1), **dbg_attrs):
    nc = bass.Bass("TRN2", target_bir_lowering=False)
    k = Ctx()
    k.nc = nc
    k.dbg_in = set(dbg_in)
    k.dbg_out = set(dbg_out)
    for kk_, vv_ in dbg_attrs.items():
        setattr(k, kk_, vv_)
    k.inp = LazyInputs(nc)
    k.out = nc.dram_tensor("out", [S, D], F32, kind="ExternalOutput").ap()
    b_out = Buf("out")
    p = k.p = Prog(nc)
    k.b_ps = [Buf("ps%d" % i, excl=True) for i in range(8)]
    k.b_const = Buf("const")
    k.ds_misc = p.dsem("misc")
    k.ds_miscp = p.dsem("miscp")
    make_scratch(k)
    k.ident = p.sb([128, 128], BF16, "ident")
    k.identf = p.sb([128, 128], F32, "identf")
    k.eps_col = p.sb([128, 1], F32, "eps")
    p.dma("sp", k.ds_misc, k.ident, k.inp["c_ident"], writes=[k.b_const])
    p.dma("sp", k.ds_misc, k.identf, k.inp["c_identf"], writes=[k.b_const])
    p.op("pool", lambda e: e.memset(k.eps_col, EPS), writes=[k.b_const])
    k.one_col = p.sb([128, 1], F32, "one")
    p.op("pool", lambda e: e.memset(k.one_col, 1.0), writes=[k.b_const])
    k.onesb = p.sb([128, 128], BF16, "onesb")
    k.zerosb = p.sb([128, 512], BF16, "zerosb")
    p.op("pool", lambda e: e.memset(k.onesb, 1.0), writes=[k.b_const])
    p.op("pool", lambda e: e.memset(k.zerosb, 0.0), writes=[k.b_const])
    k.tri = p.sb([128, 128], BF16, "tri")
    k.anti = p.sb([128, 128], BF16, "anti")
    p.dma("sp", k.ds_misc, k.tri, k.inp["c_tri"], writes=[k.b_const])
    p.dma("sp", k.ds_misc, k.anti, k.inp["c_anti"], writes=[k.b_const])
    if "B" in stages or "D" in stages or "rope" in stages:
        stage_rope(k)
    b_x = Buf("x_in")
    x_cur, bx_cur = k.inp["x"], b_x
    for l in layers:
        if "inproj" in stages:
            stage_inproj(k, l, x_cur, bx_cur)
        if "A" in stages:
            stage_gdn(k, l)
        if "B" in stages:
            stage_mla(k, l)
        if "C" in stages:
            stage_lru(k, l)
        if "D" in stages:
            stage_nsa(k, l)
        if "outproj" in stages:
            if l == layers[-1]:
                x_dst, bx_dst = k.out, b_out
            else:
                x_dst, bx_dst = k.sc["xs"], k.scb["xs"]
            stage_outproj(k, l, x_cur, bx_cur, x_dst, bx_dst)
            x_cur, bx_cur = x_dst, bx_dst
    outs = [b_out] + [k.scb[n] for n in k.dbg_out if n in k.scb]
    for eng in ("sp", "pool", "act"):
        p.wait_all(eng, outs)
    p.emit()
    return nc, k


def make_in_maps(inputs, ncores=8, used=None):
    consts = host_consts()
    maps = []
    for c in range(ncores):
        b = c % 4
        m = {}
        for name, shape, dt_ in INPUT_SPECS:
            if used is not None and name not in used:
                continue
            a = np.asarray(inputs[name])
            if name in ("x", "positions"):
                a = a[b]
            m[name] = np.ascontiguousarray(a)
        for cn, cv in consts.items():
            if used is None or cn in used:
                m[cn] = cv
        maps.append(m)
    return maps


_CACHE = {}


def kernel(**inputs):
    if "nc" not in _CACHE:
        _CACHE["nc"], _CACHE["k"] = build()
    nc = _CACHE["nc"]
    maps = make_in_maps(inputs, used=set(_CACHE["k"].inp.keys()))
    res = run_bass_kernel_spmd(nc, maps, core_ids=list(range(8)))
    out = np.stack([np.asarray(res.results[b]["out"]) for b in range(4)], axis=0)
    return out.astype(np.float32)


def load_cols(k, dst, src_ap, b_dst, eng="pool"):
    k.p.dma(eng, k.ds_miscp if eng == "pool" else k.ds_misc, dst, src_ap, writes=[b_dst], allow_slow_non_contiguous=True)


def stage_lru(k, l):
    p = k.p
    inp = k.inp
    p.mark()
    b_par = Buf("lru_par")
    cw = p.sb([128, 4, 4], F32, "lcw")
    for g in range(4):
        load_cols(k, cw[:, g, :], inp["lru_conv_w"][l][:, g * 128:(g + 1) * 128].rearrange("j q -> q j"), b_par)
    cols = {}
    for nm in ("lru_conv_b", "lru_b_r", "lru_b_i", "lru_lambda"):
        cols[nm] = p.sb([128, 4], F32, nm)
        load_cols(k, cols[nm], inp[nm][l].rearrange("(g q) -> q g", q=128), b_par)
    sp = p.sb([128, 4], F32, "lsp")
    ccol = p.sb([128, 4], F32, "lcc")
    p.op("act", lambda e: e.activation(out=sp, in_=cols["lru_lambda"], func=AF.Exp, scale=-1.0), reads=[b_par], writes=[b_par])
    p.op("act", lambda e: e.activation(out=sp, in_=sp, func=AF.Ln, bias=k.one_col, scale=1.0), reads=[b_par], writes=[b_par])
    p.op("dve", lambda e: e.tensor_scalar(out=ccol, in0=sp, scalar1=-8.0, scalar2=None, op0=ALU.mult), reads=[b_par], writes=[b_par])
    wf = p.sb([128, 2, 128], F32, "lwf")
    b_wf = Buf("lwf")
    wbf = p.sb([128, 2, 128], BF16, "lwb")
    b_wbf = Buf("lwb")
    ds_w = p.dsem("lruw")
    xin = p.sb([128, 3 + S], BF16, "lxin")
    b_xin = Buf("lxin")
    ds_x = p.dsem("lrux")
    gin = p.sb([128, S], BF16, "lgin")
    b_gin = Buf("lgin")
    ds_g = p.dsem("lrug")
    xc = p.sb([128, S], F32, "lxc")
    b_xc = Buf("lxc")
    xcb = p.sb([128, S], BF16, "lxcb")
    b_xcb = Buf("lxcb")
    ra = p.sb([128, S], F32, "lra")
    b_ra = Buf("lra")
    ig = p.sb([128, S], F32, "lig")
    b_ig = Buf("lig")
    bb = p.sb([128, S], F32, "lbb")
    b_bb = Buf("lbb")
    hh = p.sb([128, S], F32, "lhh")
    b_hh = Buf("lhh")
    yb = p.sb([128, S], BF16, "lyb")
    b_yb = Buf("lyb")
    ds_y = p.dsem("lruy")
    b_ps = k.b_ps
    p.op("pool", lambda e: e.memset(xin[:, 0:3], 0.0), writes=[b_xin])
    pc = 0
    for g in range(4):
        p.dma("sp", ds_x, xin[:, 3:3 + S], k.sc["C_xT"][g * 128:(g + 1) * 128, :], reads=[k.scb["C_xT"]], writes=[b_xin])
        p.dma("sp", ds_g, gin, k.sc["C_gateT"][g * 128:(g + 1) * 128, :], reads=[k.scb["C_gateT"]], writes=[b_gin])
        p.dma("sp", ds_w, wf[:, 0, :], inp["lru_w_r"][l, g], writes=[b_wf])
        p.dma("sp", ds_w, wf[:, 1, :], inp["lru_w_i"][l, g], writes=[b_wf], batch=True)
        p.op("dve", lambda e: e.tensor_copy(out=wbf, in_=wf), reads=[b_wf], writes=[b_wbf])
        cut = 99
        if cut >= 2:
          p.op("dve", lambda e, g=g: e.tensor_scalar(out=xc, in0=xin[:, 3:3 + S], scalar1=cw[:, g, 3:4], scalar2=cols["lru_conv_b"][:, g:g + 1], op0=ALU.mult, op1=ALU.add),
             reads=[b_xin, b_par], writes=[b_xc])
        for j in range(3 if cut >= 2 else 0):
            p.op("dve", lambda e, g=g, j=j: e.scalar_tensor_tensor(out=xc, in0=xin[:, j:j + S], scalar=cw[:, g, j:j + 1], in1=xc, op0=ALU.mult, op1=ALU.add),
                 reads=[b_xin, b_par, b_xc], writes=[b_xc])
        p.op("act", lambda e: e.activation(out=xcb, in_=xc, func=AF.Copy), reads=[b_xc], writes=[b_xcb])
        for tq in range(S // 512 if cut >= 3 else 0):
            for which, dstt, bd, bias in ((0, ra, b_ra, "lru_b_r"), (1, ig, b_ig, "lru_b_i")):
                bank = pc % 4
                pc += 1
                ps = p.psum(bank)
                p.op("pe", lambda e, ps=ps, which=which, tq=tq: e.matmul(ps, lhsT=wbf[:, which, :], rhs=xcb[:, tq * 512:(tq + 1) * 512], start=True, stop=True),
                     reads=[b_wbf, b_xcb], writes=[b_ps[bank]])
                p.op("act", lambda e, ps=ps, dstt=dstt, tq=tq, bias=bias, g=g: e.activation(out=dstt[:, tq * 512:(tq + 1) * 512], in_=ps, func=AF.Sigmoid, bias=cols[bias][:, g:g + 1]),
                     reads=[b_ps[bank], b_par], writes=[bd])
        if cut < 4:
            p.dma("pool", ds_y, k.sc["yT"][1024 + g * 128:1024 + (g + 1) * 128, :], yb, reads=[b_yb], writes=[k.scb["yT"]])
            continue
        p.op("act", lambda e, g=g: e.activation(out=ra, in_=ra, func=AF.Exp, scale=ccol[:, g:g + 1]), reads=[b_ra, b_par], writes=[b_ra])
        p.op("dve", lambda e: e.tensor_tensor(out=bb, in0=ra, in1=ra, op=ALU.mult), reads=[b_ra], writes=[b_bb])
        p.op("act", lambda e: e.activation(out=bb, in_=bb, func=AF.Sqrt, scale=-1.0, bias=k.one_col), reads=[b_bb], writes=[b_bb])
        p.op("pool", lambda e: e.tensor_tensor(out=ig, in0=ig, in1=xc, op=ALU.mult), reads=[b_ig, b_xc], writes=[b_ig])
        p.op("dve", lambda e: e.tensor_tensor(out=bb, in0=bb, in1=ig, op=ALU.mult), reads=[b_bb, b_ig], writes=[b_bb])
        if cut >= 5:
            p.op("dve", lambda e: e.tensor_tensor_scan(out=hh, data0=ra, data1=bb, initial=0.0, op0=ALU.mult, op1=ALU.add), reads=[b_ra, b_bb], writes=[b_hh])
        p.op("act", lambda e: e.activation(out=ig, in_=gin, func=AF.Silu), reads=[b_gin], writes=[b_ig])
        p.op("dve", lambda e: e.tensor_tensor(out=yb, in0=hh, in1=ig, op=ALU.mult), reads=[b_hh, b_ig], writes=[b_yb])
        p.dma("pool", ds_y, k.sc["yT"][1024 + g * 128:1024 + (g + 1) * 128, :], yb, reads=[b_yb], writes=[k.scb["yT"]])
    p.release()


TWO_PI = 6.283185307179586


def stage_rope(k):
    p = k.p
    inp = k.inp
    p.mark()
    posi = p.sb([128, S], I32, "posi")
    b_pos = Buf("posi")
    p.dma("sp", k.ds_misc, posi, inp["positions"].partition_broadcast(128), writes=[b_pos])
    posf = p.sb([128, S], F32, "posf")
    b_posf = Buf("posf")
    p.op("dve", lambda e: e.tensor_copy(out=posf, in_=posi), reads=[b_pos], writes=[b_posf])
    invf = p.sb([128, 2], F32, "invf")
    b_invf = Buf("invf")
    p.dma("sp", k.ds_misc, invf, inp["c_invf"], writes=[b_invf])
    ang = p.sb([128, S], F32, "ang")
    qi = p.sb([128, S], I32, "qi")
    qf = p.sb([128, S], F32, "qf")
    r = p.sb([128, S], F32, "r")
    ob = p.sb([128, S], BF16, "ropeo")
    b_t = Buf("ropetmp")
    b_ob = Buf("ropeo")
    ds = p.dsem("ropeo")
    negpi = p.sb([128, 1], F32, "negpi")
    p.op("pool", lambda e: e.memset(negpi, -3.141592653589793), writes=[b_t])
    for tab in range(2):
        for cs in range(2):
            shift = 1.5707963267948966 if cs == 0 else 0.0
            p.op("dve", lambda e, tab=tab, shift=shift: e.tensor_scalar(out=ang, in0=posf, scalar1=invf[:, tab:tab + 1], scalar2=shift, op0=ALU.mult, op1=ALU.add),
                 reads=[b_posf, b_invf], writes=[b_t])
            p.op("dve", lambda e: e.tensor_scalar(out=qf, in0=ang, scalar1=1.0 / TWO_PI, scalar2=None, op0=ALU.mult), reads=[b_t], writes=[b_t])
            p.op("dve", lambda e: e.tensor_copy(out=qi, in_=qf), reads=[b_t], writes=[b_t])
            p.op("dve", lambda e: e.tensor_copy(out=qf, in_=qi), reads=[b_t], writes=[b_t])
            p.op("dve", lambda e: e.scalar_tensor_tensor(out=r, in0=qf, scalar=-TWO_PI, in1=ang, op0=ALU.mult, op1=ALU.add), reads=[b_t], writes=[b_t])
            p.op("dve", lambda e: e.tensor_scalar(out=qf, in0=r, scalar1=0.0, scalar2=TWO_PI, op0=ALU.is_lt, op1=ALU.mult), reads=[b_t], writes=[b_t])
            p.op("dve", lambda e: e.tensor_tensor(out=r, in0=r, in1=qf, op=ALU.add), reads=[b_t], writes=[b_t])
            p.op("dve", lambda e: e.tensor_scalar(out=qf, in0=r, scalar1=TWO_PI, scalar2=-TWO_PI, op0=ALU.is_ge, op1=ALU.mult), reads=[b_t], writes=[b_t])
            p.op("dve", lambda e: e.tensor_tensor(out=r, in0=r, in1=qf, op=ALU.add), reads=[b_t], writes=[b_t])
            p.op("dve", lambda e: e.tensor_scalar(out=r, in0=r, scalar1=0.0, scalar2=TWO_PI, op0=ALU.max, op1=ALU.min), reads=[b_t], writes=[b_t])
            p.op("act", lambda e: e.activation(out=qf, in_=r, func=AF.Sin, bias=negpi, scale=1.0), reads=[b_t], writes=[b_t])
            p.op("dve", lambda e: e.tensor_scalar(out=ob, in0=qf, scalar1=-1.0, scalar2=None, op0=ALU.mult), reads=[b_t], writes=[b_ob])
            p.dma("sp", ds, k.sc["rope"][tab * 2 + cs], ob, reads=[b_ob], writes=[k.scb["rope"]])
    p.release()


def fm_rmsnorm(k, srcs, P, nfeat, gcol, out, reads, writes, bank, tmp):
    p = k.p
    N = srcs[0].shape[-1]
    ps = p.psum(bank)
    b_ps = k.b_ps[bank]
    for ci, src in enumerate(srcs):
        p.op("act", lambda e, src=src, ci=ci: e.activation(out=tmp["sq"][0:P, ci, 0:N], in_=src, func=AF.Square), reads=reads, writes=[tmp["b_sq"]])
    for ci in range(len(srcs)):
        p.op("pe", lambda e, ci=ci: e.matmul(ps[0:P, 0:N], lhsT=k.onesb[0:P, 0:P], rhs=tmp["sq"][0:P, ci, 0:N], start=(ci == 0), stop=(ci == len(srcs) - 1)),
             reads=[tmp["b_sq"], k.b_const], writes=[b_ps])
    p.op("act", lambda e: e.activation(out=tmp["rstd"][0:P, 0:N], in_=ps[0:P, 0:N], func=AF.Sqrt, scale=1.0 / nfeat, bias=k.eps_col[0:P]), reads=[b_ps], writes=[tmp["b_rstd"]])
    p.op("dve", lambda e: e.reciprocal(out=tmp["rstd"][0:P, 0:N], in_=tmp["rstd"][0:P, 0:N]), reads=[tmp["b_rstd"]], writes=[tmp["b_rstd"]])
    for ci, src in enumerate(srcs):
        if gcol is not None:
            p.op("dve", lambda e, src=src, ci=ci: e.scalar_tensor_tensor(out=out[ci], in0=src, scalar=gcol[ci], in1=tmp["rstd"][0:P, 0:N], op0=ALU.mult, op1=ALU.mult),
                 reads=list(reads) + [tmp["b_rstd"]], writes=writes)
        else:
            p.op("dve", lambda e, src=src, ci=ci: e.tensor_tensor(out=out[ci], in0=src, in1=tmp["rstd"][0:P, 0:N], op=ALU.mult),
                 reads=list(reads) + [tmp["b_rstd"]], writes=writes)


def fm_rope(k, x, P, perm, cos, sin, out, reads, writes, bank, tmp):
    p = k.p
    N = x.shape[-1]
    ps = p.psum(bank)
    b_ps = k.b_ps[bank]
    p.op("pe", lambda e: e.matmul(ps[0:P, 0:N], lhsT=perm, rhs=x, start=True, stop=True), reads=list(reads) + [k.b_const], writes=[b_ps])
    p.op("dve", lambda e: e.tensor_tensor(out=tmp["r1"][0:P, 0:N], in0=ps[0:P, 0:N], in1=sin, op=ALU.mult), reads=[b_ps, tmp["b_tab"]], writes=[tmp["b_r1"]])
    p.op("dve", lambda e: e.tensor_tensor(out=tmp["r2"][0:P, 0:N], in0=x, in1=cos, op=ALU.mult), reads=list(reads) + [tmp["b_tab"]], writes=[tmp["b_r2"]])
    p.op("dve", lambda e: e.tensor_tensor(out=out, in0=tmp["r1"][0:P, 0:N], in1=tmp["r2"][0:P, 0:N], op=ALU.add), reads=[tmp["b_r1"], tmp["b_r2"]], writes=writes)


def attention(k, q_chunks, k_chunks, V, b_q, b_k, b_v, scale, mask_kind, out_cb, extra_mask=None, nkt=NT):
    p = k.p
    b_ps = k.b_ps
    if getattr(k, "attn_cache", None) is None:
        k.attn_cache = ([p.sb([128, 512], BF16, "Pt%d" % i) for i in range(3)], [Buf("Pt%d" % i) for i in range(3)])
    Pt, b_Pt = k.attn_cache
    bk_list = list(b_k) if isinstance(b_k, (list, tuple)) else [b_k]
    blocks = []
    for qg in range(S // 512):
        qts = [4 * qg + j for j in range(4)]
        kts = [kt for kt in range(nkt) if any(mask_kind(kt, qt) for qt in qts)]
        last_kt = {qt: max(kt for kt in kts if mask_kind(kt, qt)) for qt in qts}
        for i, kt in enumerate(kts):
            kinds = [mask_kind(kt, qt) for qt in qts]
            js = [j for j in range(4) if kinds[j]]
            blocks.append(dict(qg=qg, kt=kt, kinds=kinds, js=js, c0=js[0] * 128, c1=(js[-1] + 1) * 128, qts=qts,
                               first=(i == 0), last=(i == len(kts) - 1), last_kt=last_kt, idx=len(blocks)))
    Obanks = [p.psum(2), p.psum(3)]

    def emit_scores(b):
        i = b["idx"]
        sb_i, pi = i % 2, i % 3
        c0, c1, kt, qg = b["c0"], b["c1"], b["kt"], b["qg"]
        ps = p.psum(sb_i)
        for ci in range(len(q_chunks)):
            p.op("pe", lambda e, ps=ps, ci=ci, kt=kt, c0=c0, c1=c1, qg=qg: e.matmul(ps[:, c0:c1], lhsT=k_chunks[ci][:, kt * 128:(kt + 1) * 128], rhs=q_chunks[ci][:, qg * 512 + c0:qg * 512 + c1], start=(ci == 0), stop=(ci == len(q_chunks) - 1)),
                 reads=[b_q] + bk_list, writes=[b_ps[sb_i]])
        p.op("act", lambda e, ps=ps, pi=pi, c0=c0, c1=c1: e.activation(out=Pt[pi][:, c0:c1], in_=ps[:, c0:c1], func=AF.Exp, scale=scale),
             reads=[b_ps[sb_i]], writes=[b_Pt[pi]])
        for j in b["js"]:
            if b["kinds"][j] in ("diag", "anti"):
                m = k.tri if b["kinds"][j] == "diag" else k.anti
                p.op("dve", lambda e, pi=pi, j=j, m=m: e.tensor_tensor(out=Pt[pi][:, j * 128:(j + 1) * 128], in0=Pt[pi][:, j * 128:(j + 1) * 128], in1=m, op=ALU.mult),
                     reads=[b_Pt[pi], k.b_const], writes=[b_Pt[pi]])
        if extra_mask is not None:
            ml, mr, mreads = extra_mask(kt, qg)
            mb = 4 + (i % 2)
            psm = p.psum(mb)
            p.op("pe", lambda e, psm=psm, ml=ml, mr=mr, c0=c0, c1=c1: e.matmul(psm[:, c0:c1], lhsT=ml, rhs=mr[:, c0:c1], start=True, stop=True),
                 reads=mreads, writes=[b_ps[mb]])
            p.op("dve", lambda e, psm=psm, pi=pi, c0=c0, c1=c1: e.tensor_tensor(out=Pt[pi][:, c0:c1], in0=Pt[pi][:, c0:c1], in1=psm[:, c0:c1], op=ALU.mult),
                 reads=[b_ps[mb], b_Pt[pi]], writes=[b_Pt[pi]])

    def emit_pv(b):
        pi = b["idx"] % 3
        kt = b["kt"]
        if b["first"]:
            for bi in range(2):
                p.op("pe", lambda e, bi=bi: e.matmul(Obanks[bi][:, 0:258], lhsT=k.zerosb[:, 0:128], rhs=k.zerosb[:, 0:258], start=True, stop=False, skip_group_check=True),
                     reads=[k.b_const], writes=[b_ps[2 + bi]])
        for j in b["js"]:
            qt = b["qts"][j]
            O = Obanks[j // 2][:, (j % 2) * 129:(j % 2) * 129 + 129]
            p.op("pe", lambda e, O=O, pi=pi, j=j, kt=kt, st=(kt == b["last_kt"][qt]): e.matmul(O, lhsT=Pt[pi][:, j * 128:(j + 1) * 128], rhs=V[:, kt, :], start=False, stop=st, skip_group_check=True),
                 reads=[b_Pt[pi], b_v], writes=[b_ps[2 + j // 2]])
        if b["last"]:
            for j in range(4):
                O = Obanks[j // 2][:, (j % 2) * 129:(j % 2) * 129 + 129]
                out_cb(b["qts"][j], O, b_ps[2 + j // 2])

    emit_scores(blocks[0])
    for i, b in enumerate(blocks):
        if i + 1 < len(blocks):
            emit_scores(blocks[i + 1])
        emit_pv(b)


def stage_mla(k, l):
    p = k.p
    inp = k.inp
    p.mark()
    b_ps = k.b_ps
    b_par = Buf("mla_par")
    qg_c = p.sb([128, 3], F32, "mqg")
    kvg_c = p.sb([128, 2], F32, "mkvg")
    load_cols(k, qg_c, inp["mla_q_norm_g"][l].rearrange("(c q) -> q c", q=128), b_par)
    load_cols(k, kvg_c, inp["mla_kv_norm_g"][l].rearrange("(c q) -> q c", q=128), b_par)
    qkg = p.sb([128, 4], F32, "mqkg")
    load_cols(k, qkg[:, 0:1], inp["mla_qk_norm_g"][l, 0, 0:128].rearrange("(q o) -> q o", o=1), b_par)
    load_cols(k, qkg[0:64, 1:2], inp["mla_qk_norm_g"][l, 0, 128:192].rearrange("(q o) -> q o", o=1), b_par)
    load_cols(k, qkg[:, 2:3], inp["mla_qk_norm_g"][l, 1, 0:128].rearrange("(q o) -> q o", o=1), b_par)
    load_cols(k, qkg[0:64, 3:4], inp["mla_qk_norm_g"][l, 1, 128:192].rearrange("(q o) -> q o", o=1), b_par)
    perm = p.sb([64, 64], BF16, "mperm")
    p.dma("sp", k.ds_misc, perm, inp["c_perm64"], writes=[b_par])
    tmp = dict(sq=p.sb([128, 3, 512], BF16, "msq"), b_sq=Buf("msq"), rstd=p.sb([128, 512], F32, "mrstd"), b_rstd=Buf("mrstd"),
               r1=p.sb([128, 512], F32, "mr1"), b_r1=Buf("mr1"), r2=p.sb([128, 512], F32, "mr2"), b_r2=Buf("mr2"), b_tab=Buf("mtab"))
    cosT = p.sb([64, S], BF16, "mcos")
    sinT = p.sb([64, S], BF16, "msin")
    p.dma("sp", k.ds_misc, cosT, k.sc["rope"][0, 0:64, :], reads=[k.scb["rope"]], writes=[tmp["b_tab"]])
    p.dma("sp", k.ds_misc, sinT, k.sc["rope"][1, 0:64, :], reads=[k.scb["rope"]], writes=[tmp["b_tab"]])
    wqf = p.sb([128, 3, 768], F32, "mwqf")
    wq = p.sb([128, 3, 768], BF16, "mwq")
    wkvf = p.sb([128, 2, 1024], F32, "mwkvf")
    wkv = p.sb([128, 2, 1024], BF16, "mwkv")
    b_wf = Buf("mwf")
    b_w = Buf("mw")
    ds_w = p.dsem("mlaw")
    p.dma("sp", ds_w, wqf, inp["mla_w_uq"][l].rearrange("(c q) h d -> q c (h d)", q=128), writes=[b_wf])
    p.dma("sp", ds_w, wkvf, inp["mla_w_ukv"][l].rearrange("(c q) h d -> q c (h d)", q=128), writes=[b_wf], batch=True)
    for c in range(3):
        p.op("dve", lambda e, c=c: e.tensor_scalar(out=wq[:, c, :], in0=wqf[:, c, :], scalar1=qg_c[:, c:c + 1], scalar2=None, op0=ALU.mult), reads=[b_wf, b_par], writes=[b_w])
    for c in range(2):
        p.op("dve", lambda e, c=c: e.tensor_scalar(out=wkv[:, c, :], in0=wkvf[:, c, :], scalar1=kvg_c[:, c:c + 1], scalar2=None, op0=ALU.mult), reads=[b_wf, b_par], writes=[b_w])
    cT = p.sb([128, 6, S], BF16, "mcT")
    b_cT = Buf("mcT")
    ds_c = p.dsem("mlac")
    for c in range(6):
        rows = 128 if c < 5 else 64
        p.dma("sp", ds_c, cT[0:rows, c, :], k.sc["B_cT"][c * 128:c * 128 + rows, :], reads=[k.scb["B_cT"]], writes=[b_cT], batch=(c > 0))
    kpeT = p.sb([64, S], BF16, "mkpe")
    b_kpe = Buf("mkpe")
    xtmp = p.sb([128, 512], BF16, "mxtmp")
    b_xtmp = Buf("mxtmp")
    for blk in range(8):
        sl = slice(blk * 512, (blk + 1) * 512)
        fm_rmsnorm(k, [cT[:, c, sl] for c in range(3)], 128, 384, None, [cT[:, c, sl] for c in range(3)], [b_cT], [b_cT], 6, tmp)
        fm_rmsnorm(k, [cT[:, c, sl] for c in (3, 4)], 128, 256, None, [cT[:, c, sl] for c in (3, 4)], [b_cT], [b_cT], 7, tmp)
        fm_rmsnorm(k, [cT[0:64, 5, sl]], 64, 64, [qkg[0:64, 3:4]], [xtmp[0:64, :]], [b_cT, b_par], [b_xtmp], 6, tmp)
        fm_rope(k, xtmp[0:64, :], 64, perm, cosT[:, sl], sinT[:, sl], kpeT[:, sl], [b_xtmp, b_par], [b_kpe], 7, tmp)
    sg = p.sb([128, NT, 512], BF16, "msg")
    b_sg = Buf("msg")
    ds_g = p.dsem("mlag")
    p.dma("sp", ds_g, sg, k.sc["B_gate"].rearrange("(a q) c -> q a c", q=128), reads=[k.scb["B_gate"]], writes=[b_sg])
    for a in range(0, NT, 8):
        p.op("act", lambda e, a=a: e.activation(out=sg[:, a:a + 8, :], in_=sg[:, a:a + 8, :], func=AF.Silu), reads=[b_sg], writes=[b_sg])
    qT = p.sb([128, S], BF16, "mqT")
    qpeT = p.sb([64, S], BF16, "mqpeT")
    kT = p.sb([128, S], BF16, "mkT")
    V = p.sb([128, NT, 129], BF16, "mV")
    b_q, b_k, b_v = Buf("mq"), Buf("mk"), Buf("mv")
    ystage = p.sb([128, S], BF16, "mys")
    b_ys = Buf("mys")
    ds_y = p.dsem("mlay")
    otm = [p.sb([128, 128], BF16, "motm%d" % i) for i in range(2)]
    b_otm = [Buf("motm%d" % i) for i in range(2)]
    rs = [p.sb([128, 1], F32, "mrs%d" % i) for i in range(2)]
    b_rs = [Buf("mrs%d" % i) for i in range(2)]
    p.op("pool", lambda e: e.memset(V[:, :, 128:129], 1.0), writes=[b_v])
    k.attn_cache = None
    pc = 0
    for h in range(4):
        for blk in range(8):
            sl = slice(blk * 512, (blk + 1) * 512)
            bank = 4 + (pc % 2)
            pc += 1
            ps = p.psum(bank)
            for c in range(3):
                p.op("pe", lambda e, ps=ps, c=c, h=h, sl=sl: e.matmul(ps, lhsT=wq[:, c, h * 192:h * 192 + 128], rhs=cT[:, c, sl], start=(c == 0), stop=(c == 2)), reads=[b_w, b_cT], writes=[b_ps[bank]])
            fm_rmsnorm(k, [ps], 128, 128, [qkg[:, 0:1]], [qT[:, sl]], [b_ps[bank], b_par], [b_q], 6, tmp)
            bank = 4 + (pc % 2)
            pc += 1
            ps = p.psum(bank)
            for c in range(3):
                p.op("pe", lambda e, ps=ps, c=c, h=h, sl=sl: e.matmul(ps[0:64, :], lhsT=wq[:, c, h * 192 + 128:h * 192 + 192], rhs=cT[:, c, sl], start=(c == 0), stop=(c == 2)), reads=[b_w, b_cT], writes=[b_ps[bank]])
            fm_rmsnorm(k, [ps[0:64, :]], 64, 64, [qkg[0:64, 1:2]], [xtmp[0:64, :]], [b_ps[bank], b_par], [b_xtmp], 7, tmp)
            fm_rope(k, xtmp[0:64, :], 64, perm, cosT[:, sl], sinT[:, sl], qpeT[:, sl], [b_xtmp, b_par], [b_q], 6, tmp)
            bank = 4 + (pc % 2)
            pc += 1
            ps = p.psum(bank)
            for c in range(2):
                p.op("pe", lambda e, ps=ps, c=c, h=h, sl=sl: e.matmul(ps, lhsT=wkv[:, c, h * 256:h * 256 + 128], rhs=cT[:, 3 + c, sl], start=(c == 0), stop=(c == 1)), reads=[b_w, b_cT], writes=[b_ps[bank]])
            fm_rmsnorm(k, [ps], 128, 128, [qkg[:, 2:3]], [kT[:, sl]], [b_ps[bank], b_par], [b_k], 7, tmp)
        for kt in range(NT):
            bank = 4 + (pc % 2)
            pc += 1
            ps = p.psum(bank)
            for c in range(2):
                p.op("pe", lambda e, ps=ps, c=c, h=h, kt=kt: e.matmul(ps[:, 0:128], lhsT=cT[:, 3 + c, kt * 128:(kt + 1) * 128], rhs=wkv[:, c, h * 256 + 128:h * 256 + 256], start=(c == 0), stop=(c == 1)), reads=[b_w, b_cT], writes=[b_ps[bank]])
            copy_op(p, rr_engine(k), V[:, kt, 0:128], ps[:, 0:128], [b_ps[bank]], [b_v])

        def out_cb(qt, O, bO, h=h):
            i2 = qt % 2
            p.op("dve", lambda e: e.reciprocal(out=rs[i2], in_=O[:, 128:129]), reads=[bO], writes=[b_rs[i2]])
            p.op("dve", lambda e: e.scalar_tensor_tensor(out=otm[i2], in0=O[:, 0:128], scalar=rs[i2], in1=sg[:, qt, h * 128:(h + 1) * 128], op0=ALU.mult, op1=ALU.mult),
                 reads=[bO, b_rs[i2], b_sg], writes=[b_otm[i2]])
            pst = p.psum(6, BF16)
            p.op("pe", lambda e: e.transpose(out=pst[:, 0:128], in_=otm[i2], identity=k.ident), reads=[b_otm[i2], k.b_const], writes=[b_ps[6]])
            p.op("act", lambda e: e.activation(out=ystage[:, qt * 128:(qt + 1) * 128], in_=pst[:, 0:128], func=AF.Copy), reads=[b_ps[6]], writes=[b_ys])

        attention(k, [qT, qpeT], [kT, kpeT], V, b_q, [b_k, b_kpe], b_v, 192.0 ** -0.5,
                  lambda kt, qt: ("diag" if kt == qt else ("full" if kt < qt else None)), out_cb)
        p.dma("pool", ds_y, k.sc["yT"][512 + h * 128:512 + (h + 1) * 128, :], ystage, reads=[b_ys], writes=[k.scb["yT"]])
    p.release()


def stage_gdn(k, l):
    p = k.p
    inp = k.inp
    p.mark()
    b_ps = k.b_ps
    b_par = Buf("gpar")
    cw = p.sb([128, 12, 4], F32, "gcw")
    for g in range(12):
        load_cols(k, cw[:, g, :], inp["gdn_conv_w"][l][:, g * 128:(g + 1) * 128].rearrange("j q -> q j"), b_par)
    ng = p.sb([128, 1], F32, "gng")
    load_cols(k, ng, inp["gdn_norm_g"][l].rearrange("(q o) -> q o", o=1), b_par)
    alog = p.sb([128, 4], F32, "galog")
    dtb = p.sb([128, 4], F32, "gdtb")
    p.dma("sp", k.ds_misc, alog, inp["gdn_a_log"][l].partition_broadcast(128), writes=[b_par])
    p.dma("sp", k.ds_misc, dtb, inp["gdn_dt_bias"][l].partition_broadcast(128), writes=[b_par])
    nea = p.sb([128, 4], F32, "gnea")
    p.op("act", lambda e: e.activation(out=nea, in_=alog, func=AF.Exp), reads=[b_par], writes=[b_par])
    p.op("dve", lambda e: e.tensor_scalar(out=nea, in0=nea, scalar1=-1.0, scalar2=None, op0=ALU.mult), reads=[b_par], writes=[b_par])
    qsc = p.sb([128, 1], F32, "gqsc")
    p.op("pool", lambda e: e.memset(qsc, 128.0 ** -0.5), writes=[b_par])
    trif = p.sb([128, 128], F32, "gtrif")
    onesf = p.sb([128, 128], F32, "gonesf")
    p.dma("sp", k.ds_misc, trif, inp["c_trif"], writes=[b_par])
    p.op("pool", lambda e: e.memset(onesf, 1.0), writes=[b_par])
    gb = p.sb([128, NT, 8], F32, "ggb")
    b_gb = Buf("ggb")
    p.dma("sp", k.ds_misc, gb, k.sc["A_gb"].rearrange("(a q) c -> q a c", q=128), reads=[k.scb["A_gb"]], writes=[b_gb])
    beta = p.sb([128, NT, 4], F32, "gbeta")
    gg = p.sb([128, NT, 4], F32, "ggg")
    b_gt = Buf("ggates")
    p.op("act", lambda e: e.activation(out=beta, in_=gb[:, :, 4:8], func=AF.Sigmoid), reads=[b_gb], writes=[b_gt])
    for h in range(4):
        p.op("act", lambda e, h=h: e.activation(out=gg[:, :, h], in_=gb[:, :, h], func=AF.Exp, bias=dtb[:, h:h + 1]), reads=[b_gb, b_par], writes=[b_gt])
    p.op("act", lambda e: e.activation(out=gg, in_=gg, func=AF.Ln, bias=k.one_col), reads=[b_gt], writes=[b_gt])
    for h in range(4):
        p.op("dve", lambda e, h=h: e.tensor_scalar(out=gg[:, :, h], in0=gg[:, :, h], scalar1=nea[:, h:h + 1], scalar2=None, op0=ALU.mult), reads=[b_gt, b_par], writes=[b_gt])
    gc = p.sb([128, NT, 4], F32, "ggc")
    gcl = p.sb([128, NT, 4], F32, "ggcl")
    eg = p.sb([128, NT, 4], F32, "geg")
    egl = p.sb([128, NT, 4], F32, "gegl")
    cdec = p.sb([128, NT, 4], F32, "gcdec")
    bge = p.sb([128, NT, 4], F32, "gbge")
    ngc = p.sb([128, NT, 4], F32, "gngc")
    ps = p.psum(4)
    for t in range(NT):
        p.op("pe", lambda e, t=t: e.matmul(ps[:, t * 4:(t + 1) * 4], lhsT=trif, rhs=gg[:, t, :], start=True, stop=True), reads=[b_gt, b_par], writes=[b_ps[4]])
    p.op("dve", lambda e: e.tensor_copy(out=gc, in_=ps[:, 0:NT * 4].rearrange("q (a b) -> q a b", a=NT)), reads=[b_ps[4]], writes=[b_gt])
    ps2 = p.psum(5)
    for t in range(NT):
        p.op("pe", lambda e, t=t: e.matmul(ps2[:, t * 4:(t + 1) * 4], lhsT=onesf, rhs=gg[:, t, :], start=True, stop=True), reads=[b_gt, b_par], writes=[b_ps[5]])
    p.op("dve", lambda e: e.tensor_copy(out=gcl, in_=ps2[:, 0:NT * 4].rearrange("q (a b) -> q a b", a=NT)), reads=[b_ps[5]], writes=[b_gt])
    p.op("act", lambda e: e.activation(out=eg, in_=gc, func=AF.Exp), reads=[b_gt], writes=[b_gt])
    p.op("act", lambda e: e.activation(out=cdec, in_=gcl, func=AF.Exp), reads=[b_gt], writes=[b_gt])
    p.op("dve", lambda e: e.tensor_tensor(out=egl, in0=gcl, in1=gc, op=ALU.subtract), reads=[b_gt], writes=[b_gt])
    p.op("act", lambda e: e.activation(out=egl, in_=egl, func=AF.Exp), reads=[b_gt], writes=[b_gt])
    p.op("dve", lambda e: e.tensor_tensor(out=bge, in0=beta, in1=eg, op=ALU.mult), reads=[b_gt], writes=[b_gt])
    p.op("dve", lambda e: e.tensor_scalar(out=ngc, in0=gc, scalar1=-1.0, scalar2=None, op0=ALU.mult), reads=[b_gt], writes=[b_gt])
    qTs = [p.sb([128, S], BF16, "gqT%d" % h) for h in range(4)]
    kTs = [p.sb([128, S], BF16, "gkT%d" % h) for h in range(4)]
    vTs = [p.sb([128, S], BF16, "gvT%d" % h) for h in range(4)]
    b_qs = [Buf("gq%d" % h) for h in range(4)]
    b_ks = [Buf("gk%d" % h) for h in range(4)]
    b_vs = [Buf("gv%d" % h) for h in range(4)]
    p.mark()
    xin = p.sb([128, 3 + S], BF16, "gxin")
    b_xin = Buf("gxin")
    ds_x = p.dsem("gdnx")
    p.op("pool", lambda e: e.memset(xin[:, 0:3], 0.0), writes=[b_xin])
    xc = p.sb([128, S], F32, "gxc")
    b_xc = Buf("gxc")
    tmp = dict(sq=p.sb([128, 1, 512], BF16, "gsq"), b_sq=Buf("gsq"), rstd=p.sb([128, 512], F32, "grstd"), b_rstd=Buf("grstd"))
    for h in range(4):
        for which, dst, bd in ((0, qTs[h], b_qs[h]), (1, kTs[h], b_ks[h]), (2, vTs[h], b_vs[h])):
            grp = which * 4 + h
            p.dma("sp", ds_x, xin[:, 3:3 + S], k.sc["A_qkvT"][grp * 128:(grp + 1) * 128, :], reads=[k.scb["A_qkvT"]], writes=[b_xin])
            p.op("dve", lambda e, grp=grp: e.tensor_scalar(out=xc, in0=xin[:, 3:3 + S], scalar1=cw[:, grp, 3:4], scalar2=None, op0=ALU.mult), reads=[b_xin, b_par], writes=[b_xc])
            for j in range(3):
                p.op("dve", lambda e, grp=grp, j=j: e.scalar_tensor_tensor(out=xc, in0=xin[:, j:j + S], scalar=cw[:, grp, j:j + 1], in1=xc, op0=ALU.mult, op1=ALU.add),
                     reads=[b_xin, b_par, b_xc], writes=[b_xc])
            if which == 2:
                p.op("act", lambda e, dst=dst: e.activation(out=dst, in_=xc, func=AF.Silu), reads=[b_xc], writes=[bd])
            else:
                p.op("act", lambda e: e.activation(out=xc, in_=xc, func=AF.Silu), reads=[b_xc], writes=[b_xc])
                for blk in range(8):
                    sl = slice(blk * 512, (blk + 1) * 512)
                    fm_rmsnorm(k, [xc[:, sl]], 128, 1.0, [qsc] if which == 0 else None, [dst[:, sl]], [b_xc, b_par], [bd], 6 + blk % 2, tmp)
    p.release()
    TN = ("Gb", "t1", "ta", "tb", "E1", "E2", "egrow", "qkT", "kend", "AT", "Bn", "QpT")
    Ts = []
    for h in range(4):
        T = {}
        for n in TN:
            T[n] = p.sb([128, 128], F32, "g%s%d" % (n, h))
        for n in ("Lm", "LmT", "TT", "Sf"):
            T[n] = [p.sb([128, 128], F32, "g%s%d_%d" % (n, h, i)) for i in range(2)]
        T["vk"] = p.sb([128, 256], F32, "gvk%d" % h)
        T["uwf"] = p.sb([128, 256], F32, "guwf%d" % h)
        T["oT"] = p.sb([128, 512], F32, "goT%d" % h)
        T["sgT"] = p.sb([128, 512], BF16, "gsgT%d" % h)
        T["ys"] = p.sb([128, 512], BF16, "gys%d" % h)
        T["tmp"] = dict(sq=p.sb([128, 1, 512], BF16, "gsq%d" % h), b_sq=Buf("gsq"), rstd=p.sb([128, 512], F32, "grstd%d" % h), b_rstd=Buf("grstd"))
        for n in ("b_Gb", "b_d", "b_qk", "b_vk", "b_kend", "b_uw", "b_AB", "b_Qp", "b_oT", "b_sg", "b_ys"):
            T[n] = Buf(n + str(h))
        for n in ("b_L", "b_LT", "b_TT", "b_S"):
            T[n] = [Buf(n + str(h) + "_" + str(i)) for i in range(2)]
        T["ds_g"] = p.dsem("gdng%d" % h)
        T["ds_y"] = p.dsem("gdny%d" % h)
        T["bankA"], T["bankB"] = 2 * h, 2 * h + 1
        Ts.append(T)
        p.op("pool", lambda e, T=T: e.memset(T["Sf"][0], 0.0), writes=[T["b_S"][0]])

    def tile_body(h, t):
        T = Ts[h]
        Gb, t1, ta, tb, E1, E2, egrow, qkT, kend, AT, Bn, QpT = [T[n] for n in TN]
        Lm, LmT, TT, Sf, vk, uwf, oT, sgT, ys = T["Lm"], T["LmT"], T["TT"], T["Sf"], T["vk"], T["uwf"], T["oT"], T["sgT"], T["ys"]
        b_Gb, b_d, b_qk, b_vk, b_kend, b_uw, b_AB, b_Qp, b_oT, b_sg, b_ys = [T[n] for n in ("b_Gb", "b_d", "b_qk", "b_vk", "b_kend", "b_uw", "b_AB", "b_Qp", "b_oT", "b_sg", "b_ys")]
        b_L, b_LT, b_TT, b_S = T["b_L"], T["b_LT"], T["b_TT"], T["b_S"]
        qT, kT, vT, b_q, b_k, b_v = qTs[h], kTs[h], vTs[h], b_qs[h], b_ks[h], b_vs[h]
        bA, bB = T["bankA"], T["bankB"]
        ts = slice(t * 128, (t + 1) * 128)
        si, so = t % 2, (t + 1) % 2
        if t % 4 == 0:
            blk = t // 4
            p.dma("sp", T["ds_g"], sgT, k.sc["A_gateT"][h * 128:(h + 1) * 128, blk * 512:(blk + 1) * 512], reads=[k.scb["A_gateT"]], writes=[b_sg])
            p.op("act", lambda e: e.activation(out=sgT, in_=sgT, func=AF.Silu), reads=[b_sg], writes=[b_sg])
        p.op("dve", lambda e: e.tensor_copy(out=Gb, in_=gg[:, t, h:h + 1].to_broadcast([128, 128])), reads=[b_gt], writes=[b_Gb])
        yield
        pr = p.psum(bA)
        p.op("pe", lambda e: e.matmul(pr[:, 0:128], lhsT=Gb, rhs=trif, start=True, stop=True), reads=[b_Gb, b_par], writes=[b_ps[bA]])
        yield
        p.op("dve", lambda e: e.tensor_scalar(out=t1, in0=pr[:, 0:128], scalar1=gc[:, t, h:h + 1], scalar2=None, op0=ALU.subtract), reads=[b_ps[bA], b_gt], writes=[b_d])
        p.op("act", lambda e: e.activation(out=egrow, in_=pr[:, 0:128], func=AF.Exp), reads=[b_ps[bA]], writes=[b_d])
        p.op("dve", lambda e: e.tensor_scalar(out=ta, in0=t1, scalar1=0.0, scalar2=None, op0=ALU.min), reads=[b_d], writes=[b_d])
        p.op("dve", lambda e: e.tensor_scalar(out=tb, in0=t1, scalar1=0.0, scalar2=-1.0, op0=ALU.max, op1=ALU.mult), reads=[b_d], writes=[b_d])
        yield
        p.op("act", lambda e: e.activation(out=ta, in_=ta, func=AF.Exp), reads=[b_d], writes=[b_d])
        p.op("act", lambda e: e.activation(out=tb, in_=tb, func=AF.Exp), reads=[b_d], writes=[b_d])
        p.op("dve", lambda e: e.tensor_tensor(out=E1, in0=ta, in1=k.tri, op=ALU.mult), reads=[b_d, k.b_const], writes=[b_d])
        p.op("dve", lambda e: e.tensor_tensor(out=E2, in0=tb, in1=k.anti, op=ALU.mult), reads=[b_d, k.b_const], writes=[b_d])
        yield
        pk = p.psum(bB)
        p.op("pe", lambda e: e.matmul(pk[:, 0:128], lhsT=kT[:, ts], rhs=kT[:, ts], start=True, stop=True), reads=[b_k], writes=[b_ps[bB]])
        p.op("pe", lambda e: e.matmul(pk[:, 128:256], lhsT=kT[:, ts], rhs=qT[:, ts], start=True, stop=True), reads=[b_k, b_q], writes=[b_ps[bB]])
        yield
        p.op("dve", lambda e: e.scalar_tensor_tensor(out=Lm[0], in0=pk[:, 0:128], scalar=beta[:, t, h:h + 1], in1=E2, op0=ALU.mult, op1=ALU.mult),
             reads=[b_ps[bB], b_gt, b_d], writes=[b_L[0]])
        p.op("dve", lambda e: e.tensor_tensor(out=qkT, in0=pk[:, 128:256], in1=E1, op=ALU.mult), reads=[b_ps[bB], b_d], writes=[b_qk])
        yield
        ptr = p.psum(bA)
        ptrb = p.psum(bA, BF16)
        p.op("pe", lambda e: e.transpose(out=ptrb[:, 0:128], in_=vT[:, ts], identity=k.ident), reads=[b_v, k.b_const], writes=[b_ps[bA]])
        p.op("pe", lambda e: e.transpose(out=ptrb[:, 128:256], in_=kT[:, ts], identity=k.ident), reads=[b_k, k.b_const], writes=[b_ps[bA]])
        p.op("pe", lambda e: e.transpose(out=ptr[:, 256:384], in_=Lm[0], identity=k.identf), reads=[b_L[0], k.b_const], writes=[b_ps[bA]])
        yield
        p.op("dve", lambda e: e.tensor_scalar(out=vk[:, 0:128], in0=ptrb[:, 0:128], scalar1=beta[:, t, h:h + 1], scalar2=None, op0=ALU.mult), reads=[b_ps[bA], b_gt], writes=[b_vk])
        p.op("dve", lambda e: e.tensor_scalar(out=vk[:, 128:256], in0=ptrb[:, 128:256], scalar1=bge[:, t, h:h + 1], scalar2=None, op0=ALU.mult), reads=[b_ps[bA], b_gt], writes=[b_vk])
        p.op("dve", lambda e: e.tensor_scalar(out=kend, in0=ptrb[:, 128:256], scalar1=egl[:, t, h:h + 1], scalar2=None, op0=ALU.mult), reads=[b_ps[bA], b_gt], writes=[b_kend])
        p.op("dve", lambda e: e.tensor_copy(out=LmT[0], in_=ptr[:, 256:384]), reads=[b_ps[bA]], writes=[b_LT[0]])
        p.op("dve", lambda e: e.tensor_tensor(out=TT[0], in0=k.identf, in1=LmT[0], op=ALU.subtract), reads=[b_LT[0], k.b_const], writes=[b_TT[0]])
        ci, ti = 0, 0
        for lev in range(6):
            ni = 1 - ci
            pd = p.psum(bB)
            p.op("pe", lambda e, pd=pd, ci=ci: e.matmul(pd[:, 0:128], lhsT=LmT[ci], rhs=Lm[ci], start=True, stop=True), reads=[b_L[ci], b_LT[ci]], writes=[b_ps[bB]])
            p.op("pe", lambda e, pd=pd, ci=ci: e.matmul(pd[:, 128:256], lhsT=Lm[ci], rhs=LmT[ci], start=True, stop=True), reads=[b_L[ci], b_LT[ci]], writes=[b_ps[bB]])
            yield
            p.op("act", lambda e, pd=pd, ni=ni: e.activation(out=Lm[ni], in_=pd[:, 0:128], func=AF.Copy), reads=[b_ps[bB]], writes=[b_L[ni]])
            p.op("act", lambda e, pd=pd, ni=ni: e.activation(out=LmT[ni], in_=pd[:, 128:256], func=AF.Copy), reads=[b_ps[bB]], writes=[b_LT[ni]])
            yield
            pt2 = p.psum(bA)
            p.op("pe", lambda e, pt2=pt2, ni=ni, ti=ti: e.matmul(pt2[:, 0:128], lhsT=Lm[ni], rhs=TT[ti], start=True, stop=True), reads=[b_L[ni], b_TT[ti]], writes=[b_ps[bA]])
            yield
            p.op("dve", lambda e, pt2=pt2, ti=ti: e.tensor_tensor(out=TT[1 - ti], in0=pt2[:, 0:128], in1=TT[ti], op=ALU.add), reads=[b_ps[bA], b_TT[ti]], writes=[b_TT[1 - ti]])
            yield
            ci, ti = ni, 1 - ti
        yield
        pu = p.psum(bB)
        p.op("pe", lambda e, ti=ti: e.matmul(pu[:, 0:256], lhsT=TT[ti], rhs=vk, start=True, stop=True), reads=[b_TT[ti], b_vk], writes=[b_ps[bB]])
        yield
        p.op("act", lambda e: e.activation(out=uwf, in_=pu[:, 0:256], func=AF.Copy), reads=[b_ps[bB]], writes=[b_uw])
        yield
        pa = p.psum(bA)
        p.op("pe", lambda e: e.matmul(pa[:, 0:128], lhsT=uwf[:, 128:256], rhs=kend, start=True, stop=True), reads=[b_uw, b_kend], writes=[b_ps[bA]])
        p.op("pe", lambda e: e.matmul(pa[:, 128:256], lhsT=kend, rhs=uwf[:, 0:128], start=True, stop=True), reads=[b_uw, b_kend], writes=[b_ps[bA]])
        yield
        p.op("dve", lambda e: e.scalar_tensor_tensor(out=AT, in0=k.identf, scalar=cdec[:, t, h:h + 1], in1=pa[:, 0:128], op0=ALU.mult, op1=ALU.subtract),
             reads=[b_ps[bA], b_gt, k.b_const], writes=[b_AB])
        p.op("dve", lambda e: e.tensor_copy(out=Bn, in_=pa[:, 128:256]), reads=[b_ps[bA]], writes=[b_AB])
        yield
        pq = p.psum(bB)
        p.op("pe", lambda e: e.matmul(pq[:, 0:128], lhsT=uwf[:, 128:256], rhs=qkT, start=True, stop=True), reads=[b_uw, b_qk], writes=[b_ps[bB]])
        p.op("dve", lambda e: e.tensor_tensor(out=t1, in0=qT[:, ts], in1=egrow, op=ALU.mult), reads=[b_q, b_d], writes=[b_d])
        yield
        p.op("dve", lambda e: e.tensor_tensor(out=QpT, in0=t1, in1=pq[:, 0:128], op=ALU.subtract), reads=[b_ps[bB], b_d], writes=[b_Qp])
        yield
        po = p.psum(bB)
        p.op("pe", lambda e: e.matmul(po[:, 256:384], lhsT=Sf[si], rhs=QpT, start=True, stop=False), reads=[b_S[si], b_Qp], writes=[b_ps[bB]])
        p.op("pe", lambda e: e.matmul(po[:, 256:384], lhsT=uwf[:, 0:128], rhs=qkT, start=False, stop=True), reads=[b_uw, b_qk], writes=[b_ps[bB]])
        yield
        p.op("act", lambda e: e.activation(out=oT[:, (t % 4) * 128:(t % 4 + 1) * 128], in_=po[:, 256:384], func=AF.Copy), reads=[b_ps[bB]], writes=[b_oT])
        yield
        pS = p.psum(bA)
        p.op("pe", lambda e: e.matmul(pS[:, 256:384], lhsT=AT, rhs=Sf[si], start=True, stop=True), reads=[b_AB, b_S[si]], writes=[b_ps[bA]])
        yield
        p.op("dve", lambda e: e.tensor_tensor(out=Sf[so], in0=pS[:, 256:384], in1=Bn, op=ALU.add), reads=[b_ps[bA], b_AB], writes=[b_S[so]])
        yield
        if t % 4 == 3:
            blk = t // 4
            fm_rmsnorm(k, [oT], 128, 128.0, [ng], [oT], [b_oT, b_par], [b_oT], bB, T["tmp"])
            p.op("dve", lambda e: e.tensor_tensor(out=ys, in0=oT, in1=sgT, op=ALU.mult), reads=[b_oT, b_sg], writes=[b_ys])
            p.dma("pool", T["ds_y"], k.sc["yT"][h * 128:(h + 1) * 128, blk * 512:(blk + 1) * 512], ys, reads=[b_ys], writes=[k.scb["yT"]])

    for t in range(getattr(k, "gdn_tiles", NT)):
        gens = [tile_body(h, t) for h in range(4)]
        while gens:
            for g_ in list(gens):
                try:
                    next(g_)
                except StopIteration:
                    gens.remove(g_)
    p.release()


def stage_nsa(k, l):
    p = k.p
    inp = k.inp
    p.mark()
    b_ps = k.b_ps
    b_par = Buf("npar")
    sc128 = 128.0 ** -0.5
    qT = p.sb([128, 4, S], BF16, "nqT")
    ksT = p.sb([128, S], BF16, "nksT")
    kwT = p.sb([128, S], BF16, "nkwT")
    Vs = p.sb([128, NT, 129], BF16, "nVs")
    Vw = p.sb([128, NT, 129], BF16, "nVw")
    oc = p.sb([128, NT, 4, 128], BF16, "noc")
    imp = p.sb([128, NT, 64], F32, "nimp")
    gts = p.sb([128, NT, 12], F32, "ngts")
    selT = p.sb([64, S], BF16, "nselT")
    kcT = p.sb([128, 256], BF16, "nkcT")
    Vc = p.sb([128, 2, 193], BF16, "nVc")
    b_q, b_ks, b_kw, b_vs, b_vw, b_oc, b_imp, b_gts, b_selT, b_kc, b_vc = [Buf(n) for n in ("nq", "nks", "nkw", "nvs", "nvw", "noc", "nimp", "ngts", "nselT", "nkc", "nvc")]
    qng = p.sb([128, 1], F32, "nqng")
    kng = p.sb([128, 3], F32, "nkng")
    load_cols(k, qng, inp["nsa_q_norm_g"][l].rearrange("(q o) -> q o", o=1), b_par)
    load_cols(k, kng, inp["nsa_k_norm_g"][l].rearrange("j q -> q j"), b_par)
    perm = p.sb([128, 128], BF16, "nperm")
    p.dma("sp", k.ds_misc, perm, inp["c_perm128"], writes=[b_par])
    ds_a = p.dsem("nsaa")
    ds_b = p.dsem("nsab")
    p.dma("sp", ds_a, gts, k.sc["D_gl"].rearrange("(a q) c -> q a c", q=128), reads=[k.scb["D_gl"]], writes=[b_gts])
    p.op("act", lambda e: e.activation(out=gts, in_=gts, func=AF.Sigmoid), reads=[b_gts], writes=[b_gts])
    p.dma("sp", ds_a, Vs[:, :, 0:128], k.sc["D_v"][:, 0:128].rearrange("(a q) c -> q a c", q=128), reads=[k.scb["D_v"]], writes=[b_vs])
    p.dma("sp", ds_a, Vw[:, :, 0:128], k.sc["D_v"][:, 128:256].rearrange("(a q) c -> q a c", q=128), reads=[k.scb["D_v"]], writes=[b_vw])
    p.op("pool", lambda e: e.memset(Vs[:, :, 128:129], 1.0), writes=[b_vs])
    p.op("pool", lambda e: e.memset(Vw[:, :, 128:129], 1.0), writes=[b_vw])
    p.op("pool", lambda e: e.memset(imp, 0.0), writes=[b_imp])
    p.op("pool", lambda e: e.memset(kcT, 0.0), writes=[b_kc])
    p.op("pool", lambda e: e.memset(Vc, 0.0), writes=[b_vc])
    p.mark()
    tmp = dict(sq=p.sb([128, 1, 512], BF16, "nsq"), b_sq=Buf("nsq"), rstd=p.sb([128, 512], F32, "nrstd"), b_rstd=Buf("nrstd"),
               r1=p.sb([128, 512], F32, "nr1"), b_r1=Buf("nr1"), r2=p.sb([128, 512], F32, "nr2"), b_r2=Buf("nr2"), b_tab=Buf("ntab"))
    cosT = p.sb([128, S], BF16, "ncos")
    sinT = p.sb([128, S], BF16, "nsin")
    p.dma("sp", ds_a, cosT, k.sc["rope"][2], reads=[k.scb["rope"]], writes=[tmp["b_tab"]])
    p.dma("sp", ds_a, sinT, k.sc["rope"][3], reads=[k.scb["rope"]], writes=[tmp["b_tab"]])
    qraw = p.sb([128, S], F32, "nqraw")
    b_qraw = Buf("nqraw")
    kraw = p.sb([128, S], BF16, "nkraw")
    b_kraw = Buf("nkraw")
    xtmp = p.sb([128, 512], BF16, "nxtmp")
    b_xtmp = Buf("nxtmp")
    for h in range(4):
        p.dma("sp", ds_b, qraw, k.sc["D_qT"][h * 128:(h + 1) * 128, :], reads=[k.scb["D_qT"]], writes=[b_qraw])
        for blk in range(8):
            sl = slice(blk * 512, (blk + 1) * 512)
            fm_rmsnorm(k, [qraw[:, sl]], 128, 128.0, [qng], [xtmp], [b_qraw, b_par], [b_xtmp], 6, tmp)
            fm_rope(k, xtmp, 128, perm, cosT[:, sl], sinT[:, sl], qT[:, h, sl], [b_xtmp, b_par], [b_q], 7, tmp)
    for which, dst, bd in ((1, ksT, b_ks), (2, kwT, b_kw)):
        r0 = 256 if which == 1 else 384
        p.dma("sp", ds_b, kraw, k.sc["D_kvT"][r0:r0 + 128, :], reads=[k.scb["D_kvT"]], writes=[b_kraw])
        for blk in range(8):
            sl = slice(blk * 512, (blk + 1) * 512)
            fm_rmsnorm(k, [kraw[:, sl]], 128, 128.0, [kng[:, which:which + 1]], [xtmp], [b_kraw, b_par], [b_xtmp], 6, tmp)
            fm_rope(k, xtmp, 128, perm, cosT[:, sl], sinT[:, sl], dst[:, sl], [b_xtmp, b_par], [bd], 7, tmp)
    w1f = p.sb([128, 8, 256], F32, "nw1f")
    w1b = p.sb([128, 32, 256], BF16, "nw1b")
    b_w1f, b_w1 = Buf("nw1f"), Buf("nw1")
    w2f = p.sb([128, 2, 128], F32, "nw2f")
    w2b = p.sb([128, 2, 128], BF16, "nw2b")
    b_w2 = Buf("nw2")
    peTf = p.sb([128, 32], F32, "npeTf")
    peT = p.sb([128, 32], BF16, "npeT")
    b_pe = Buf("npe")
    b1c = p.sb([128, 2], F32, "nb1c")
    b2c = p.sb([128, 1], F32, "nb2c")
    b2r = p.sb([1, 128], F32, "nb2r")
    b2rb = p.sb([1, 128], BF16, "nb2rb")
    biasc = p.sb([128, 2], F32, "nbiasc")
    b_b = Buf("nbias")
    hidT = p.sb([128, 2, 256], BF16, "nhidT")
    b_hid = Buf("nhid")
    kcf = p.sb([128, 256], F32, "nkcf")
    b_kcf = Buf("nkcf")
    xr = p.sb([128, S], BF16, "nxr")
    b_xr = Buf("nxr")
    for j in range(2):
        p.dma("sp", ds_b, xr, k.sc["D_kvT"][j * 128:(j + 1) * 128, :], reads=[k.scb["D_kvT"]], writes=[b_xr])
        for lq in range(4):
            p.dma("sp", ds_a, w1f, inp["nsa_cmp_w1"][l, j, lq * 1024:(lq + 1) * 1024, :].rearrange("(a q) n -> q a n", q=128), writes=[b_w1f])
            copy_op(p, rr_engine(k, ("act", "dve")), w1b[:, lq * 8:(lq + 1) * 8, :], w1f, [b_w1f], [b_w1])
        p.dma("sp", ds_a, w2f, inp["nsa_cmp_w2"][l, j].rearrange("(c q) d -> q c d", q=128), writes=[b_w2])
        p.op("dve", lambda e: e.tensor_copy(out=w2b, in_=w2f), reads=[b_w2], writes=[b_w2])
        load_cols(k, peTf, inp["nsa_cmp_pe"][l, j].rearrange("a d -> d a"), b_pe)
        p.op("dve", lambda e: e.tensor_copy(out=peT, in_=peTf), reads=[b_pe], writes=[b_pe])
        load_cols(k, b1c, inp["nsa_cmp_b1"][l, j].rearrange("(c q) -> q c", q=128), b_b)
        load_cols(k, b2c, inp["nsa_cmp_b2"][l, j].rearrange("(q o) -> q o", o=1), b_b)
        p.dma("sp", ds_a, b2r, inp["nsa_cmp_b2"][l, j].rearrange("(o q) -> o q", o=1), writes=[b_b])
        p.op("dve", lambda e: e.tensor_copy(out=b2rb, in_=b2r), reads=[b_b], writes=[b_b])
        for nc_ in range(2):
            pc_ = p.psum(4)
            for a in range(32):
                p.op("pe", lambda e, pc_=pc_, a=a, nc_=nc_: e.matmul(pc_[:, 0:1], lhsT=w1b[:, a, nc_ * 128:(nc_ + 1) * 128], rhs=peT[:, a:a + 1], start=(a == 0), stop=(a == 31)),
                     reads=[b_w1, b_pe], writes=[b_ps[4]])
            p.op("dve", lambda e, pc_=pc_, nc_=nc_: e.tensor_tensor(out=biasc[:, nc_:nc_ + 1], in0=pc_[:, 0:1], in1=b1c[:, nc_:nc_ + 1], op=ALU.add), reads=[b_ps[4], b_b], writes=[b_b])
            ph = p.psum(5)
            for a in range(32):
                p.op("pe", lambda e, ph=ph, a=a, nc_=nc_: e.matmul(ph[:, 0:255], lhsT=w1b[:, a, nc_ * 128:(nc_ + 1) * 128], rhs=xr[:, a:a + 16 * 254 + 1:16], start=(a == 0), stop=(a == 31)),
                     reads=[b_w1, b_xr], writes=[b_ps[5]])
            p.op("act", lambda e, ph=ph, nc_=nc_: e.activation(out=hidT[:, nc_, 0:255], in_=ph[:, 0:255], func=AF.Silu, bias=biasc[:, nc_:nc_ + 1]), reads=[b_ps[5], b_b], writes=[b_hid])
        if j == 0:
            pk_ = p.psum(4)
            for nc_ in range(2):
                p.op("pe", lambda e, pk_=pk_, nc_=nc_: e.matmul(pk_[:, 0:255], lhsT=w2b[:, nc_, :], rhs=hidT[:, nc_, 0:255], start=(nc_ == 0), stop=(nc_ == 1)), reads=[b_w2, b_hid], writes=[b_ps[4]])
            p.op("act", lambda e, pk_=pk_: e.activation(out=kcf[:, 0:255], in_=pk_[:, 0:255], func=AF.Identity, bias=b2c), reads=[b_ps[4], b_b], writes=[b_kcf])
            fm_rmsnorm(k, [kcf[:, 0:255]], 128, 128.0, [kng[:, 0:1]], [xtmp[:, 0:255]], [b_kcf, b_par], [b_xtmp], 6, tmp)
            fm_rope(k, xtmp[:, 0:255], 128, perm, cosT[:, 31:31 + 16 * 254 + 1:16], sinT[:, 31:31 + 16 * 254 + 1:16], kcT[:, 0:255], [b_xtmp, b_par], [b_kc], 7, tmp)
        else:
            for cc in range(2):
                M = 128 if cc == 0 else 127
                pv_ = p.psum(4)
                for nc_ in range(2):
                    p.op("pe", lambda e, pv_=pv_, nc_=nc_, cc=cc, M=M: e.matmul(pv_[0:M, 0:128], lhsT=hidT[:, nc_, cc * 128:cc * 128 + M], rhs=w2b[:, nc_, :], start=(nc_ == 0), stop=False), reads=[b_w2, b_hid], writes=[b_ps[4]])
                p.op("pe", lambda e, pv_=pv_, M=M: e.matmul(pv_[0:M, 0:128], lhsT=k.onesb[0:1, 0:M], rhs=b2rb, start=False, stop=True), reads=[b_b, k.b_const], writes=[b_ps[4]])
                p.op("act", lambda e, pv_=pv_, cc=cc, M=M: e.activation(out=Vc[0:M, cc, 0:128], in_=pv_[0:M, 0:128], func=AF.Copy), reads=[b_ps[4]], writes=[b_vc])
    p.dma("sp", ds_a, Vc[:, :, 128:193], inp["c_ovl"].rearrange("c q n -> q c n"), writes=[b_vc])
    p.release()
    p.mark()
    cmask = p.sb([128, 2, S], BF16, "ncmask")
    b_cm = Buf("ncm")
    p.dma("sp", ds_a, cmask, inp["c_cmask"].rearrange("c q t -> q c t"), writes=[b_cm])
    Et = [p.sb([128, 512], BF16, "nEt%d" % i) for i in range(2)]
    b_Et = [Buf("nEt%d" % i) for i in range(2)]
    rsc = p.sb([128, 4], F32, "nrsc")
    b_rsc = Buf("nrsc")
    itmp = p.sb([128, 64], F32, "nitmp")
    b_it = Buf("nitmp")
    for qg in range(8):
        ccs = [0] if qg < 4 else [0, 1]
        for h in range(4):
            for cc in ccs:
                M = 128 if cc == 0 else 127
                ps_ = p.psum(cc)
                p.op("pe", lambda e, ps_=ps_, cc=cc, M=M, h=h, qg=qg: e.matmul(ps_[0:M, :], lhsT=kcT[:, cc * 128:cc * 128 + M], rhs=qT[:, h, qg * 512:(qg + 1) * 512], start=True, stop=True), reads=[b_kc, b_q], writes=[b_ps[cc]])
                p.op("act", lambda e, ps_=ps_, cc=cc, M=M: e.activation(out=Et[cc][0:M, :], in_=ps_[0:M, :], func=AF.Exp, scale=sc128), reads=[b_ps[cc]], writes=[b_Et[cc]])
                p.op("dve", lambda e, cc=cc, M=M, qg=qg: e.tensor_tensor(out=Et[cc][0:M, :], in0=Et[cc][0:M, :], in1=cmask[0:M, cc, qg * 512:(qg + 1) * 512], op=ALU.mult), reads=[b_Et[cc], b_cm], writes=[b_Et[cc]])
            for jt in range(4):
                qt = qg * 4 + jt
                bank = 2 + jt % 2
                R = p.psum(bank)[:, (jt // 2) * 193:(jt // 2) * 193 + 193]
                for ci, cc in enumerate(ccs):
                    M = 128 if cc == 0 else 127
                    p.op("pe", lambda e, R=R, cc=cc, M=M, jt=jt, ci=ci, ccs=ccs: e.matmul(R, lhsT=Et[cc][0:M, jt * 128:(jt + 1) * 128], rhs=Vc[0:M, cc, :], start=(ci == 0), stop=(ci == len(ccs) - 1)),
                         reads=[b_Et[cc], b_vc], writes=[b_ps[bank]])
                p.op("dve", lambda e, R=R, jt=jt: e.tensor_scalar(out=rsc[:, jt:jt + 1], in0=R[:, 192:193], scalar1=1e-30, scalar2=None, op0=ALU.max), reads=[b_ps[bank]], writes=[b_rsc])
                p.op("dve", lambda e, jt=jt: e.reciprocal(out=rsc[:, jt:jt + 1], in_=rsc[:, jt:jt + 1]), reads=[b_rsc], writes=[b_rsc])
                p.op("dve", lambda e, R=R, jt=jt, qt=qt: e.scalar_tensor_tensor(out=imp[:, qt, :], in0=R[:, 128:192], scalar=rsc[:, jt:jt + 1], in1=imp[:, qt, :], op0=ALU.mult, op1=ALU.add),
                     reads=[b_ps[bank], b_rsc, b_imp], writes=[b_imp])
                p.op("dve", lambda e, jt=jt, qt=qt, h=h: e.tensor_scalar(out=rsc[:, jt:jt + 1], in0=rsc[:, jt:jt + 1], scalar1=gts[:, qt, h * 3:h * 3 + 1], scalar2=None, op0=ALU.mult), reads=[b_rsc, b_gts], writes=[b_rsc])
                p.op("act", lambda e, R=R, jt=jt, qt=qt, h=h: e.activation(out=oc[:, qt, h, :], in_=R[:, 0:128], func=AF.Copy, scale=rsc[:, jt:jt + 1]), reads=[b_ps[bank], b_rsc], writes=[b_oc])
    p.release()
    p.mark()
    valid = p.sb([128, NT, 64], F32, "nvalid")
    seladd = p.sb([128, NT, 64], F32, "nseladd")
    b_vc2 = Buf("nvalid")
    p.dma("sp", ds_a, valid, inp["c_valid"], writes=[b_vc2])
    p.dma("sp", ds_a, seladd, inp["c_seladd"], writes=[b_vc2])
    m8 = p.sb([128, 8], F32, "nm8")
    wk = p.sb([128, 64], F32, "nwk")
    selm = p.sb([128, 64], BF16, "nselm")
    b_tk = Buf("ntk")
    b_selm = Buf("nselm")
    p.op("dve", lambda e: e.tensor_tensor(out=imp, in0=imp, in1=valid, op=ALU.mult), reads=[b_imp, b_vc2], writes=[b_imp])
    p.op("dve", lambda e: e.tensor_tensor(out=imp, in0=imp, in1=seladd, op=ALU.add), reads=[b_imp, b_vc2], writes=[b_imp])
    for qt in range(NT):
        p.op("dve", lambda e, qt=qt: e.max(out=m8, in_=imp[:, qt, :]), reads=[b_imp], writes=[b_tk])
        p.op("dve", lambda e, qt=qt: e.match_replace(out=wk, in_to_replace=m8, in_values=imp[:, qt, :], imm_value=-2.0), reads=[b_imp, b_tk], writes=[b_tk])
        p.op("dve", lambda e: e.max(out=m8, in_=wk), reads=[b_tk], writes=[b_tk])
        p.op("dve", lambda e, qt=qt: e.tensor_scalar(out=wk, in0=imp[:, qt, :], scalar1=m8[:, 7:8], scalar2=None, op0=ALU.is_ge), reads=[b_imp, b_tk], writes=[b_tk])
        p.op("dve", lambda e, qt=qt: e.tensor_tensor(out=selm, in0=wk, in1=valid[:, qt, :], op=ALU.mult), reads=[b_tk, b_vc2], writes=[b_selm])
        pst = p.psum(6, BF16)
        p.op("pe", lambda e, pst=pst: e.transpose(out=pst[0:64, 0:128], in_=selm, identity=k.ident), reads=[b_selm, k.b_const], writes=[b_ps[6]])
        p.op("act", lambda e, pst=pst, qt=qt: e.activation(out=selT[:, qt * 128:(qt + 1) * 128], in_=pst[0:64, 0:128], func=AF.Copy), reads=[b_ps[6]], writes=[b_selT])
    p.release()
    p.mark()
    expand = p.sb([64, NT, 128], BF16, "nexp")
    b_exp = Buf("nexp")
    p.dma("sp", ds_a, expand, inp["c_expand"], writes=[b_exp])
    sg = p.sb([128, NT, 512], BF16, "nsg")
    b_sg = Buf("nsg")
    p.dma("sp", ds_b, sg, k.sc["D_gate"].rearrange("(a q) c -> q a c", q=128), reads=[k.scb["D_gate"]], writes=[b_sg])
    for a in range(0, NT, 8):
        p.op("act", lambda e, a=a: e.activation(out=sg[:, a:a + 8, :], in_=sg[:, a:a + 8, :], func=AF.Silu), reads=[b_sg], writes=[b_sg])
    acc = p.sb([128, NT, 128], F32, "nacc")
    b_acc = Buf("nacc")
    ystage = p.sb([128, S], BF16, "nys")
    b_ys = Buf("nys")
    ds_y = p.dsem("nsay")
    otm = [p.sb([128, 128], BF16, "notm%d" % i) for i in range(2)]
    b_otm = [Buf("notm%d" % i) for i in range(2)]
    rs = [p.sb([128, 1], F32, "nrs%d" % i) for i in range(2)]
    b_rs = [Buf("nrs%d" % i) for i in range(2)]
    k.attn_cache = None
    for h in range(4):
        def cb_sel(qt, O, bO, h=h):
            i2 = qt % 2
            p.op("dve", lambda e: e.reciprocal(out=rs[i2], in_=O[:, 128:129]), reads=[bO], writes=[b_rs[i2]])
            p.op("dve", lambda e: e.tensor_scalar(out=rs[i2], in0=rs[i2], scalar1=gts[:, qt, h * 3 + 1:h * 3 + 2], scalar2=None, op0=ALU.mult), reads=[b_rs[i2], b_gts], writes=[b_rs[i2]])
            p.op("dve", lambda e: e.scalar_tensor_tensor(out=acc[:, qt, :], in0=O[:, 0:128], scalar=rs[i2], in1=oc[:, qt, h, :], op0=ALU.mult, op1=ALU.add),
                 reads=[bO, b_rs[i2], b_oc], writes=[b_acc])

        def cb_win(qt, O, bO, h=h):
            i2 = qt % 2
            p.op("dve", lambda e: e.reciprocal(out=rs[i2], in_=O[:, 128:129]), reads=[bO], writes=[b_rs[i2]])
            p.op("dve", lambda e: e.tensor_scalar(out=rs[i2], in0=rs[i2], scalar1=gts[:, qt, h * 3 + 2:h * 3 + 3], scalar2=None, op0=ALU.mult), reads=[b_rs[i2], b_gts], writes=[b_rs[i2]])
            p.op("dve", lambda e: e.scalar_tensor_tensor(out=acc[:, qt, :], in0=O[:, 0:128], scalar=rs[i2], in1=acc[:, qt, :], op0=ALU.mult, op1=ALU.add),
                 reads=[bO, b_rs[i2], b_acc], writes=[b_acc])
            p.op("dve", lambda e: e.tensor_tensor(out=otm[i2], in0=acc[:, qt, :], in1=sg[:, qt, h * 128:(h + 1) * 128], op=ALU.mult), reads=[b_acc, b_sg], writes=[b_otm[i2]])
            pst = p.psum(6, BF16)
            p.op("pe", lambda e: e.transpose(out=pst[:, 0:128], in_=otm[i2], identity=k.ident), reads=[b_otm[i2], k.b_const], writes=[b_ps[6]])
            p.op("act", lambda e: e.activation(out=ystage[:, qt * 128:(qt + 1) * 128], in_=pst[:, 0:128], func=AF.Copy), reads=[b_ps[6]], writes=[b_ys])

        attention(k, [qT[:, h, :]], [ksT], Vs, b_q, b_ks, b_vs, sc128,
                  lambda kt, qt: ("diag" if kt == qt else ("full" if kt < qt else None)), cb_sel,
                  extra_mask=lambda kt, qg: (expand[:, kt, :], selT[:, qg * 512:(qg + 1) * 512], [b_exp, b_selT]))
        attention(k, [qT[:, h, :]], [kwT], Vw, b_q, b_kw, b_vw, sc128,
                  lambda kt, qt: ("diag" if kt == qt else ("full" if 0 < qt - kt < 4 else ("anti" if qt - kt == 4 else None))), cb_win)
        p.dma("pool", ds_y, k.sc["yT"][1536 + h * 128:1536 + (h + 1) * 128, :], ystage, reads=[b_ys], writes=[k.scb["yT"]])
    p.release()
    p.release()
```
